# Optimizing a Trainium2 kernel written in Bass

```python
import jax
import jax.numpy as jnp
from jax import lax
import numpy as np

D_MODEL = 1024
BATCH = 8
SEQ = 2048
DEPTH = 4

GRID_W = 64
CTX_LEN = 256
N_MIXERS = 3
N_SUB = 3
D_FF = 2816
MACARON_WEIGHT = 0.5
NORM_EPS = 1e-6

MLA_HEADS = 8
MLA_Q_RANK = 512
MLA_KV_RANK = 256
MLA_NOPE = 128
MLA_ROPE = 64
MLA_V = 128
MLA_SCALE = (MLA_NOPE + MLA_ROPE) ** -0.5
ROPE_PAIRS = MLA_ROPE // 4
ROPE_BASE = 10000.0
Q_BLOCK = 128

FNET_GROUPS = 8
FNET_GROUP_DIM = D_MODEL // FNET_GROUPS

RWKV_HEAD = 64
RWKV_HEADS = D_MODEL // RWKV_HEAD
RWKV_DECAY_LORA = 64
RWKV_AAA_LORA = 64
RWKV_GATE_LORA = 160
RWKV_GN_EPS = 64e-5
N_DIR = 2

N_LAYERS_A = (DEPTH + 2) // 3
N_LAYERS_B = (DEPTH + 1) // 3
N_LAYERS_C = DEPTH // 3

kernel_name = 'hybrid_mla_fnet_rwkv7_flow_block'


def rms_norm(x, g):
    x32 = x.astype(jnp.float32)
    y = x32 * lax.rsqrt(jnp.mean(x32 * x32, axis=-1, keepdims=True) + NORM_EPS)
    return (y * g.astype(jnp.float32)).astype(x.dtype)


def modulate(z, g, m):
    return rms_norm(z, g) * (1.0 + m[:, 1]) + m[:, 0]


def gated_residual(z, y, g, m, weight):
    return z + weight * m[:, 2] * rms_norm(y, g)


def swiglu(x, w_gate, w_up, w_down):
    return (jax.nn.silu(x @ w_gate) * (x @ w_up)) @ w_down


def ffn_sublayer(z, m, g_pre, g_post, w_gate, w_up, w_down):
    y = swiglu(modulate(z, g_pre, m), w_gate, w_up, w_down)
    return gated_residual(z, y, g_post, m, MACARON_WEIGHT)


def axial_rope_tables(T):
    rows = T // GRID_W
    row = jnp.repeat(jnp.arange(rows), GRID_W).astype(jnp.float32)
    col = jnp.tile(jnp.arange(GRID_W), rows).astype(jnp.float32)
    inv = ROPE_BASE ** (-jnp.arange(ROPE_PAIRS, dtype=jnp.float32) / ROPE_PAIRS)
    ang = jnp.stack([row[:, None] * inv, col[:, None] * inv], axis=1)
    return jnp.cos(ang), jnp.sin(ang)


def apply_axial_rope(x, cos, sin):
    xr = x.astype(jnp.float32).reshape(x.shape[:-1] + (2, 2, ROPE_PAIRS))
    x1, x2 = xr[..., 0, :], xr[..., 1, :]
    out = jnp.stack([x1 * cos - x2 * sin, x2 * cos + x1 * sin], axis=-2)
    return out.reshape(x.shape).astype(x.dtype)


def attention(q, k, v):
    s = jnp.einsum('bqhd,bkhd->bhqk', q, k).astype(jnp.float32)
    p = jax.nn.softmax(s, axis=-1).astype(v.dtype)
    return jnp.einsum('bhqk,bkhd->bqhd', p, v)


def blocked_attention(q, k, v):
    B, T, H, Dk = q.shape
    qb = jnp.moveaxis(q.reshape(B, T // Q_BLOCK, Q_BLOCK, H, Dk), 1, 0)
    o = lax.map(lambda qi: attention(qi, k, v), qb)
    return jnp.moveaxis(o, 0, 1).reshape(B, T, H, v.shape[-1])


def mla_queries(u, w_dq, q_norm, w_uq, cos, sin):
    B, T, _ = u.shape
    q = (rms_norm(u @ w_dq, q_norm) @ w_uq).reshape(B, T, MLA_HEADS, MLA_NOPE + MLA_ROPE)
    q_nope, q_pe = q[..., :MLA_NOPE], q[..., MLA_NOPE:]
    if cos is not None:
        q_pe = apply_axial_rope(q_pe, cos[:, None], sin[:, None])
    return jnp.concatenate([q_nope, q_pe], axis=-1) * MLA_SCALE


def mla_keys_values(u, w_dkv, kv_norm, w_ukv, cos, sin):
    B, T, _ = u.shape
    ckv = u @ w_dkv
    c_kv, k_pe = ckv[..., :MLA_KV_RANK], ckv[..., MLA_KV_RANK:]
    kv = (rms_norm(c_kv, kv_norm) @ w_ukv).reshape(B, T, MLA_HEADS, MLA_NOPE + MLA_V)
    k_nope, v = kv[..., :MLA_NOPE], kv[..., MLA_NOPE:]
    if cos is not None:
        k_pe = apply_axial_rope(k_pe, cos, sin)
    k_pe = jnp.broadcast_to(k_pe[:, :, None, :], (B, T, MLA_HEADS, MLA_ROPE))
    return jnp.concatenate([k_nope, k_pe], axis=-1), v


def mla_mixer(u, uc, cos, sin, ctx_out, w_dq, q_norm, w_uq, w_dkv, kv_norm, w_ukv, w_o):
    B, T, _ = u.shape
    k_c, v_c = mla_keys_values(uc, w_dkv, kv_norm, w_ukv, None, None)
    k_l, v_l = mla_keys_values(u, w_dkv, kv_norm, w_ukv, cos, sin)
    q_l = mla_queries(u, w_dq, q_norm, w_uq, cos, sin)
    o = blocked_attention(q_l, jnp.concatenate([k_l, k_c], axis=1), jnp.concatenate([v_l, v_c], axis=1))
    y = o.reshape(B, T, MLA_HEADS * MLA_V) @ w_o
    yc = None
    if ctx_out:
        q_c = mla_queries(uc, w_dq, q_norm, w_uq, None, None)
        yc = attention(q_c, k_c, v_c).reshape(B, uc.shape[1], MLA_HEADS * MLA_V) @ w_o
    return y, yc


def fourier_mix(u):
    B, T, D = u.shape
    z = u.astype(jnp.float32).reshape(B, T, FNET_GROUPS, FNET_GROUP_DIM)
    f = jnp.fft.fft2(z, axes=(1, 3), norm='ortho').real
    return f.reshape(B, T, D).astype(u.dtype)


def fnet_mixer(u, uc, ctx_out, w_o, b_o):
    y = fourier_mix(u) @ w_o + b_o
    yc = fourier_mix(uc) @ w_o + b_o if ctx_out else None
    return y, yc


def split_heads(z):
    return z.reshape(z.shape[:-1] + (RWKV_HEADS, RWKV_HEAD))


def centred_token_shift(x):
    z = jnp.pad(x, ((0, 0), (1, 1), (0, 0)))
    return 0.5 * (z[:, :-2] + z[:, 2:]) - x


def rwkv_features(u, mix, w_r, w_k, w_v, w0, w1, w2, a0, a1, a2, g1, g2, k_k, k_a):
    B, T, D = u.shape
    f32 = jnp.float32
    xx = centred_token_shift(u)
    xr, xw, xk, xv, xa, xg = (u + xx * mix[m] for m in range(6))
    r = (xr @ w_r).astype(f32)
    k = (xk @ w_k).astype(f32)
    v = (xv @ w_v).astype(f32)
    g = jax.nn.sigmoid(xg @ g1) @ g2
    w_lora = jnp.einsum('ebtr,erd->ebtd', jnp.tanh(jnp.einsum('btd,edr->ebtr', xw, w1)), w2)
    w_log = -jax.nn.softplus(-(w0[:, None, None, :] + w_lora).astype(f32)) - 0.5
    decay = jnp.exp(-jnp.exp(w_log))
    a_lora = jnp.einsum('ebtr,erd->ebtd', jnp.einsum('btd,edr->ebtr', xa, a1), a2)
    a = jax.nn.sigmoid((a0[:, None, None, :] + a_lora).astype(f32))
    kk = split_heads(k * k_k.astype(f32))
    kk = kk / jnp.maximum(jnp.sqrt(jnp.sum(kk * kk, axis=-1, keepdims=True)), 1e-12)
    k_dir = k[None] * (1.0 + (a - 1.0) * k_a.astype(f32))
    return split_heads(r), split_heads(decay), split_heads(k_dir), split_heads(v), kk, split_heads(a), g


def wkv_step(S, inp):
    r, w, k, v, a, b = inp
    sa = jnp.einsum('ebhij,ebhj->ebhi', S, a)
    S = S * w[..., None, :] + sa[..., :, None] * b[..., None, :] + v[..., :, None] * k[..., None, :]
    return S, jnp.einsum('ebhij,ebhj->ebhi', S, r)


def wkv_bidirectional(S0, r, decay, k_dir, v, kk, a):
    def orient(z):
        return jnp.stack([z[0], z[1][:, ::-1]])

    def both(z):
        return jnp.stack([z, z[:, ::-1]])

    xs = (both(r), orient(decay), orient(k_dir), both(v), both(-kk), orient(kk[None] * a))
    xs = tuple(jnp.moveaxis(z, 2, 0) for z in xs)
    S, ys = lax.scan(wkv_step, S0, xs)
    ys = jnp.moveaxis(ys, 0, 2)
    return S, ys[0] + ys[1][:, ::-1]


def rwkv_output(y, r, k_dir, v, g, r_k, ln_w, ln_b, w_o):
    B, T, H, N = y.shape
    f32 = jnp.float32
    mu = jnp.mean(y, axis=-1, keepdims=True)
    var = jnp.mean(jnp.square(y - mu), axis=-1, keepdims=True)
    yn = ((y - mu) * lax.rsqrt(var + RWKV_GN_EPS)).reshape(B, T, H * N)
    yn = yn * ln_w.astype(f32) + ln_b.astype(f32)
    coef = jnp.einsum('bthn,ebthn,hn->bth', r, k_dir, r_k.astype(f32))
    bonus = (coef[..., None] * v).reshape(B, T, H * N)
    out = (yn + bonus) * g.astype(f32)
    return out.astype(g.dtype) @ w_o


def rwkv_mixer(u, uc, ctx_out, mix, w_r, w_k, w_v, w0, w1, w2, a0, a1, a2, g1, g2,
               k_k, k_a, r_k, ln_w, ln_b, w_o):
    B = u.shape[0]
    fc = rwkv_features(uc, mix, w_r, w_k, w_v, w0, w1, w2, a0, a1, a2, g1, g2, k_k, k_a)
    S0 = jnp.zeros((N_DIR, B, RWKV_HEADS, RWKV_HEAD, RWKV_HEAD), jnp.float32)
    S_ctx, y_c = wkv_bidirectional(S0, *fc[:6])
    fl = rwkv_features(u, mix, w_r, w_k, w_v, w0, w1, w2, a0, a1, a2, g1, g2, k_k, k_a)
    _, y_l = wkv_bidirectional(S_ctx, *fl[:6])
    y = rwkv_output(y_l, fl[0], fl[2], fl[3], fl[6], r_k, ln_w, ln_b, w_o)
    yc = rwkv_output(y_c, fc[0], fc[2], fc[3], fc[6], r_k, ln_w, ln_b, w_o) if ctx_out else None
    return y, yc


def setup_inputs(seed: int = 0) -> dict:
    key = jax.random.key(seed)
    ks = iter(jax.random.split(key, 64))
    f32 = jnp.float32

    def nrm(shape, fan_in, scale=1.0):
        return jax.random.normal(next(ks), shape, f32) * (scale * fan_in ** -0.5)

    def gain(shape):
        return 1.0 + 0.02 * jax.random.normal(next(ks), shape, f32)

    def small(shape, s=0.01):
        return s * jax.random.normal(next(ks), shape, f32)

    def unif(shape, lo, hi):
        return jax.random.uniform(next(ks), shape, f32, lo, hi)

    D, L, F = D_MODEL, DEPTH, D_FF
    NA, NB, NC = N_LAYERS_A, N_LAYERS_B, N_LAYERS_C
    H, N = RWKV_HEADS, RWKV_HEAD
    return {
        'x': jax.random.normal(next(ks), (BATCH, SEQ, D), f32),
        'c': jax.random.normal(next(ks), (BATCH, D), f32),
        'ctx': jax.random.normal(next(ks), (BATCH, CTX_LEN, D), f32),
        'c_ctx': jax.random.normal(next(ks), (D,), f32),
        'mod_w': nrm((L, D, N_SUB * 3 * D), D, 0.5),
        'mod_b': small((L, N_SUB * 3 * D), 0.1),
        'norm_pre': gain((L, N_SUB, D)),
        'norm_post': gain((L, N_SUB, D)),
        'ffn_w_gate': nrm((L, 2, D, F), D),
        'ffn_w_up': nrm((L, 2, D, F), D),
        'ffn_w_down': nrm((L, 2, F, D), F),
        'mla_w_dq': nrm((NA, D, MLA_Q_RANK), D),
        'mla_q_norm': gain((NA, MLA_Q_RANK)),
        'mla_w_uq': nrm((NA, MLA_Q_RANK, MLA_HEADS * (MLA_NOPE + MLA_ROPE)), MLA_Q_RANK),
        'mla_w_dkv': nrm((NA, D, MLA_KV_RANK + MLA_ROPE), D),
        'mla_kv_norm': gain((NA, MLA_KV_RANK)),
        'mla_w_ukv': nrm((NA, MLA_KV_RANK, MLA_HEADS * (MLA_NOPE + MLA_V)), MLA_KV_RANK),
        'mla_w_o': nrm((NA, MLA_HEADS * MLA_V, D), MLA_HEADS * MLA_V),
        'fnet_w_o': nrm((NB, D, D), D),
        'fnet_b_o': small((NB, D)),
        'rwkv_mix': unif((NC, 6, D), 0.0, 1.0),
        'rwkv_w_r': nrm((NC, D, D), D),
        'rwkv_w_k': nrm((NC, D, D), D),
        'rwkv_w_v': nrm((NC, D, D), D),
        'rwkv_w0': unif((NC, N_DIR, D), -5.0, 1.0),
        'rwkv_w1': nrm((NC, N_DIR, D, RWKV_DECAY_LORA), D),
        'rwkv_w2': nrm((NC, N_DIR, RWKV_DECAY_LORA, D), RWKV_DECAY_LORA, 0.1),
        'rwkv_a0': small((NC, N_DIR, D), 0.1),
        'rwkv_a1': nrm((NC, N_DIR, D, RWKV_AAA_LORA), D),
        'rwkv_a2': nrm((NC, N_DIR, RWKV_AAA_LORA, D), RWKV_AAA_LORA, 0.1),
        'rwkv_g1': nrm((NC, D, RWKV_GATE_LORA), D),
        'rwkv_g2': nrm((NC, RWKV_GATE_LORA, D), RWKV_GATE_LORA),
        'rwkv_k_k': 0.85 + small((NC, D), 0.05),
        'rwkv_k_a': 1.0 + small((NC, D), 0.05),
        'rwkv_r_k': small((NC, H, N), 0.1),
        'rwkv_ln_w': gain((NC, D)),
        'rwkv_ln_b': small((NC, D)),
        'rwkv_w_o': nrm((NC, D, D), D),
    }


def reference(x, c, ctx, c_ctx, mod_w, mod_b, norm_pre, norm_post, ffn_w_gate, ffn_w_up, ffn_w_down,
              mla_w_dq, mla_q_norm, mla_w_uq, mla_w_dkv, mla_kv_norm, mla_w_ukv, mla_w_o,
              fnet_w_o, fnet_b_o,
              rwkv_mix, rwkv_w_r, rwkv_w_k, rwkv_w_v, rwkv_w0, rwkv_w1, rwkv_w2, rwkv_a0, rwkv_a1, rwkv_a2,
              rwkv_g1, rwkv_g2, rwkv_k_k, rwkv_k_a, rwkv_r_k, rwkv_ln_w, rwkv_ln_b, rwkv_w_o):
    B, T, D = x.shape
    cos, sin = axial_rope_tables(T)
    h, hc = x, ctx
    sc, scc = jax.nn.silu(c), jax.nn.silu(c_ctx)
    for i in range(DEPTH):
        kind, j, last = i % N_MIXERS, i // N_MIXERS, i == DEPTH - 1
        ctx_in = (not last) or kind != 1
        mod_l = (sc @ mod_w[i] + mod_b[i]).reshape(B, N_SUB, 3, 1, D)
        mod_c = (scc @ mod_w[i] + mod_b[i]).reshape(1, N_SUB, 3, 1, D)
        h = ffn_sublayer(h, mod_l[:, 0], norm_pre[i, 0], norm_post[i, 0],
                         ffn_w_gate[i, 0], ffn_w_up[i, 0], ffn_w_down[i, 0])
        if ctx_in:
            hc = ffn_sublayer(hc, mod_c[:, 0], norm_pre[i, 0], norm_post[i, 0],
                              ffn_w_gate[i, 0], ffn_w_up[i, 0], ffn_w_down[i, 0])
        u = modulate(h, norm_pre[i, 1], mod_l[:, 1])
        uc = modulate(hc, norm_pre[i, 1], mod_c[:, 1]) if ctx_in else None
        if kind == 0:
            y, yc = mla_mixer(u, uc, cos, sin, not last, mla_w_dq[j], mla_q_norm[j], mla_w_uq[j],
                              mla_w_dkv[j], mla_kv_norm[j], mla_w_ukv[j], mla_w_o[j])
        elif kind == 1:
            y, yc = fnet_mixer(u, uc, not last, fnet_w_o[j], fnet_b_o[j])
        else:
            y, yc = rwkv_mixer(u, uc, not last, rwkv_mix[j], rwkv_w_r[j], rwkv_w_k[j], rwkv_w_v[j],
                               rwkv_w0[j], rwkv_w1[j], rwkv_w2[j], rwkv_a0[j], rwkv_a1[j], rwkv_a2[j],
                               rwkv_g1[j], rwkv_g2[j], rwkv_k_k[j], rwkv_k_a[j], rwkv_r_k[j],
                               rwkv_ln_w[j], rwkv_ln_b[j], rwkv_w_o[j])
        h = gated_residual(h, y, norm_post[i, 1], mod_l[:, 1], 1.0)
        h = ffn_sublayer(h, mod_l[:, 2], norm_pre[i, 2], norm_post[i, 2],
                         ffn_w_gate[i, 1], ffn_w_up[i, 1], ffn_w_down[i, 1])
        if not last:
            hc = gated_residual(hc, yc, norm_post[i, 1], mod_c[:, 1], 1.0)
            hc = ffn_sublayer(hc, mod_c[:, 2], norm_pre[i, 2], norm_post[i, 2],
                              ffn_w_gate[i, 1], ffn_w_up[i, 1], ffn_w_down[i, 1])
    return h
```

```python
import numpy as np
import ml_dtypes
from contextlib import ExitStack, contextmanager
import concourse.bass as bass
import concourse.mybir as mybir
from concourse.bass_utils import run_bass_kernel_spmd

F32 = mybir.dt.float32
BF16 = mybir.dt.bfloat16
AF = mybir.ActivationFunctionType
ALU = mybir.AluOpType
AX = mybir.AxisListType

D = 1024
T = 2048
TC = 256
NT = (T + TC) // 128
FF = 2816
NFC = FF // 128
DEPTH = 4
EPS = 1e-6


class Op:
    __slots__ = ("eng", "fn", "waits", "signal", "pos", "dma", "dslot", "dval", "know", "idx", "sigval")


class Prog:
    R = 14
    CE = ("pe", "dve", "act", "pool")
    QE = ("sp", "act", "pool")
    ALLE = ("pe", "dve", "act", "pool", "sp")

    def __init__(self, nc, stack):
        self.nc = nc
        self.sem = {e: stack.enter_context(nc.semaphore("s_" + e)) for e in self.CE}
        self.dsem = {q: [stack.enter_context(nc.semaphore(f"d_{q}{i}")) for i in range(self.R)] for q in self.QE}
        self.sigcount = {e: 0 for e in self.CE}
        self.dq = {q: {"n": 0, "hist": [None] * self.R, "val": [0] * self.R} for q in self.QE}
        self.ops = []
        self.tok = {}
        self.npos = {e: 0 for e in self.CE}
        self.know = {e: {} for e in self.ALLE}
        self.dknown = {e: set() for e in self.ALLE}
        self.phase_start = 0
        self.pstack = None
        self.nphase = 0
        self.same_engine_sync = True

    def sb(self, name, shape, dtype):
        return self.pstack.enter_context(self.nc.sbuf_tensor(f"{name}_{self.nphase}", list(shape), dtype))

    def ps(self, name, shape, dtype=F32):
        return self.pstack.enter_context(self.nc.psum_tensor(f"{name}_{self.nphase}", list(shape), dtype))

    @contextmanager
    def phase(self):
        with ExitStack() as st:
            self.pstack = st
            yield self
            self.flush()
            self.pstack = None
        self.nphase += 1

    def add(self, eng, fn, reads=(), writes=(), dma=False):
        op = Op()
        op.eng, op.fn, op.dma, op.signal, op.idx, op.sigval = eng, fn, dma, False, len(self.ops), None
        deps = set()
        for t in reads:
            s = self.tok.get(t)
            if s is not None and s[0] is not None:
                deps.add(s[0])
        for t in writes:
            s = self.tok.get(t)
            if s is not None:
                if s[0] is not None:
                    deps.add(s[0])
                deps.update(s[1])
        if dma:
            q = self.dq[eng]
            n = q["n"]
            q["n"] += 1
            slot = n % self.R
            op.dslot, op.dval = slot, 16 * (n // self.R + 1)
            q["val"][slot] = op.dval
            if q["hist"][slot] is not None:
                deps.add(q["hist"][slot])
            q["hist"][slot] = op.idx
        know, dk = self.know[eng], self.dknown[eng]
        waits = []
        for d in sorted(deps):
            if d < self.phase_start:
                continue
            Dp = self.ops[d]
            if Dp.dma:
                if d in dk:
                    continue
                dk.add(d)
                waits.append(d)
            else:
                if Dp.eng == eng and (eng == "pe" or not self.same_engine_sync):
                    continue
                if know.get(Dp.eng, -1) >= Dp.pos:
                    continue
                Dp.signal = True
                waits.append(d)
            for e2, p2 in Dp.know.items():
                if know.get(e2, -1) < p2:
                    know[e2] = p2
        op.waits = waits
        if dma:
            op.pos = None
            op.know = dict(know)
        else:
            op.pos = self.npos[eng]
            self.npos[eng] += 1
            op.know = dict(know)
            op.know[eng] = op.pos
        for t in reads:
            self.tok.setdefault(t, [None, []])[1].append(op.idx)
        for t in writes:
            self.tok[t] = [op.idx, []]
        self.ops.append(op)
        return op

    def flush(self):
        ops = self.ops[self.phase_start:]
        last = {}
        for op in ops:
            if not op.dma:
                last[op.eng] = op
        for op in last.values():
            op.signal = True
        for op in ops:
            if not op.dma and op.signal:
                self.sigcount[op.eng] += 1
                op.sigval = self.sigcount[op.eng]
        final_sig = dict(self.sigcount)
        final_d = {q: list(self.dq[q]["val"]) for q in self.QE}
        per = {e: [op for op in ops if op.eng == e] for e in self.ALLE}
        allops = self.ops
        sem, dsem = self.sem, self.dsem

        def mk(name):
            def body(e):
                for op in per[name]:
                    for d in op.waits:
                        Dp = allops[d]
                        if Dp.dma:
                            e.wait_ge(dsem[Dp.eng][Dp.dslot], Dp.dval)
                        else:
                            e.wait_ge(sem[Dp.eng], Dp.sigval)
                    ins = op.fn(e)
                    if op.dma:
                        ins.then_inc(dsem[op.eng][op.dslot], 16)
                    elif op.signal:
                        ins.then_inc(sem[op.eng], 1)
                for e2 in self.CE:
                    if final_sig[e2] > 0:
                        e.wait_ge(sem[e2], final_sig[e2])
                for q in self.QE:
                    for s in range(self.R):
                        if final_d[q][s] > 0:
                            e.wait_ge(dsem[q][s], final_d[q][s])
            return body

        with self.nc.Block() as blk:
            blk.tensor(mk("pe"))
            blk.vector(mk("dve"))
            blk.scalar(mk("act"))
            blk.gpsimd(mk("pool"))
            blk.sync(mk("sp"))
        self.phase_start = len(self.ops)
        for e in self.ALLE:
            self.know[e] = {c: self.npos[c] - 1 for c in self.CE}
            self.dknown[e] = set()

    def dma(self, out, in_, reads, writes, q="sp", **kw):
        return self.add(q, lambda e: e.dma_start(out=out, in_=in_, **kw), reads, writes, dma=True)

    def mm(self, out, lhsT, rhs, start, stop, reads, writes):
        return self.add("pe", lambda e: e.matmul(out, lhsT, rhs, start=start, stop=stop), reads, writes)

    def tr(self, out, in_, ident, reads, writes):
        if in_.dtype == F32:
            return self.add("pe", lambda e: e.matmul(out, in_, ident, start=True, stop=True), reads, writes)
        return self.add("pe", lambda e: e.transpose(out, in_, ident), reads, writes)

    def act(self, out, in_, func, reads, writes, **kw):
        return self.add("act", lambda e: e.activation(out=out, in_=in_, func=func, **kw), reads, writes)

    def tt(self, eng, out, in0, in1, op, reads, writes):
        return self.add(eng, lambda e: e.tensor_tensor(out=out, in0=in0, in1=in1, op=op), reads, writes)

    def ts(self, eng, out, in0, s1, s2, op0, op1, reads, writes):
        if s2 is None:
            return self.add(eng, lambda e: e.tensor_scalar(out=out, in0=in0, scalar1=s1, scalar2=None, op0=op0), reads, writes)
        return self.add(eng, lambda e: e.tensor_scalar(out=out, in0=in0, scalar1=s1, scalar2=s2, op0=op0, op1=op1), reads, writes)

    def stt(self, eng, out, in0, scalar, in1, op0, op1, reads, writes):
        return self.add(eng, lambda e: e.scalar_tensor_tensor(out=out, in0=in0, scalar=scalar, in1=in1, op0=op0, op1=op1), reads, writes)

    def recip(self, out, in_, reads, writes):
        return self.add("dve", lambda e: e.reciprocal(out=out, in_=in_), reads, writes)

    def copy(self, eng, out, in_, reads, writes):
        if eng == "act":
            return self.add("act", lambda e: e.activation(out=out, in_=in_, func=AF.Copy), reads, writes)
        return self.add(eng, lambda e: e.tensor_copy(out=out, in_=in_), reads, writes)


class Ctx:
    pass


def mod_phase(P, C):
    NB = 1152
    with P.phase():
        cc = P.sb("cc", [128, 8, 2], F32)
        sc = P.sb("sc", [128, 8, 2], F32)
        wb = [P.sb(f"mw{i}", [128, 8, NB], F32) for i in range(2)]
        bb = [P.sb(f"mb{i}", [2, NB], F32) for i in range(2)]
        rs = [P.sb(f"mr{i}", [2, NB], F32) for i in range(2)]
        pp = [P.ps(f"mp{i}", [2, 512]) for i in range(4)]
        P.dma(cc[:, :, 0], C.c.rearrange("o (k p) -> p (o k)", p=128), [], ["cc0"], allow_slow_non_contiguous=True)
        P.dma(cc[:, :, 1], C.c_ctx.rearrange("(k p) -> p k", p=128), [], ["cc1"], allow_slow_non_contiguous=True)
        P.act(sc[:], cc[:], AF.Silu, ["cc0", "cc1"], ["sc"])
        it = 0
        pi = 0
        for l in range(DEPTH):
            for cb in range(9216 // NB):
                w = wb[it % 2]
                wt = ("mw", it % 2)
                c0 = cb * NB
                P.dma(w[:], C.mod_w[l, :, c0:c0 + NB].rearrange("(k p) n -> p k n", p=128), [], [wt],
                      q=("sp" if it % 2 == 0 else "pool"))
                P.dma(bb[it % 2][:], C.mod_b[l:l + 1, c0:c0 + NB].broadcast_to([2, NB]), [], [("mb", it % 2)])
                for (o, n) in ((0, 512), (512, 512), (1024, 128)):
                    pt = pp[pi % 4]
                    ptk = ("mp", pi % 4)
                    for k in range(8):
                        P.mm(pt[:, 0:n], sc[:, k, :], w[:, k, o:o + n], k == 0, k == 7, ["sc", wt], [ptk])
                    P.tt("dve", rs[it % 2][:, o:o + n], pt[:, 0:n], bb[it % 2][:, o:o + n], ALU.add,
                         [ptk, ("mb", it % 2)], [("mr", it % 2, o)])
                    pi += 1
                P.dma(C.modv[l, :, c0:c0 + NB], rs[it % 2][:], [("mr", it % 2, 0), ("mr", it % 2, 512), ("mr", it % 2, 1024)],
                      [("modv", l, cb)])
                it += 1


def load_mod_tiles(P, C, l, sub, who, weight, A, Bt, G, tmp1, tmp2, key, k1, k2):
    base = sub * 3 * D
    def row(ap):
        return ap.broadcast_to([128, D])
    P.dma(Bt[:], row(C.modv[l, who:who + 1, base:base + D]), [], [("B", key)])
    P.dma(A[:], row(C.modv[l, who:who + 1, base + D:base + 2 * D]), [], [("A", key)])
    P.dma(G[:], row(C.modv[l, who:who + 1, base + 2 * D:base + 3 * D]), [], [("G", key)])
    P.dma(tmp1[:, 0:D], row(C.norm_pre[l, sub:sub + 1, :]), [], [k1])
    P.dma(tmp2[:, 0:D], row(C.norm_post[l, sub:sub + 1, :]), [], [k2])
    P.stt("dve", A[:], A[:], 1.0, tmp1[:, 0:D], ALU.add, ALU.mult, [("A", key), k1], [("A", key)])
    P.stt("dve", G[:], G[:], float(weight), tmp2[:, 0:D], ALU.mult, ALU.mult, [("G", key), k2], [("G", key)])


def rms_rstd(P, src, junk, ss, rstd, n, reads, key, eps=EPS):
    P.act(junk, src, AF.Square, reads + [], [("junk", key), ("ss", key)], accum_out=ss)
    P.ts("dve", rstd, ss, 1.0 / n, eps, ALU.mult, ALU.add, [("ss", key)], [("rstd", key)])
    P.act(rstd, rstd, AF.Sqrt, [("rstd", key)], [("rstd", key)])
    P.recip(rstd, rstd, [("rstd", key)], [("rstd", key)])


def ffn_phase(P, C, l, f, src, dst, ntiles_lat, do_ctx):
    sub = 0 if f == 0 else 2
    TB = 256
    with P.phase():
        Wg = P.sb("Wg", [128, 8, FF], BF16)
        Wu = P.sb("Wu", [128, 8, FF], BF16)
        Wd = P.sb("Wd", [128, NFC, D], BF16)
        stg = [P.sb(f"stg{i}", [128, 1408], F32) for i in range(2)]
        A = P.sb("A", [128, D], F32)
        Bt = P.sb("B", [128, D], F32)
        G = P.sb("G", [128, D], F32)
        xT = [P.sb(f"xT{i}", [128, 8, TB], BF16) for i in range(2)]
        hT = P.sb("hT", [128, NFC, TB], BF16)
        xb = [P.sb(f"xb{i}", [128, D], F32) for i in range(3)]
        xm = [P.sb(f"xm{i}", [128, D], BF16) for i in range(2)]
        tmp = P.sb("tmp", [128, D], F32)
        junk = P.sb("junk", [128, D], BF16)
        sg = [P.sb(f"sg{i}", [128, TB], F32) for i in range(2)]
        small = P.sb("small", [128, 16], F32)
        ident = P.sb("ident", [128, 128], BF16)
        psT = P.ps("psT", [128, 8, 128], BF16)
        psG = [P.ps(f"psG{i}", [128, 512]) for i in range(2)]
        psU = [P.ps(f"psU{i}", [128, 512]) for i in range(2)]
        psY = P.ps("psY", [128, D])
        P.dma(ident[:], C.ident_bf[:, :], [], ["ident"])
        cv = ["pool", "act", "dve"]
        n = 0
        for (Wsrc, Wdst, nm) in ((C.ffn_w_gate[l, f], Wg, "Wg"), (C.ffn_w_up[l, f], Wu, "Wu")):
            for k in range(8):
                for hh in range(2):
                    s = stg[n % 2]
                    P.dma(s[:], Wsrc[k * 128:(k + 1) * 128, hh * 1408:(hh + 1) * 1408], [], [("stg", n % 2)],
                          q=("sp" if n % 2 == 0 else "act"))
                    P.copy(cv[n % 3], Wdst[:, k, hh * 1408:(hh + 1) * 1408], s[:], [("stg", n % 2)], [(nm, k)])
                    n += 1
        for fc in range(NFC):
            s = stg[n % 2]
            P.dma(s[:, 0:D], C.ffn_w_down[l, f, fc * 128:(fc + 1) * 128, :], [], [("stg", n % 2)],
                  q=("sp" if n % 2 == 0 else "act"))
            P.copy(cv[n % 3], Wd[:, fc, :], s[:, 0:D], [("stg", n % 2)], [("Wd", fc)])
            n += 1
        Wg_r = [("Wg", k) for k in range(8)]
        Wu_r = [("Wu", k) for k in range(8)]
        blocks = [(0, ti) for ti in range(0, ntiles_lat, 2)]
        if do_ctx:
            blocks.append((1, 0))
        cur_who = None
        gi = 0
        for bi, (who, t0) in enumerate(blocks):
            if who != cur_who:
                load_mod_tiles(P, C, l, sub, who, 0.5, A, Bt, G, stg[0], stg[1], "m", ("stg", 0), ("stg", 1))
                cur_who = who
            xTb = xT[bi % 2]
            xtk = ("xT", bi % 2)
            tiles = []
            for j in range(2):
                ti = t0 + j
                x = xb[gi % 3]
                xk = ("xb", gi % 3)
                gi += 1
                tiles.append((ti, x, xk))
                P.dma(x[:], src(who, ti), [("h", who, ti)], [xk])
                ss, rstd = small[:, 0:1], small[:, 1:2]
                rms_rstd(P, x[:], junk[:], ss, rstd, D, [xk], "pre")
                P.stt("dve", tmp[:], x[:], rstd, A[:], ALU.mult, ALU.mult, [xk, ("rstd", "pre"), ("A", "m")], ["tmp"])
                m = xm[j]
                P.tt("dve", m[:], tmp[:], Bt[:], ALU.add, ["tmp", ("B", "m")], [("xm", j)])
                for k in range(8):
                    P.tr(psT[:, k, :], m[:, k * 128:(k + 1) * 128], ident[:], [("xm", j), "ident"], ["psT"])
                P.copy("act", xTb[:, :, j * 128:(j + 1) * 128], psT[:], ["psT"], [(xtk, j)])
            xr = [(xtk, 0), (xtk, 1)]
            for fc in range(NFC):
                pg, pu = psG[fc % 2], psU[fc % 2]
                for k in range(8):
                    P.mm(pg[:, 0:TB], Wg[:, k, fc * 128:(fc + 1) * 128], xTb[:, k, :], k == 0, k == 7,
                         [("Wg", k)] + xr, [("psG", fc % 2)])
                for k in range(8):
                    P.mm(pu[:, 0:TB], Wu[:, k, fc * 128:(fc + 1) * 128], xTb[:, k, :], k == 0, k == 7,
                         [("Wu", k)] + xr, [("psU", fc % 2)])
                P.act(sg[fc % 2][:], pg[:, 0:TB], AF.Silu, [("psG", fc % 2)], [("sg", fc % 2)])
                P.tt("dve", hT[:, fc, :], sg[fc % 2][:], pu[:, 0:TB], ALU.mult, [("sg", fc % 2), ("psU", fc % 2)], [("hT", fc)])
            for j, (ti, x, xk) in enumerate(tiles):
                for nh in range(2):
                    for fc in range(NFC):
                        P.mm(psY[:, nh * 512:(nh + 1) * 512], hT[:, fc, j * 128:(j + 1) * 128],
                             Wd[:, fc, nh * 512:(nh + 1) * 512], fc == 0, fc == NFC - 1,
                             [("hT", fc), ("Wd", fc)], ["psY"])
                ss, rstd = small[:, 2:3], small[:, 3:4]
                rms_rstd(P, psY[:], junk[:], ss, rstd, D, ["psY"], "post")
                P.stt("dve", tmp[:], psY[:], rstd, G[:], ALU.mult, ALU.mult, ["psY", ("rstd", "post"), ("G", "m")], ["tmp"])
                P.tt("pool", x[:], tmp[:], x[:], ALU.add, ["tmp", xk], [xk])
                P.dma(dst(who, ti), x[:], [xk], [("h", who, ti)])


class CB:
    pass


def alloc_common(P, nxb=2):
    cb = CB()
    cb.A = P.sb("cA", [128, D], F32)
    cb.Bt = P.sb("cB", [128, D], F32)
    cb.G = P.sb("cG", [128, D], F32)
    cb.t1 = P.sb("ct1", [128, D], F32)
    cb.t2 = P.sb("ct2", [128, D], F32)
    cb.xb = [P.sb(f"cxb{i}", [128, D], F32) for i in range(nxb)]
    cb.xm = [P.sb(f"cxm{i}", [128, D], BF16) for i in range(2)]
    cb.tmp = P.sb("ctmp", [128, D], F32)
    cb.junk = P.sb("cjunk", [128, D], BF16)
    cb.small = P.sb("csmall", [128, 16], F32)
    cb.ident = P.sb("cident", [128, 128], BF16)
    cb.psT = P.ps("cpsT", [128, 8, 128], BF16)
    cb.gi = 0
    cb.mi = 0
    P.dma(cb.ident[:], P.C.ident_bf[:, :], [], ["ident"])
    return cb


def mod_tiles(P, cb, l, sub, who, weight):
    load_mod_tiles(P, P.C, l, sub, who, weight, cb.A, cb.Bt, cb.G, cb.t1, cb.t2, "m", "ct1", "ct2")


def pro_tile(P, cb, src_ap, htok):
    x = cb.xb[cb.gi % len(cb.xb)]
    xk = ("cxb", cb.gi % len(cb.xb))
    cb.gi += 1
    P.dma(x[:], src_ap, [htok], [xk])
    ss, rstd = cb.small[:, 0:1], cb.small[:, 1:2]
    rms_rstd(P, x[:], cb.junk[:], ss, rstd, D, [xk], "pre")
    P.stt("dve", cb.tmp[:], x[:], rstd, cb.A[:], ALU.mult, ALU.mult, [xk, ("rstd", "pre"), ("A", "m")], ["ctmp"])
    j = cb.mi % 2
    cb.mi += 1
    m = cb.xm[j]
    P.tt("pool", m[:], cb.tmp[:], cb.Bt[:], ALU.add, ["ctmp", ("B", "m")], [("cxm", j)])
    for k in range(8):
        P.tr(cb.psT[:, k, :], m[:, k * 128:(k + 1) * 128], cb.ident[:], [("cxm", j), "ident"], ["cpsT"])
    return x, xk


def epi_tile(P, cb, psY, pkeys, x, xk, dst_ap, htok, bias=None):
    ss, rstd = cb.small[:, 2:3], cb.small[:, 3:4]
    src = psY
    sk = list(pkeys)
    if bias is not None:
        P.tt("dve", cb.t1[:], psY, bias[0], ALU.add, sk + [bias[1]], ["ct1"])
        src = cb.t1[:]
        sk = ["ct1"]
    rms_rstd(P, src, cb.junk[:], ss, rstd, D, sk, "post")
    P.stt("dve", cb.tmp[:], src, rstd, cb.G[:], ALU.mult, ALU.mult, sk + [("rstd", "post"), ("G", "m")], ["ctmp"])
    P.tt("pool", x[:], cb.tmp[:], x[:], ALU.add, ["ctmp", xk], [xk])
    P.dma(dst_ap, x[:], [xk], [htok])


def load_w_bf16(P, dst, src_ap, stg, nk, ncols, name, cnt):
    cv = ["pool", "act", "dve"]
    for k in range(nk):
        n = cnt[0]
        s = stg[n % 2]
        P.dma(s[:, 0:ncols], src_ap[k * 128:(k + 1) * 128, :], [], [("cstg", n % 2)], q=("sp" if n % 2 == 0 else "act"))
        P.copy(cv[n % 3], dst[:, k, 0:ncols], s[:, 0:ncols], [("cstg", n % 2)], [(name, k)])
        cnt[0] += 1


MLA_SCALE = 192.0 ** -0.5


def mla_proj_phase(P, C, l, j, ctx_q):
    with P.phase():
        cb = alloc_common(P)
        stg = [P.sb(f"stg{i}", [128, 1024], F32) for i in range(2)]
        Wdq = P.sb("Wdq", [128, 8, 512], BF16)
        Wdc = P.sb("Wdc", [128, 8, 256], BF16)
        Wdr = P.sb("Wdr", [128, 8, 128], BF16)
        Wdrp = P.sb("Wdrp", [128, 8, 128], BF16)
        Wqn = P.sb("Wqn", [128, 4, 1024], BF16)
        Wqr = P.sb("Wqr", [128, 4, 512], BF16)
        Wqrp = P.sb("Wqrp", [128, 4, 512], BF16)
        Wk = P.sb("Wk", [128, 2, 1024], BF16)
        Wv = P.sb("Wv", [128, 2, 1024], BF16)
        cnt = [0]
        load_w_bf16(P, Wdq, C.mla_w_dq[j], stg, 8, 512, "Wdq", cnt)
        load_w_bf16(P, Wdc, C.mla_dkv_c[j], stg, 8, 256, "Wdc", cnt)
        load_w_bf16(P, Wdr, C.mla_dkv_r[j], stg, 8, 128, "Wdr", cnt)
        load_w_bf16(P, Wdrp, C.mla_dkv_rp[j], stg, 8, 128, "Wdrp", cnt)
        load_w_bf16(P, Wqn, C.mla_uq_n[j], stg, 4, 1024, "Wqn", cnt)
        load_w_bf16(P, Wqr, C.mla_uq_r[j], stg, 4, 512, "Wqr", cnt)
        load_w_bf16(P, Wqrp, C.mla_uq_rp[j], stg, 4, 512, "Wqrp", cnt)
        load_w_bf16(P, Wk, C.mla_ukv_k[j], stg, 2, 1024, "Wk", cnt)
        load_w_bf16(P, Wv, C.mla_ukv_v[j], stg, 2, 1024, "Wv", cnt)
        gq = P.sb("gq", [128, 512], F32)
        gkv = P.sb("gkv", [128, 256], F32)
        P.dma(gq[:], C.mla_q_norm[j:j + 1, :].broadcast_to([128, 512]), [], ["gq"])
        P.dma(gkv[:], C.mla_kv_norm[j:j + 1, :].broadcast_to([128, 256]), [], ["gkv"])
        cos2 = P.sb("cos2", [128, T], F32)
        sin2 = P.sb("sin2", [128, T], F32)
        P.dma(cos2[:], C.rope_cos[:, :], [], ["cos2"])
        P.dma(sin2[:], C.rope_sin[:, :], [], ["sin2"], q="act")
        uT = P.sb("uT", [128, 8, 512], BF16)
        cqnT = P.sb("cqnT", [128, 4, 512], BF16)
        ckvnT = P.sb("ckvnT", [128, 2, 512], BF16)
        cqn = P.sb("cqn", [128, 512], BF16)
        ckvn = P.sb("ckvn", [128, 256], BF16)
        Vt = [P.sb(f"Vt{i}", [128, 1024], BF16) for i in range(2)]
        QTb = P.sb("QTb", [128, 8, 512], BF16)
        KTb = P.sb("KTb", [128, 8, 512], BF16)
        QrTb = P.sb("QrTb", [128, 4, 512], BF16)
        KrTb = P.sb("KrTb", [128, 512], BF16)
        r1 = P.sb("r1", [128, 512], F32)
        r2 = P.sb("r2", [128, 512], F32)
        sm2 = P.sb("sm2", [128, 8], F32)
        psq = P.ps("psq", [128, 512])
        pskv = P.ps("pskv", [128, 512])
        psT2 = P.ps("psT2", [128, 8, 128], BF16)
        psV = P.ps("psV", [128, 1024])
        psB = [P.ps(f"psB{i}", [128, 512]) for i in range(2)]
        blocks = [(0, t0, 4) for t0 in range(0, 16, 4)] + [(1, 0, 2)]
        cur = None
        bctr = 0
        for (who, t0, nt) in blocks:
            if who != cur:
                mod_tiles(P, cb, l, 1, who, 1.0)
                cur = who
            nq = nt * 128
            c0 = t0 * 128 + (T if who == 1 else 0)
            for jt in range(nt):
                ti = t0 + jt
                gt = ti + (16 if who == 1 else 0)
                src = (C.hL if who == 0 else C.hC)[ti * 128:(ti + 1) * 128, :]
                pro_tile(P, cb, src, ("h", who, ti))
                P.copy("act", uT[:, :, jt * 128:(jt + 1) * 128], cb.psT[:], ["cpsT"], [("uT", jt)])
                for k in range(8):
                    P.mm(psq[:, :], uT[:, k, jt * 128:(jt + 1) * 128], Wdq[:, k, :], k == 0, k == 7, [("uT", jt), ("Wdq", k)], ["psq"])
                for k in range(8):
                    P.mm(pskv[:, 0:256], uT[:, k, jt * 128:(jt + 1) * 128], Wdc[:, k, :], k == 0, k == 7, [("uT", jt), ("Wdc", k)], ["pskv"])
                P.act(cb.junk[:, 0:512], psq[:, :], AF.Square, ["psq"], [("junk", "q"), "ssq"], accum_out=sm2[:, 0:1])
                P.ts("dve", sm2[:, 1:2], sm2[:, 0:1], 1.0 / 512, EPS, ALU.mult, ALU.add, ["ssq"], ["rq"])
                P.act(sm2[:, 1:2], sm2[:, 1:2], AF.Sqrt, ["rq"], ["rq"])
                P.recip(sm2[:, 1:2], sm2[:, 1:2], ["rq"], ["rq"])
                P.stt("dve", cqn[:], psq[:, :], sm2[:, 1:2], gq[:], ALU.mult, ALU.mult, ["psq", "rq", "gq"], ["cqn"])
                P.act(cb.junk[:, 512:768], pskv[:, 0:256], AF.Square, ["pskv"], [("junk", "kv"), "sskv"], accum_out=sm2[:, 2:3])
                P.ts("dve", sm2[:, 3:4], sm2[:, 2:3], 1.0 / 256, EPS, ALU.mult, ALU.add, ["sskv"], ["rkv"])
                P.act(sm2[:, 3:4], sm2[:, 3:4], AF.Sqrt, ["rkv"], ["rkv"])
                P.recip(sm2[:, 3:4], sm2[:, 3:4], ["rkv"], ["rkv"])
                P.stt("dve", ckvn[:], pskv[:, 0:256], sm2[:, 3:4], gkv[:], ALU.mult, ALU.mult, ["pskv", "rkv", "gkv"], ["ckvn"])
                for r in range(4):
                    P.tr(psT2[:, r, :], cqn[:, r * 128:(r + 1) * 128], cb.ident[:], ["cqn", "ident"], ["psT2"])
                for r in range(2):
                    P.tr(psT2[:, 4 + r, :], ckvn[:, r * 128:(r + 1) * 128], cb.ident[:], ["ckvn", "ident"], ["psT2"])
                P.copy("act", cqnT[:, :, jt * 128:(jt + 1) * 128], psT2[:, 0:4, :], [], ["psT2", ("cqnT", jt)])
                P.copy("act", ckvnT[:, :, jt * 128:(jt + 1) * 128], psT2[:, 4:6, :], [], ["psT2", ("ckvnT", jt)])
                for nh in range(2):
                    for r in range(2):
                        P.mm(psV[:, nh * 512:(nh + 1) * 512], ckvnT[:, r, jt * 128:(jt + 1) * 128], Wv[:, r, nh * 512:(nh + 1) * 512],
                             r == 0, r == 1, [("ckvnT", jt), ("Wv", r)], ["psV"])
                vt = Vt[gt % 2]
                P.copy("act", vt[:], psV[:], ["psV"], [("Vt", gt % 2)])
                P.dma(C.sV[:, gt, :], vt[:], [("Vt", gt % 2)], [("sV", gt)])
            uTr = [("uT", jt) for jt in range(nt)]
            cqr = [("cqnT", jt) for jt in range(nt)]
            ckr = [("ckvnT", jt) for jt in range(nt)]
            for h in range(8):
                pb = psB[bctr % 2]; pk = ("psB", bctr % 2); bctr += 1
                for r in range(4):
                    P.mm(pb[:, 0:nq], Wqn[:, r, h * 128:(h + 1) * 128], cqnT[:, r, 0:nq], r == 0, r == 3, cqr + [("Wqn", r)], [pk])
                P.copy("act" if h % 2 == 0 else "dve", QTb[:, h, 0:nq], pb[:, 0:nq], [pk], [("QTb", h)])
                pb = psB[bctr % 2]; pk = ("psB", bctr % 2); bctr += 1
                for r in range(2):
                    P.mm(pb[:, 0:nq], Wk[:, r, h * 128:(h + 1) * 128], ckvnT[:, r, 0:nq], r == 0, r == 1, ckr + [("Wk", r)], [pk])
                P.copy("dve" if h % 2 == 0 else "act", KTb[:, h, 0:nq], pb[:, 0:nq], [pk], [("KTb", h)])
            jobs = [(Wqr, Wqrp, 4, hp, cqnT, cqr, "Wqr", "Wqrp", QrTb[:, hp, 0:nq], ("QrTb", hp)) for hp in range(4)]
            jobs.append((Wdr, Wdrp, 8, 0, uT, uTr, "Wdr", "Wdrp", KrTb[:, 0:nq], ("KrTb", 0)))
            for (Wa, Wb, nk, hp, rhsT, rk, na, nb_, dst, dk) in jobs:
                pa = psB[bctr % 2]; pka = ("psB", bctr % 2); bctr += 1
                for r in range(nk):
                    P.mm(pa[:, 0:nq], Wa[:, r, hp * 128:(hp + 1) * 128], rhsT[:, r, 0:nq], r == 0, r == nk - 1, rk + [(na, r)], [pka])
                if who == 1:
                    P.copy("act", dst, pa[:, 0:nq], [pka], [dk])
                    continue
                P.tt("dve", r1[:, 0:nq], pa[:, 0:nq], cos2[:, c0:c0 + nq], ALU.mult, [pka, "cos2"], ["r1"])
                pb = psB[bctr % 2]; pkb = ("psB", bctr % 2); bctr += 1
                for r in range(nk):
                    P.mm(pb[:, 0:nq], Wb[:, r, hp * 128:(hp + 1) * 128], rhsT[:, r, 0:nq], r == 0, r == nk - 1, rk + [(nb_, r)], [pkb])
                P.tt("dve", r2[:, 0:nq], pb[:, 0:nq], sin2[:, c0:c0 + nq], ALU.mult, [pkb, "sin2"], ["r2"])
                P.tt("pool", dst, r1[:, 0:nq], r2[:, 0:nq], ALU.add, ["r1", "r2"], [dk])
            P.dma(C.sQT[:, :, c0:c0 + nq], QTb[:, :, 0:nq], [("QTb", h) for h in range(8)], [("sQT", c0)])
            P.dma(C.sKT[:, :, c0:c0 + nq], KTb[:, :, 0:nq], [("KTb", h) for h in range(8)], [("sKT", c0)], q="act")
            P.dma(C.sQrT[:, :, c0:c0 + nq], QrTb[:, :, 0:nq], [("QrTb", hp) for hp in range(4)], [("sQrT", c0)])
            P.dma(C.sKrT[:, c0:c0 + nq], KrTb[:, 0:nq], [("KrTb", 0)], [("sKrT", c0)], q="act")


def mla_attn_phase(P, C, l, j, ctx_q):
    with P.phase():
        cb = alloc_common(P)
        stg = [P.sb(f"stg{i}", [128, 1024], F32) for i in range(2)]
        Wo = P.sb("Wo", [128, 8, 1024], BF16)
        cnt = [0]
        load_w_bf16(P, Wo, C.mla_w_o[j], stg, 8, 1024, "Wo", cnt)
        KT = P.sb("KT", [128, 8, T + TC], BF16)
        KrT = P.sb("KrT", [128, T + TC], BF16)
        V = P.sb("V", [128, NT, 1024], BF16)
        P.dma(KT[:], C.sKT[:, :, :], [], ["KT"])
        P.dma(KrT[:], C.sKrT[:, :], [], ["KrT"], q="act")
        P.dma(V[:, 0:9, :], C.sV[:, 0:9, :], [], ["V0"])
        P.dma(V[:, 9:18, :], C.sV[:, 9:18, :], [], ["V1"], q="act")
        ones = P.sb("ones", [128, 128], BF16)
        P.add("pool", lambda e: e.memset(ones[:], 1.0), [], ["ones"])
        QTb = [P.sb(f"QTb{i}", [128, 8, 512], BF16) for i in range(2)]
        QrTb = [P.sb(f"QrTb{i}", [128, 4, 512], BF16) for i in range(2)]
        OTb = P.sb("OTb", [128, 8, 512], BF16)
        PT = [P.sb(f"PT{i}", [128, 512], BF16) for i in range(3)]
        rz = P.sb("rz", [128, 512], F32)
        psS = [P.ps(f"psS{i}", [128, 512]) for i in range(2)]
        psO = P.ps("psO", [128, 512])
        psZ = P.ps("psZ", [128, 512])
        psY = P.ps("psY", [128, 1024])
        blocks = [(0, t0, 4, list(range(NT))) for t0 in range(0, 16, 4)]
        if ctx_q:
            blocks.append((1, 0, 2, [16, 17]))
        cur = None
        sc_ = 0
        pc_ = 0
        for bi, (who, t0, nt, kcs) in enumerate(blocks):
            if who != cur:
                mod_tiles(P, cb, l, 1, who, 1.0)
                cur = who
            nq = nt * 128
            c0 = t0 * 128 + (T if who == 1 else 0)
            qt, qr = QTb[bi % 2], QrTb[bi % 2]
            qk, qrk = ("QTb", bi % 2), ("QrTb", bi % 2)
            P.dma(qt[:, :, 0:nq], C.sQT[:, :, c0:c0 + nq], [], [qk])
            P.dma(qr[:, :, 0:nq], C.sQrT[:, :, c0:c0 + nq], [], [qrk], q="act")
            for h in range(8):
                pbase = (h % 2) * 64
                pend = []
                n = len(kcs)
                for i in range(n + 1):
                    if i < n:
                        kc = kcs[i]
                        ps_ = psS[sc_ % 2]; psk = ("psS", sc_ % 2); sc_ += 1
                        P.mm(ps_[:, 0:nq], KT[:, h, kc * 128:(kc + 1) * 128], qt[:, h, 0:nq], True, False, ["KT", qk], [psk])
                        P.mm(ps_[:, 0:nq], KrT[pbase:pbase + 64, kc * 128:(kc + 1) * 128], qr[pbase:pbase + 64, h // 2, 0:nq],
                             False, True, ["KrT", qrk], [psk])
                        pt = PT[pc_ % 3]; ptk = ("PT", pc_ % 3); pc_ += 1
                        P.act(pt[:, 0:nq], ps_[:, 0:nq], AF.Exp, [psk], [ptk], scale=MLA_SCALE)
                        pend.append((kc, pt, ptk))
                    if i > 0:
                        kc, pt, ptk = pend[i - 1]
                        P.mm(psO[:, 0:nq], V[:, kc, h * 128:(h + 1) * 128], pt[:, 0:nq], i == 1, i == n, ["V0", "V1", ptk], ["psO"])
                        P.mm(psZ[:, 0:nq], ones[:], pt[:, 0:nq], i == 1, i == n, ["ones", ptk], ["psZ"])
                P.recip(rz[:, 0:nq], psZ[:, 0:nq], ["psZ"], ["rz"])
                P.tt("dve", OTb[:, h, 0:nq], psO[:, 0:nq], rz[:, 0:nq], ALU.mult, ["psO", "rz"], [("OTb", h)])
            for jt in range(nt):
                ti = t0 + jt
                hap = (C.hL if who == 0 else C.hC)[ti * 128:(ti + 1) * 128, :]
                x = cb.xb[cb.gi % 2]; xk = ("cxb", cb.gi % 2); cb.gi += 1
                P.dma(x[:], hap, [("h", who, ti)], [xk])
                for nh in range(2):
                    for h in range(8):
                        P.mm(psY[:, nh * 512:(nh + 1) * 512], OTb[:, h, jt * 128:(jt + 1) * 128], Wo[:, h, nh * 512:(nh + 1) * 512],
                             h == 0, h == 7, [("OTb", h), ("Wo", h)], ["psY"])
                epi_tile(P, cb, psY[:], ["psY"], x, xk, hap, ("h", who, ti))


def _perm64():
    p = np.arange(64)
    return np.where((p % 32) < 16, p + 16, p - 16)


def host_layout(inputs):
    o = {}
    perm = _perm64()
    wdkv = inputs["mla_w_dkv"]
    o["mla_dkv_c"] = np.ascontiguousarray(wdkv[:, :, :256])
    r = wdkv[:, :, 256:320]
    o["mla_dkv_r"] = np.ascontiguousarray(np.concatenate([r, r], axis=-1))
    rp = r[:, :, perm]
    o["mla_dkv_rp"] = np.ascontiguousarray(np.concatenate([rp, rp], axis=-1))
    wuq = inputs["mla_w_uq"].reshape(-1, 512, 8, 192)
    o["mla_uq_n"] = np.ascontiguousarray(wuq[:, :, :, :128].reshape(-1, 512, 1024))
    o["mla_uq_r"] = np.ascontiguousarray(wuq[:, :, :, 128:].reshape(-1, 512, 512))
    o["mla_uq_rp"] = np.ascontiguousarray(wuq[:, :, :, 128:][:, :, :, perm].reshape(-1, 512, 512))
    wukv = inputs["mla_w_ukv"].reshape(-1, 256, 8, 256)
    o["mla_ukv_k"] = np.ascontiguousarray(wukv[:, :, :, :128].reshape(-1, 256, 1024))
    o["mla_ukv_v"] = np.ascontiguousarray(wukv[:, :, :, 128:].reshape(-1, 256, 1024))
    for k in ("mla_w_dq", "mla_q_norm", "mla_kv_norm", "mla_w_o", "c_ctx", "mod_w", "mod_b", "norm_pre", "norm_post",
              "ffn_w_gate", "ffn_w_up", "ffn_w_down"):
        o[k] = np.ascontiguousarray(inputs[k])
    o["ident_bf"] = np.eye(128, dtype=np.float32).astype(ml_dtypes.bfloat16)
    t = np.arange(T)
    p = np.arange(64)
    axis, half, pair = p // 32, (p % 32) // 16, p % 16
    inv = (np.float32(10000.0) ** (-(pair.astype(np.float32)) / np.float32(16.0))).astype(np.float32)
    pos = np.where(axis[:, None] == 0, (t // 64)[None, :], (t % 64)[None, :]).astype(np.float32)
    ang = (pos * inv[:, None]).astype(np.float32)
    cs = np.cos(ang).astype(np.float32)
    sn = (np.sin(ang) * np.where(half == 0, -1.0, 1.0)[:, None]).astype(np.float32)
    def dft(n):
        k = (np.arange(n)[:, None] * np.arange(n)[None, :]) % n
        a = 2.0 * np.pi * k.astype(np.float64) / n
        return np.cos(a), np.sin(a)
    bf = ml_dtypes.bfloat16
    c_, s_ = dft(T)
    o["dft_ct"] = c_.astype(np.float32).astype(bf); o["dft_st"] = s_.astype(np.float32).astype(bf)
    c_, s_ = dft(TC)
    o["dft_ct_c"] = c_.astype(np.float32).astype(bf); o["dft_st_c"] = s_.astype(np.float32).astype(bf)
    c_, s_ = dft(128)
    o["dft_cc"] = c_.astype(np.float32).astype(bf); o["dft_scn"] = (-s_).astype(np.float32).astype(bf)
    for k in ("fnet_w_o", "fnet_b_o", "rwkv_mix", "rwkv_w_r", "rwkv_w_k", "rwkv_w_v", "rwkv_w0", "rwkv_w2", "rwkv_a0", "rwkv_a2",
              "rwkv_g1", "rwkv_g2", "rwkv_k_k", "rwkv_k_a", "rwkv_r_k", "rwkv_ln_w", "rwkv_ln_b", "rwkv_w_o"):
        o[k] = np.ascontiguousarray(inputs[k])
    o["rwkv_w1c"] = np.ascontiguousarray(np.concatenate([inputs["rwkv_w1"][:, 0], inputs["rwkv_w1"][:, 1]], axis=-1))
    o["rwkv_a1c"] = np.ascontiguousarray(np.concatenate([inputs["rwkv_a1"][:, 0], inputs["rwkv_a1"][:, 1]], axis=-1))
    o["ident_f"] = np.eye(128, dtype=np.float32)
    ii = np.arange(128)
    s_, t_ = ii[:, None], ii[None, :]
    tri0 = (s_ <= t_).astype(np.float32); tri1 = (s_ >= t_).astype(np.float32)
    st0 = (s_ < t_).astype(np.float32); st1 = (s_ > t_).astype(np.float32)
    o["rw_tri"] = np.stack([tri0, tri1])
    rep4 = lambda m: np.ascontiguousarray(np.concatenate([m] * 4, axis=1))
    o["rw_mS"] = np.stack([rep4(st0), rep4(st1)])
    o["rw_mI"] = np.stack([rep4(tri0), rep4(tri1)])
    bd = np.kron(np.eye(4, dtype=np.float32), np.ones((32, 32), np.float32))
    o["rw_mSd"] = np.stack([rep4(st0 * bd), rep4(st1 * bd)])
    o["rw_mSo"] = np.stack([rep4(st0 * (1 - bd)), rep4(st1 * (1 - bd))])
    o["rw_mTd"] = np.stack([rep4(st0.T * bd), rep4(st1.T * bd)])
    o["rw_I4"] = rep4(np.eye(128, dtype=np.float32))
    o["rope_cos"] = np.ascontiguousarray(np.concatenate([cs, cs], axis=0))
    o["rope_sin"] = np.ascontiguousarray(np.concatenate([sn, sn], axis=0))
    return o


IN_SHAPES = {
    "c_ctx": ([D], F32), "mod_w": ([DEPTH, D, 9 * D], F32), "mod_b": ([DEPTH, 9 * D], F32),
    "norm_pre": ([DEPTH, 3, D], F32), "norm_post": ([DEPTH, 3, D], F32),
    "ffn_w_gate": ([DEPTH, 2, D, FF], F32), "ffn_w_up": ([DEPTH, 2, D, FF], F32), "ffn_w_down": ([DEPTH, 2, FF, D], F32),
    "ident_bf": ([128, 128], BF16), "rope_cos": ([128, T], F32), "rope_sin": ([128, T], F32),
    "mla_w_dq": ([2, D, 512], F32), "mla_q_norm": ([2, 512], F32), "mla_kv_norm": ([2, 256], F32), "mla_w_o": ([2, D, D], F32),
    "mla_dkv_c": ([2, D, 256], F32), "mla_dkv_r": ([2, D, 128], F32), "mla_dkv_rp": ([2, D, 128], F32),
    "mla_uq_n": ([2, 512, 1024], F32), "mla_uq_r": ([2, 512, 512], F32), "mla_uq_rp": ([2, 512, 512], F32),
    "mla_ukv_k": ([2, 256, 1024], F32), "mla_ukv_v": ([2, 256, 1024], F32),
    "fnet_w_o": ([1, D, D], F32), "fnet_b_o": ([1, D], F32),
    "dft_ct": ([T, T], BF16), "dft_st": ([T, T], BF16), "dft_ct_c": ([TC, TC], BF16), "dft_st_c": ([TC, TC], BF16),
    "dft_cc": ([128, 128], BF16), "dft_scn": ([128, 128], BF16),
    "rwkv_mix": ([1, 6, D], F32), "rwkv_w_r": ([1, D, D], F32), "rwkv_w_k": ([1, D, D], F32), "rwkv_w_v": ([1, D, D], F32),
    "rwkv_w0": ([1, 2, D], F32), "rwkv_w2": ([1, 2, 64, D], F32), "rwkv_a0": ([1, 2, D], F32), "rwkv_a2": ([1, 2, 64, D], F32),
    "rwkv_g1": ([1, D, 160], F32), "rwkv_g2": ([1, 160, D], F32), "rwkv_k_k": ([1, D], F32), "rwkv_k_a": ([1, D], F32),
    "rwkv_r_k": ([1, 16, 64], F32), "rwkv_ln_w": ([1, D], F32), "rwkv_ln_b": ([1, D], F32), "rwkv_w_o": ([1, D, D], F32),
    "rwkv_w1c": ([1, D, 128], F32), "rwkv_a1c": ([1, D, 128], F32), "ident_f": ([128, 128], F32),
    "rw_tri": ([2, 128, 128], F32), "rw_mS": ([2, 128, 512], F32), "rw_mI": ([2, 128, 512], F32), "rw_mSd": ([2, 128, 512], F32), "rw_mSo": ([2, 128, 512], F32),
    "rw_mTd": ([2, 128, 512], F32), "rw_I4": ([128, 512], F32),
}


def build(nsteps=None, dbg=False):
    nc = bass.Bass("TRN2", target_bir_lowering=False)
    C = Ctx()

    def din(name, shape, dt=F32):
        return nc.dram_tensor(name, list(shape), dt, kind="ExternalInput").ap()

    def scratch(name, shape, dt=F32):
        return nc.dram_tensor(name, list(shape), dt, kind="Internal").ap()

    C.x = din("x", [T, D])
    C.c = din("c", [1, D])
    C.ctx = din("ctx", [TC, D])
    for k, (shp, dt) in IN_SHAPES.items():
        setattr(C, k, din(k, shp, dt))
    C.out = nc.dram_tensor("out", [T, D], F32, kind="ExternalOutput").ap()
    C.modv = scratch("modv", [DEPTH, 2, 9 * D])
    C.hL = scratch("hL", [T, D])
    C.hC = scratch("hC", [TC, D])
    C.sQT = scratch("sQT", [128, 8, T + TC], BF16)
    C.sQrT = scratch("sQrT", [128, 4, T + TC], BF16)
    C.sKT = scratch("sKT", [128, 8, T + TC], BF16)
    C.sKrT = scratch("sKrT", [128, T + TC], BF16)
    C.sV = scratch("sV", [128, NT, 1024], BF16)
    for nm in ("sR", "sK", "sVv", "sKK", "sG"):
        setattr(C, nm, scratch(nm, [NTOK, D]))
    C.sLW = scratch("sLW", [2, NTOK, D])
    C.sA = scratch("sA", [2, NTOK, D])
    C.sY = scratch("sY", [2, NTOK, D])
    if dbg:
        C.dbg_hC = nc.dram_tensor("dbg_hC", [TC, D], F32, kind="ExternalOutput").ap()

    def tile_ap(base_l, base_c):
        def f(who, ti):
            b = base_l if who == 0 else base_c
            return b[ti * 128:(ti + 1) * 128, :]
        return f

    steps = []
    for l in range(DEPTH):
        steps += [("ffn", l, 0), ("mix", l), ("ffn", l, 1)]
    if nsteps is not None:
        steps = steps[:nsteps]
    with ExitStack() as st:
        P = Prog(nc, st)
        P.C = C
        mod_phase(P, C)
        for si, stp in enumerate(steps):
            l = stp[1]
            kind, j, last = l % 3, l // 3, l == DEPTH - 1
            final = (nsteps is None and si == len(steps) - 1)
            if stp[0] == "ffn":
                f = stp[2]
                src = tile_ap(C.x, C.ctx) if si == 0 else tile_ap(C.hL, C.hC)
                dst = tile_ap(C.out, C.hC) if final else tile_ap(C.hL, C.hC)
                ffn_phase(P, C, l, f, src, dst, T // 128, (f == 0) or (not last))
            else:
                if kind == 0:
                    mla_proj_phase(P, C, l, j, not last)
                    mla_attn_phase(P, C, l, j, not last)
                elif kind == 1:
                    fnet_phase(P, C, l, j, not last)
                else:
                    rwkv_phases(P, C, l, j, not last)
        if nsteps is not None:
            with P.phase():
                P.dma(C.out[:, :], C.hL[:, :], [], ["dm"])
                if dbg:
                    P.dma(C.dbg_hC[:, :], C.hC[:, :], [], ["dh"])
    return nc


_NC_CACHE = {}


def make_in_maps(inputs, ncores=8):
    shared = host_layout(inputs)
    in_maps = []
    for b in range(ncores):
        m = dict(shared)
        m["x"] = np.ascontiguousarray(inputs["x"][b])
        m["c"] = np.ascontiguousarray(inputs["c"][b:b + 1])
        m["ctx"] = np.ascontiguousarray(inputs["ctx"][b])
        in_maps.append(m)
    return in_maps


def kernel(**inputs):
    inputs = {k: np.asarray(v) for k, v in inputs.items()}
    if "nc" not in _NC_CACHE:
        _NC_CACHE["nc"] = build()
    nc = _NC_CACHE["nc"]
    in_maps = make_in_maps(inputs, 8)
    res = run_bass_kernel_spmd(nc, in_maps, core_ids=list(range(8)))
    return np.stack([np.asarray(r["out"]) for r in res.results], axis=0).astype(np.float32)


def fnet_phase(P, C, l, j, ctx_out):
    with P.phase():
        cb = alloc_common(P)
        stg = [P.sb(f"stg{i}", [128, 1024], F32) for i in range(2)]
        Wo = P.sb("Wo", [128, 8, 1024], BF16)
        cnt = [0]
        load_w_bf16(P, Wo, C.fnet_w_o[j], stg, 8, 1024, "Wo", cnt)
        bo = P.sb("bo", [128, D], F32)
        P.dma(bo[:], C.fnet_b_o[j:j + 1, :].broadcast_to([128, D]), [], ["bo"])
        cc = P.sb("cc", [128, 128], BF16)
        scn = P.sb("scn", [128, 128], BF16)
        P.dma(cc[:], C.dft_cc[:, :], [], ["cc"])
        P.dma(scn[:], C.dft_scn[:, :], [], ["scn"])
        Zc = P.sb("Zc", [128, 16, 1024], BF16)
        Zs = P.sb("Zs", [128, 16, 1024], BF16)
        NQ = 256
        CTb = [P.sb(f"CTb{i}", [128, 16, NQ], BF16) for i in range(2)]
        STb = [P.sb(f"STb{i}", [128, 16, NQ], BF16) for i in range(2)]
        uT = P.sb("uT", [128, 8, 128], BF16)
        FT = P.sb("FT", [128, 8, NQ], BF16)
        psZ = [P.ps(f"psZ{i}", [128, 1024]) for i in range(2)]
        psF = [P.ps(f"psF{i}", [128, 512]) for i in range(1)]
        psY = P.ps("psY", [128, 1024])
        groups = [(0, T, C.hL, C.dft_ct, C.dft_st)]
        if ctx_out:
            groups.append((1, TC, C.hC, C.dft_ct_c, C.dft_st_c))
        bi = 0
        fi = 0
        for (who, Tt, hbase, ct, st_) in groups:
            ntl = Tt // 128
            mod_tiles(P, cb, l, 1, who, 1.0)
            scale = float((Tt * 128) ** -0.5)
            for ti in range(ntl):
                pro_tile(P, cb, hbase[ti * 128:(ti + 1) * 128, :], ("h", who, ti))
                P.copy("act", uT[:], cb.psT[:], ["cpsT"], ["uT"])
                for (tab, tk, pz, pzk, Zd, zk, ce) in ((cc, "cc", psZ[0], "psZ0", Zc, "Zc", "act"), (scn, "scn", psZ[1], "psZ1", Zs, "Zs", "dve")):
                    for g in range(8):
                        P.mm(pz[:, g * 128:(g + 1) * 128], uT[:, g, :], tab[:], True, True, ["uT", tk], [pzk])
                    P.copy(ce, Zd[:, ti, :], pz[:], [pzk], [(zk, ti)])
            zr = [("Zc", ti) for ti in range(ntl)] + [("Zs", ti) for ti in range(ntl)]
            for b0 in range(0, Tt, NQ):
                cbuf, sbuf_ = CTb[bi % 2], STb[bi % 2]
                ck, sk = ("CTb", bi % 2), ("STb", bi % 2)
                bi += 1
                P.dma(cbuf[:, 0:ntl, :], ct[:, b0:b0 + NQ].rearrange("(c p) n -> p c n", p=128), [], [ck])
                P.dma(sbuf_[:, 0:ntl, :], st_[:, b0:b0 + NQ].rearrange("(c p) n -> p c n", p=128), [], [sk], q="act")
                for g in range(8):
                    pf = psF[0]; pfk = ("psF", 0); fi += 1
                    for tc_ in range(ntl):
                        P.mm(pf[:, 0:NQ], Zc[:, tc_, g * 128:(g + 1) * 128], cbuf[:, tc_, :], tc_ == 0, False, zr + [ck], [pfk])
                        P.mm(pf[:, 0:NQ], Zs[:, tc_, g * 128:(g + 1) * 128], sbuf_[:, tc_, :], False, tc_ == ntl - 1, zr + [sk], [pfk])
                    P.act(FT[:, g, :], pf[:, 0:NQ], AF.Copy, [pfk], [("FT", g)], scale=scale)
                for jt in range(NQ // 128):
                    ti = b0 // 128 + jt
                    hap = hbase[ti * 128:(ti + 1) * 128, :]
                    x = cb.xb[cb.gi % 2]; xk = ("cxb", cb.gi % 2); cb.gi += 1
                    P.dma(x[:], hap, [("h", who, ti)], [xk])
                    for nh in range(2):
                        for g in range(8):
                            P.mm(psY[:, nh * 512:(nh + 1) * 512], FT[:, g, jt * 128:(jt + 1) * 128], Wo[:, g, nh * 512:(nh + 1) * 512],
                                 g == 0, g == 7, [("FT", g), ("Wo", g)], ["psY"])
                    epi_tile(P, cb, psY[:], ["psY"], x, xk, hap, ("h", who, ti), bias=(bo[:], "bo"))


NTOK = T + TC
DECAY_C = float(np.exp(-0.5))


def rwkv_feat_phase(P, C, l, j):
    UW = 2308
    with P.phase():
        cb = alloc_common(P)
        stg = [P.sb(f"stg{i}", [128, 1024], F32) for i in range(2)]
        uT = P.sb("uT", [128, 8, UW], BF16)
        xxT = P.sb("xxT", [128, 8, 256], BF16)
        tmpw = P.sb("tmpw", [128, 256], F32)
        P.add("pool", lambda e: e.memset(uT[:], 0.0), [], ["uTall"])
        Wr = P.sb("Wr", [128, 8, 1024], BF16)
        Wk = P.sb("Wk", [128, 8, 1024], BF16)
        Wv = P.sb("Wv", [128, 8, 1024], BF16)
        W1 = P.sb("W1", [128, 8, 128], BF16)
        A1 = P.sb("A1", [128, 8, 128], BF16)
        G1 = P.sb("G1", [128, 8, 160], BF16)
        W2 = P.sb("W2", [128, 1, 1024], BF16)
        A2 = P.sb("A2", [128, 1, 1024], BF16)
        G2 = P.sb("G2", [128, 2, 1024], BF16)
        cnt = [0]
        load_w_bf16(P, Wr, C.rwkv_w_r[j], stg, 8, 1024, "Wr", cnt)
        load_w_bf16(P, Wk, C.rwkv_w_k[j], stg, 8, 1024, "Wk", cnt)
        load_w_bf16(P, Wv, C.rwkv_w_v[j], stg, 8, 1024, "Wv", cnt)
        load_w_bf16(P, W1, C.rwkv_w1c[j], stg, 8, 128, "W1", cnt)
        load_w_bf16(P, A1, C.rwkv_a1c[j], stg, 8, 128, "A1", cnt)
        load_w_bf16(P, G1, C.rwkv_g1[j], stg, 8, 160, "G1", cnt)
        load_w_bf16(P, W2, C.rwkv_w2[j].rearrange("e r d -> (e r) d"), stg, 1, 1024, "W2", cnt)
        load_w_bf16(P, A2, C.rwkv_a2[j].rearrange("e r d -> (e r) d"), stg, 1, 1024, "A2", cnt)
        load_w_bf16(P, G2, C.rwkv_g2[j, 0:128, :], stg, 1, 1024, "G2", cnt)
        n = cnt[0]; s = stg[n % 2]
        P.dma(s[0:32, :], C.rwkv_g2[j, 128:160, :], [], [("cstg", n % 2)])
        P.copy("dve", G2[0:32, 1, :], s[0:32, :], [("cstg", n % 2)], [("G2", 1)])
        cnt[0] += 1
        mixc = P.sb("mixc", [128, 6, 8], F32)
        for m_ in range(6):
            P.dma(mixc[:, m_, :], C.rwkv_mix[j, m_].rearrange("(k p) -> p k", p=128), [], ["mixc"], allow_slow_non_contiguous=True)
        w0t = [P.sb(f"w0t{e}", [128, D], F32) for e in range(2)]
        a0t = [P.sb(f"a0t{e}", [128, D], F32) for e in range(2)]
        kkt = cb.t1
        for e in range(2):
            P.dma(w0t[e][:], C.rwkv_w0[j, e:e + 1, :].broadcast_to([128, D]), [], [("w0t", e)])
            P.dma(a0t[e][:], C.rwkv_a0[j, e:e + 1, :].broadcast_to([128, D]), [], [("a0t", e)], q="act")
        def ucol(who, ti):
            return (1 if who == 0 else 2051) + ti * 128
        for (who, ntl, hb) in ((0, 16, C.hL), (1, 2, C.hC)):
            mod_tiles(P, cb, l, 1, who, 1.0)
            for ti in range(ntl):
                pro_tile(P, cb, hb[ti * 128:(ti + 1) * 128, :], ("h", who, ti))
                c0 = ucol(who, ti)
                P.copy("act", uT[:, :, c0:c0 + 128], cb.psT[:], ["cpsT", "uTall"], [("uTt", who, ti)])
        allu = [("uTt", 0, ti) for ti in range(16)] + [("uTt", 1, ti) for ti in range(2)]
        P.dma(kkt[:], C.rwkv_k_k[j:j + 1, :].broadcast_to([128, D]), [], ["kkt", "ct1"])
        xm = [P.sb(f"xm{m}", [128, 8, 256], BF16) for m in range(6)]
        hW = P.sb("hW", [128, 256], BF16)
        hA = P.sb("hA", [128, 256], BF16)
        hG = P.sb("hG", [128, 2, 256], BF16)
        ot = [P.sb(f"ot{i}", [128, D], F32) for i in range(2)]
        sq = cb.tmp
        s16 = P.sb("s16", [128, 32], F32)
        psH = [P.ps(f"psH{i}", [128, 512]) for i in range(2)]
        psO = [P.ps(f"psO{i}", [128, 1024]) for i in range(2)]
        oc = [0]
        pc = [0]

        def out_tile(name_key):
            i = oc[0] % 2; oc[0] += 1
            return ot[i], ("ot", i)

        def ps_tile():
            i = pc[0] % 2; pc[0] += 1
            return psO[i], ("psO", i)

        blocks = [(0, t0, 2) for t0 in range(0, 16, 2)] + [(1, 0, 2)]
        hc_ = 0
        for (who, t0, nt) in blocks:
            nq = nt * 128
            c0 = ucol(who, t0)
            g0 = (t0 + (16 if who == 1 else 0)) * 128
            for k in range(8):
                P.tt("dve", tmpw[:, 0:nq], uT[:, k, c0 - 1:c0 - 1 + nq], uT[:, k, c0 + 1:c0 + 1 + nq], ALU.add, allu, ["tmpw"])
                P.stt("dve", xxT[:, k, 0:nq], tmpw[:, 0:nq], 0.5, uT[:, k, c0:c0 + nq], ALU.mult, ALU.subtract, ["tmpw"] + allu, [("xxT", k)])
            allx = [("xxT", k) for k in range(8)]
            for m in range(6):
                for k in range(8):
                    P.stt("dve" if (m + k) % 2 == 0 else "dve", xm[m][:, k, 0:nq], xxT[:, k, 0:nq], mixc[:, m, k:k + 1], uT[:, k, c0:c0 + nq],
                          ALU.mult, ALU.add, allx + allu + ["mixc"], [("xm", m, k)])
            xr_ = lambda m: [("xm", m, k) for k in range(8)]
            ph = psH[hc_ % 2]; phk = ("psH", hc_ % 2); hc_ += 1
            for k in range(8):
                P.mm(ph[:, 0:nq], W1[:, k, :], xm[1][:, k, 0:nq], k == 0, k == 7, xr_(1) + [("W1", k)], [phk])
            P.act(hW[:, 0:nq], ph[:, 0:nq], AF.Tanh, [phk], ["hW"])
            ph = psH[hc_ % 2]; phk = ("psH", hc_ % 2); hc_ += 1
            for k in range(8):
                P.mm(ph[:, 0:nq], A1[:, k, :], xm[4][:, k, 0:nq], k == 0, k == 7, xr_(4) + [("A1", k)], [phk])
            P.copy("dve", hA[:, 0:nq], ph[:, 0:nq], [phk], ["hA"])
            for (gi_, lo, hi) in ((0, 0, 128), (1, 128, 160)):
                ph = psH[hc_ % 2]; phk = ("psH", hc_ % 2); hc_ += 1
                for k in range(8):
                    P.mm(ph[0:hi - lo, 0:nq], G1[:, k, lo:hi], xm[5][:, k, 0:nq], k == 0, k == 7, xr_(5) + [("G1", k)], [phk])
                P.act(hG[0:hi - lo, gi_, 0:nq], ph[0:hi - lo, 0:nq], AF.Sigmoid, [phk], [("hG", gi_)])
            for jt in range(nt):
                r0 = g0 + jt * 128
                cs = slice(jt * 128, (jt + 1) * 128)
                for (m, Wm, wn, dstA) in ((0, Wr, "Wr", C.sR), (3, Wv, "Wv", C.sVv)):
                    pt, ptk = ps_tile()
                    for nh in range(2):
                        for k in range(8):
                            P.mm(pt[:, nh * 512:(nh + 1) * 512], xm[m][:, k, cs], Wm[:, k, nh * 512:(nh + 1) * 512], k == 0, k == 7,
                                 xr_(m) + [(wn, k)], [ptk])
                    o, ok = out_tile(0)
                    P.copy("act", o[:], pt[:], [ptk], [ok])
                    P.dma(dstA[r0:r0 + 128, :], o[:], [ok], [(wn, "out", r0)])
                pt, ptk = ps_tile()
                for nh in range(2):
                    for k in range(8):
                        P.mm(pt[:, nh * 512:(nh + 1) * 512], xm[2][:, k, cs], Wk[:, k, nh * 512:(nh + 1) * 512], k == 0, k == 7,
                             xr_(2) + [("Wk", k)], [ptk])
                o, ok = out_tile(0)
                P.copy("act", o[:], pt[:], [ptk], [ok])
                P.dma(C.sK[r0:r0 + 128, :], o[:], [ok], [("k", "out", r0)])
                o2, ok2 = out_tile(0)
                P.tt("dve", o2[:], o[:], kkt[:], ALU.mult, [ok, "kkt"], [ok2])
                P.tt("pool", sq[:], o2[:], o2[:], ALU.mult, [ok2], ["ctmp"])
                P.add("dve", lambda e, o_=s16[:, 0:16], i_=sq[:].rearrange("p (h n) -> p h n", n=64): e.tensor_reduce(out=o_, in_=i_, axis=AX.X, op=ALU.add), ["ctmp"], ["s16"])
                P.act(s16[:, 0:16], s16[:, 0:16], AF.Sqrt, ["s16"], ["s16"])
                P.ts("dve", s16[:, 0:16], s16[:, 0:16], 1e-12, None, ALU.max, None, ["s16"], ["s16"])
                P.recip(s16[:, 16:32], s16[:, 0:16], ["s16"], ["s16r"])
                for h in range(16):
                    P.ts("dve", o2[:, h * 64:(h + 1) * 64], o2[:, h * 64:(h + 1) * 64], s16[:, 16 + h:17 + h], None, ALU.mult, None,
                         [ok2, "s16r"], [ok2])
                P.dma(C.sKK[r0:r0 + 128, :], o2[:], [ok2], [("kk", "out", r0)])
                pt, ptk = ps_tile()
                for nh in range(2):
                    P.mm(pt[:, nh * 512:(nh + 1) * 512], hG[:, 0, cs], G2[:, 0, nh * 512:(nh + 1) * 512], True, False, [("hG", 0), ("hG", 1), ("G2", 0)], [ptk])
                    P.mm(pt[:, nh * 512:(nh + 1) * 512], hG[0:32, 1, cs], G2[0:32, 1, nh * 512:(nh + 1) * 512], False, True, [("hG", 1), ("G2", 1)], [ptk])
                o, ok = out_tile(0)
                P.copy("act", o[:], pt[:], [ptk], [ok])
                P.dma(C.sG[r0:r0 + 128, :], o[:], [ok], [("g", "out", r0)])
                for e in range(2):
                    pt, ptk = ps_tile()
                    for nh in range(2):
                        P.mm(pt[:, nh * 512:(nh + 1) * 512], hW[e * 64:(e + 1) * 64, cs], W2[e * 64:(e + 1) * 64, 0, nh * 512:(nh + 1) * 512], True, True,
                             ["hW", ("W2", 0)], [ptk])
                    o, ok = out_tile(0)
                    P.tt("dve", o[:], pt[:], w0t[e][:], ALU.add, [ptk, ("w0t", e)], [ok])
                    P.act(o[:], o[:], AF.Sigmoid, [ok], [ok])
                    P.ts("dve", o[:], o[:], -DECAY_C, None, ALU.mult, None, [ok], [ok])
                    P.dma(C.sLW[e, r0:r0 + 128, :], o[:], [ok], [("lw", e, r0)])
                    pt, ptk = ps_tile()
                    for nh in range(2):
                        P.mm(pt[:, nh * 512:(nh + 1) * 512], hA[e * 64:(e + 1) * 64, cs], A2[e * 64:(e + 1) * 64, 0, nh * 512:(nh + 1) * 512], True, True,
                             ["hA", ("A2", 0)], [ptk])
                    o, ok = out_tile(0)
                    P.tt("dve", o[:], pt[:], a0t[e][:], ALU.add, [ptk, ("a0t", e)], [ok])
                    P.act(o[:], o[:], AF.Sigmoid, [ok], [ok])
                    P.dma(C.sA[e, r0:r0 + 128, :], o[:], [ok], [("a", e, r0)])


class _Cut(Exception):
    pass


def rwkv_scan_phase(P, C, l, j, st_lim=NT, g_lim=4, stage=99):
    with P.phase():
        try:
            _rwkv_scan_body(P, C, l, j, st_lim, g_lim, stage)
        except _Cut:
            pass


def _rwkv_scan_body(P, C, l, j, st_lim, g_lim, stage):
    def cut(n):
        if stage == n:
            raise _Cut()
    identf = P.sb("identf", [128, 128], F32)
    ones = P.sb("onesf", [128, 128], F32)
    tri = [P.sb(f"tri{e}", [128, 128], F32) for e in range(2)]
    mSd = [P.sb(f"mSd{e}", [128, 512], F32) for e in range(2)]
    mSo = [P.sb(f"mSo{e}", [128, 512], F32) for e in range(2)]
    mS = [P.sb(f"mS{e}", [128, 512], F32) for e in range(2)]
    mI = [P.sb(f"mI{e}", [128, 512], F32) for e in range(2)]
    mTd = [P.sb(f"mTd{e}", [128, 512], F32) for e in range(2)]
    I4 = P.sb("I4", [128, 512], F32)
    P.dma(I4[:], C.rw_I4[:, :], [], ["I4"])
    P.dma(identf[:], C.ident_f[:, :], [], ["identf"])
    P.add("pool", lambda e_: e_.memset(ones[:], 1.0), [], ["ones"])
    for e in range(2):
        P.dma(tri[e][:], C.rw_tri[e], [], [("tri", e)])
        P.dma(mS[e][:], C.rw_mS[e], [], [("mS", e)])
        P.dma(mI[e][:], C.rw_mI[e], [], [("mI", e)], q="act")
        P.dma(mSd[e][:], C.rw_mSd[e], [], [("mSd", e)])
        P.dma(mSo[e][:], C.rw_mSo[e], [], [("mSo", e)], q="act")
        P.dma(mTd[e][:], C.rw_mTd[e], [], [("mTd", e)])
    kat = P.sb("kat", [128, D], F32)
    P.dma(kat[:], C.rwkv_k_a[j:j + 1, :].broadcast_to([128, D]), [], ["kat"])
    ST = [P.sb(f"ST{e}", [128, 8, 64], F32) for e in range(2)]
    for e in range(2):
        P.add("pool", lambda e_, t_=ST[e]: e_.memset(t_[:], 0.0), [], [("ST", e, hp) for hp in range(8)])
    tr_, tk_, tv_, tkk, tlw, ta = [P.sb(n, [128, D], F32) for n in ("tr_", "tk_", "tv_", "tkk", "tlw", "ta")]
    tb, tkd, cumS, E, Etot = [P.sb(n, [128, D], F32) for n in ("tb", "tkd", "cumS", "E", "Etot")]
    Rt, At, Bt_, Kt, Be, Ke = [P.sb(n, [128, D], F32) for n in ("Rt", "At", "Bt_", "Kt", "Be", "Ke")]
    FT = P.sb("FT", [128, 8, 4, 128], F32)
    Dg = P.sb("Dg", [128, 8, 128], F32)
    Gb = [P.sb(f"Gb{i}", [128, 4, 128], F32) for i in range(2)]
    Lb = [P.sb(f"Lb{i}", [128, 4, 128], F32) for i in range(2)]
    MrbT = P.sb("MrbT", [128, 4, 128], F32)
    LakT = P.sb("LakT", [128, 4, 128], F32)
    MrkT = P.sb("MrkT", [128, 4, 128], F32)
    Xs = P.sb("Xs", [128, 4, 64], F32)
    Zs = P.sb("Zs", [128, 4, 64], F32)
    Ws = P.sb("Ws", [128, 4, 64], F32)
    Us = P.sb("Us", [128, 4, 64], F32)
    Go = P.sb("Go", [128, 4, 128], F32)
    NTb = [P.sb(f"NTb{i}", [128, 4, 128], F32) for i in range(2)]
    Yt = P.sb("Yt", [128, D], F32)
    banks = [P.ps(f"bk{i}", [128, 512]) for i in range(8)]
    bc = [0]

    def bank():
        i = bc[0] % 8
        bc[0] += 1
        return banks[i], ("bank", i)

    order = {0: [16, 17] + list(range(16)), 1: [17, 16] + list(range(15, -1, -1))}
    cut(-1)
    for st in range(st_lim):
        for e in range(2):
            ti = order[e][st]
            r0 = ti * 128
            for (tile_, src, nm, q) in ((tr_, C.sR, "tr", "sp"), (tk_, C.sK, "tk", "act"), (tv_, C.sVv, "tv", "sp"), (tkk, C.sKK, "tkk", "act"),
                                        (tlw, C.sLW[e], "tlw", "sp"), (ta, C.sA[e], "ta", "act")):
                P.dma(tile_[:], src[r0:r0 + 128, :], [], [nm], q=q)
            P.tt("pool", tb[:], tkk[:], ta[:], ALU.mult, ["tkk", "ta"], ["tb"])
            P.tt("dve", tkd[:], ta[:], kat[:], ALU.mult, ["ta", "kat"], ["tkd"])
            P.tt("dve", tkd[:], tkd[:], kat[:], ALU.subtract, ["tkd", "kat"], ["tkd"])
            P.stt("dve", tkd[:], tkd[:], 1.0, tk_[:], ALU.add, ALU.mult, ["tkd", "tk"], ["tkd"])
            cut(1)
            for nh in range(2):
                cs = slice(nh * 512, (nh + 1) * 512)
                bk, bkk = bank()
                P.mm(bk[:, :], tri[e][:], tlw[:, cs], True, True, [("tri", e), "tlw"], [bkk])
                P.copy("act", cumS[:, cs], bk[:, :], [], [bkk, ("cumS", nh)])
                bk, bkk = bank()
                P.mm(bk[:, :], ones[:], tlw[:, cs], True, True, ["ones", "tlw"], [bkk])
                P.act(Etot[:, cs], bk[:, :], AF.Exp, [], [bkk, ("Etot", nh)])
                P.tt("dve", E[:, cs], bk[:, :], cumS[:, cs], ALU.subtract, [("cumS", nh)], [bkk, ("E4", nh)])
            cut(2)
            cS = [("cumS", 0), ("cumS", 1)]
            P.act(E[:], E[:], AF.Exp, [("E4", 0), ("E4", 1)], ["E"])
            P.tt("dve", Be[:], tb[:], E[:], ALU.mult, ["tb", "E"], ["Be"])
            P.tt("pool", Ke[:], tkd[:], E[:], ALU.mult, ["tkd", "E"], ["Ke"])
            P.act(E[:], cumS[:], AF.Exp, cS + ["Be", "Ke"], ["E"])
            P.tt("dve", Rt[:], tr_[:], E[:], ALU.mult, ["tr", "E"], ["Rt"])
            P.tt("dve", E[:], cumS[:], tlw[:], ALU.subtract, cS + ["tlw", "Rt"], ["E"])
            P.act(E[:], E[:], AF.Exp, ["E"], ["E"])
            P.stt("dve", At[:], tkk[:], -1.0, E[:], ALU.mult, ALU.mult, ["tkk", "E"], ["At"])
            P.act(E[:], cumS[:], AF.Exp, cS + ["At"], ["E"], scale=-1.0)
            P.tt("dve", Bt_[:], tb[:], E[:], ALU.mult, ["tb", "E"], ["Bt_"])
            P.tt("pool", Kt[:], tkd[:], E[:], ALU.mult, ["tkd", "E"], ["Kt"])
            for hp in range(8):
                P.tt("pool", Dg[:, hp, :], identf[:], Etot[:, hp * 128:(hp + 1) * 128], ALU.mult, ["identf", ("Etot", hp // 4)], [("Dg", hp)])
            cut(4)
            for hp in range(8):
                bk, bkk = bank()
                for si, (srcT, sk) in enumerate(((At, "At"), (Rt, "Rt"), (Bt_, "Bt_"), (Kt, "Kt"))):
                    P.mm(bk[:, si * 128:(si + 1) * 128], srcT[:, hp * 128:(hp + 1) * 128], identf[:], True, True, [sk, "identf"], [bkk])
                P.copy("act" if hp % 2 else "dve", FT[:, hp, :, :], bk[:, :].rearrange("p (s t) -> p s t", s=4), [], [bkk, ("FT", hp)])
            cut(5)
            for hg in range(g_lim):
                gq, hp0 = hg % 2, (hg // 2) * 4
                heads = [2 * (hp0 + i) + gq for i in range(4)]
                hps = [hp0 + i for i in range(4)]
                ftk = [("FT", hp) for hp in hps]
                stk = [("ST", e, hp) for hp in hps]

                def ft(h, si):
                    q = h % 2
                    return FT[q * 64:q * 64 + 64, h // 2, si, :]

                def grp(a_si, b_si, outs):
                    bk, bkk = bank()
                    for i, h in enumerate(heads):
                        P.mm(bk[:, i * 128:(i + 1) * 128], ft(h, a_si), ft(h, b_si), True, True, ftk, [bkk])
                    for (dst, dk, mask, mk) in outs:
                        P.tt("dve", dst[:].rearrange("p s t -> p (s t)"), bk[:, :], mask[:], ALU.mult, [mk], [bkk, dk])

                grp(2, 0, [(Gb[0], ("Gb", 0), mSd[e], ("mSd", e)), (Go, "Go", mSo[e], ("mSo", e))])
                grp(0, 2, [(Lb[0], ("Lb", 0), mTd[e], ("mTd", e))])
                grp(3, 0, [(LakT, "LakT", mS[e], ("mS", e))])
                grp(2, 1, [(MrbT, "MrbT", mI[e], ("mI", e))])
                grp(3, 1, [(MrkT, "MrkT", mI[e], ("mI", e))])
                cut(5.2)
                bk, bkk = bank()
                for i, h in enumerate(heads):
                    q, hp = h % 2, h // 2
                    P.mm(bk[:, i * 64:(i + 1) * 64], ft(h, 0), ST[e][q * 64:q * 64 + 64, hp, :], True, False, ftk + stk, [bkk])
                    P.mm(bk[:, i * 64:(i + 1) * 64], LakT[:, i, :], tv_[:, h * 64:(h + 1) * 64], False, True, ["LakT", "tv"], [bkk])
                P.copy("act", Xs[:].rearrange("p s t -> p (s t)"), bk[:, 0:256], [], [bkk, "Xs"])
                cut(6)
                fl = lambda t_: t_[:].rearrange("p s t -> p (s t)")
                P.tt("pool", fl(NTb[0]), I4[:], fl(Gb[0]), ALU.add, ["I4", ("Gb", 0)], [("NT", 0)])
                for k in range(1, 5):
                    gp, lp = Gb[(k - 1) % 2], Lb[(k - 1) % 2]
                    gpk, lpk = ("Gb", (k - 1) % 2), ("Lb", (k - 1) % 2)
                    gn, ln = Gb[k % 2], Lb[k % 2]
                    gnk, lnk = ("Gb", k % 2), ("Lb", k % 2)
                    bl, blk = bank()
                    for i in range(4):
                        P.mm(bl[:, i * 128:(i + 1) * 128], gp[:, i, :], lp[:, i, :], True, True, [gpk, lpk], [blk])
                    if k < 4:
                        bg, bgk = bank()
                        for i in range(4):
                            P.mm(bg[:, i * 128:(i + 1) * 128], lp[:, i, :], gp[:, i, :], True, True, [gpk, lpk], [bgk])
                    P.copy("act", fl(ln), bl[:, :], [], [blk, lnk])
                    if k < 4:
                        P.copy("dve", fl(gn), bg[:, :], [], [bgk, gnk])
                    bn, bnk = bank()
                    for i in range(4):
                        P.mm(bn[:, i * 128:(i + 1) * 128], ln[:, i, :], NTb[(k - 1) % 2][:, i, :], True, True, [lnk, ("NT", (k - 1) % 2)], [bnk])
                    P.tt("dve", fl(NTb[k % 2]), fl(NTb[(k - 1) % 2]), bn[:, :], ALU.add, [("NT", (k - 1) % 2)], [bnk, ("NT", k % 2)])
                NT, NTk = NTb[0], ("NT", 0)
                bk, bkk = bank()
                for i in range(4):
                    P.mm(bk[:, i * 64:(i + 1) * 64], NT[:, i, :], Xs[:, i, :], True, True, [NTk, "Xs"], [bkk])
                P.copy("act", fl(Zs), bk[:, 0:256], [], [bkk, "Zs"])
                ucur, uk = Zs, "Zs"
                for it in range(3):
                    bk, bkk = bank()
                    for i in range(4):
                        P.mm(bk[:, i * 64:(i + 1) * 64], Go[:, i, :], ucur[:, i, :], True, True, ["Go", uk], [bkk])
                    P.copy("act", fl(Ws), bk[:, 0:256], [], [bkk, "Ws"])
                    bk, bkk = bank()
                    for i in range(4):
                        P.mm(bk[:, i * 64:(i + 1) * 64], NT[:, i, :], Ws[:, i, :], True, True, [NTk, "Ws"], [bkk])
                    P.tt("dve", fl(Us), fl(Zs), bk[:, 0:256], ALU.add, ["Zs"], [bkk, "Us"])
                    ucur, uk = Us, "Us"
                cut(7)
                bk, bkk = bank()
                for i, h in enumerate(heads):
                    q, hp = h % 2, h // 2
                    P.mm(bk[:, i * 64:(i + 1) * 64], ft(h, 1), ST[e][q * 64:q * 64 + 64, hp, :], True, False, ftk + stk, [bkk])
                    P.mm(bk[:, i * 64:(i + 1) * 64], MrbT[:, i, :], Us[:, i, :], False, False, ["MrbT", "Us"], [bkk])
                    P.mm(bk[:, i * 64:(i + 1) * 64], MrkT[:, i, :], tv_[:, h * 64:(h + 1) * 64], False, True, ["MrkT", "tv"], [bkk])
                P.copy("act", Yt[:].rearrange("p (a q n) -> p a q n", q=2, n=64)[:, hp0:hp0 + 4, gq, :],
                       bk[:, 0:256].rearrange("p (s t) -> p s t", s=4), [], [bkk, ("Yt", hg)])
                cut(8)
                bk, bkk = bank()
                for i, h in enumerate(heads):
                    hp = h // 2
                    cs = slice(hp * 128, (hp + 1) * 128)
                    P.mm(bk[:, i * 64:(i + 1) * 64], Dg[:, hp, :], ST[e][:, hp, :], True, False, [("Dg", hp)] + stk, [bkk])
                    P.mm(bk[:, i * 64:(i + 1) * 64], Be[:, cs], Us[:, i, :], False, False, ["Be", "Us"], [bkk])
                    P.mm(bk[:, i * 64:(i + 1) * 64], Ke[:, cs], tv_[:, h * 64:(h + 1) * 64], False, True, ["Ke", "tv"], [bkk])
                b4 = bk[:, 0:256].rearrange("p (s t) -> p s t", s=4)
                P.copy("dve", ST[e][gq * 64:gq * 64 + 64, hp0:hp0 + 4, :], b4[gq * 64:gq * 64 + 64, :, :], [], [bkk] + stk)
            P.dma(C.sY[e, r0:r0 + 128, :], Yt[:], [("Yt", hg) for hg in range(4)], [("sY", e, ti)])


def rwkv_out_phase(P, C, l, j):
    with P.phase():
        cb = alloc_common(P)
        stg = [P.sb(f"stg{i}", [128, 1024], F32) for i in range(2)]
        Wo = P.sb("Wo", [128, 8, 1024], BF16)
        cnt = [0]
        load_w_bf16(P, Wo, C.rwkv_w_o[j], stg, 8, 1024, "Wo", cnt)
        kat, rkt, lnw, lnb = [P.sb(n, [128, D], F32) for n in ("kat", "rkt", "lnw", "lnb")]
        P.dma(kat[:], C.rwkv_k_a[j:j + 1, :].broadcast_to([128, D]), [], ["kat"])
        P.dma(rkt[:], C.rwkv_r_k[j].rearrange("h n -> (h n)").rearrange("(o d) -> o d", o=1).broadcast_to([128, D]), [], ["rkt"])
        P.dma(lnw[:], C.rwkv_ln_w[j:j + 1, :].broadcast_to([128, D]), [], ["lnw"])
        P.dma(lnb[:], C.rwkv_ln_b[j:j + 1, :].broadcast_to([128, D]), [], ["lnb"])
        y0, y1, tr_, tk_, tv_, a0, a1, tg = [P.sb(n, [128, D], F32) for n in ("y0", "y1", "tr_", "tk_", "tv_", "a0", "a1", "tg")]
        s16 = P.sb("s16", [128, 64], F32)
        ob = P.sb("ob", [128, D], BF16)
        oT = P.sb("oT", [128, 8, 128], BF16)
        psY = P.ps("psY", [128, D])
        for (who, ntl, hb) in ((0, 16, C.hL), (1, 2, C.hC)):
            mod_tiles(P, cb, l, 1, who, 1.0)
            for ti in range(ntl):
                r0 = (ti + (16 if who == 1 else 0)) * 128
                for (tile_, src, nm, q) in ((y0, C.sY[0], "y0", "sp"), (y1, C.sY[1], "y1", "act"), (tr_, C.sR, "tr", "sp"), (tk_, C.sK, "tk", "act"),
                                            (tv_, C.sVv, "tv", "sp"), (a0, C.sA[0], "a0", "act"), (a1, C.sA[1], "a1", "sp"), (tg, C.sG, "tg", "act")):
                    P.dma(tile_[:], src[r0:r0 + 128, :], [], [nm], q=q)
                v3 = lambda t_: t_[:].rearrange("p (h n) -> p h n", n=64)
                P.tt("dve", y0[:], y0[:], y1[:], ALU.add, ["y0", "y1"], ["y0"])
                P.add("dve", lambda e, o_=s16[:, 0:16], i_=v3(y0): e.tensor_reduce(out=o_, in_=i_, axis=AX.X, op=ALU.add), ["y0"], ["mean"])
                P.ts("dve", s16[:, 0:16], s16[:, 0:16], 1.0 / 64, None, ALU.mult, None, ["mean"], ["mean"])
                for h in range(16):
                    P.ts("dve", y0[:, h * 64:(h + 1) * 64], y0[:, h * 64:(h + 1) * 64], s16[:, h:h + 1], None, ALU.subtract, None,
                         ["y0", "mean"], ["y0"])
                P.tt("pool", y1[:], y0[:], y0[:], ALU.mult, ["y0", "y1"], ["y1"])
                P.add("dve", lambda e, o_=s16[:, 16:32], i_=v3(y1): e.tensor_reduce(out=o_, in_=i_, axis=AX.X, op=ALU.add), ["y1"], ["var"])
                P.ts("dve", s16[:, 16:32], s16[:, 16:32], 1.0 / 64, 64e-5, ALU.mult, ALU.add, ["var"], ["var"])
                P.act(s16[:, 16:32], s16[:, 16:32], AF.Sqrt, ["var"], ["var"])
                P.recip(s16[:, 16:32], s16[:, 16:32], ["var"], ["var"])
                for h in range(16):
                    P.ts("dve", y0[:, h * 64:(h + 1) * 64], y0[:, h * 64:(h + 1) * 64], s16[:, 16 + h:17 + h], None, ALU.mult, None,
                         ["y0", "var"], ["y0"])
                P.tt("dve", y0[:], y0[:], lnw[:], ALU.mult, ["y0", "lnw"], ["y0"])
                P.tt("pool", y0[:], y0[:], lnb[:], ALU.add, ["y0", "lnb"], ["y0"])
                P.tt("dve", a0[:], a0[:], a1[:], ALU.add, ["a0", "a1"], ["a0"])
                P.stt("dve", a0[:], a0[:], -2.0, kat[:], ALU.add, ALU.mult, ["a0", "kat"], ["a0"])
                P.stt("dve", a0[:], a0[:], 2.0, tk_[:], ALU.add, ALU.mult, ["a0", "tk"], ["a0"])
                P.tt("pool", a0[:], a0[:], tr_[:], ALU.mult, ["a0", "tr"], ["a0"])
                P.tt("dve", a0[:], a0[:], rkt[:], ALU.mult, ["a0", "rkt"], ["a0"])
                P.add("dve", lambda e, o_=s16[:, 32:48], i_=v3(a0): e.tensor_reduce(out=o_, in_=i_, axis=AX.X, op=ALU.add), ["a0"], ["coef"])
                for h in range(16):
                    P.stt("dve", y0[:, h * 64:(h + 1) * 64], tv_[:, h * 64:(h + 1) * 64], s16[:, 32 + h:33 + h], y0[:, h * 64:(h + 1) * 64],
                          ALU.mult, ALU.add, ["tv", "coef", "y0"], ["y0"])
                P.tt("dve", ob[:], y0[:], tg[:], ALU.mult, ["y0", "tg"], ["ob"])
                for k in range(8):
                    P.tr(cb.psT[:, k, :], ob[:, k * 128:(k + 1) * 128], cb.ident[:], ["ob", "ident"], ["cpsT"])
                P.copy("act", oT[:], cb.psT[:], ["cpsT"], ["oT"])
                hap = hb[ti * 128:(ti + 1) * 128, :]
                x = cb.xb[cb.gi % 2]; xk = ("cxb", cb.gi % 2); cb.gi += 1
                P.dma(x[:], hap, [("h", who, ti)], [xk])
                for nh in range(2):
                    for k in range(8):
                        P.mm(psY[:, nh * 512:(nh + 1) * 512], oT[:, k, :], Wo[:, k, nh * 512:(nh + 1) * 512], k == 0, k == 7, ["oT", ("Wo", k)], ["psY"])
                epi_tile(P, cb, psY[:], ["psY"], x, xk, hap, ("h", who, ti))


def rwkv_phases(P, C, l, j, ctx_out):
    import os
    stop = int(os.environ.get("RW_STOP", "3"))
    rwkv_feat_phase(P, C, l, j)
    if stop >= 2:
        rwkv_scan_phase(P, C, l, j)
    if stop >= 3:
        rwkv_out_phase(P, C, l, j)
```

```python
import numpy as np
import ml_dtypes
from contextlib import ExitStack, contextmanager
import concourse.bass as bass
import concourse.mybir as mybir
from concourse.bass_utils import run_bass_kernel_spmd

F32 = mybir.dt.float32
BF16 = mybir.dt.bfloat16
AF = mybir.ActivationFunctionType
ALU = mybir.AluOpType
AX = mybir.AxisListType

D = 1024
T = 2048
TC = 256
NT = (T + TC) // 128
FF = 2816
NFC = FF // 128
DEPTH = 4
EPS = 1e-6


class Op:
    __slots__ = ("eng", "fn", "waits", "signal", "pos", "dma", "dslot", "dval", "know", "idx", "sigval")


class Prog:
    R = 14
    CE = ("pe", "dve", "act", "pool")
    QE = ("sp", "act", "pool")
    ALLE = ("pe", "dve", "act", "pool", "sp")

    def __init__(self, nc, stack):
        self.nc = nc
        self.sem = {e: stack.enter_context(nc.semaphore("s_" + e)) for e in self.CE}
        self.dsem = {q: [stack.enter_context(nc.semaphore(f"d_{q}{i}")) for i in range(self.R)] for q in self.QE}
        self.sigcount = {e: 0 for e in self.CE}
        self.dq = {q: {"n": 0, "hist": [None] * self.R, "val": [0] * self.R} for q in self.QE}
        self.ops = []
        self.tok = {}
        self.npos = {e: 0 for e in self.CE}
        self.know = {e: {} for e in self.ALLE}
        self.dknown = {e: set() for e in self.ALLE}
        self.phase_start = 0
        self.pstack = None
        self.nphase = 0
        self.same_engine_sync = True

    def sb(self, name, shape, dtype):
        return self.pstack.enter_context(self.nc.sbuf_tensor(f"{name}_{self.nphase}", list(shape), dtype))

    def ps(self, name, shape, dtype=F32):
        return self.pstack.enter_context(self.nc.psum_tensor(f"{name}_{self.nphase}", list(shape), dtype))

    @contextmanager
    def phase(self):
        with ExitStack() as st:
            self.pstack = st
            yield self
            self.flush()
            self.pstack = None
        self.nphase += 1

    def add(self, eng, fn, reads=(), writes=(), dma=False):
        op = Op()
        op.eng, op.fn, op.dma, op.signal, op.idx, op.sigval = eng, fn, dma, False, len(self.ops), None
        deps = set()
        for t in reads:
            s = self.tok.get(t)
            if s is not None and s[0] is not None:
                deps.add(s[0])
        for t in writes:
            s = self.tok.get(t)
            if s is not None:
                if s[0] is not None:
                    deps.add(s[0])
                deps.update(s[1])
        if dma:
            q = self.dq[eng]
            n = q["n"]
            q["n"] += 1
            slot = n % self.R
            op.dslot, op.dval = slot, 16 * (n // self.R + 1)
            q["val"][slot] = op.dval
            if q["hist"][slot] is not None:
                deps.add(q["hist"][slot])
            q["hist"][slot] = op.idx
        know, dk = self.know[eng], self.dknown[eng]
        waits = []
        for d in sorted(deps):
            if d < self.phase_start:
                continue
            Dp = self.ops[d]
            if Dp.dma:
                if d in dk:
                    continue
                dk.add(d)
                waits.append(d)
            else:
                if Dp.eng == eng and (eng == "pe" or not self.same_engine_sync):
                    continue
                if know.get(Dp.eng, -1) >= Dp.pos:
                    continue
                Dp.signal = True
                waits.append(d)
            for e2, p2 in Dp.know.items():
                if know.get(e2, -1) < p2:
                    know[e2] = p2
        op.waits = waits
        if dma:
            op.pos = None
            op.know = dict(know)
        else:
            op.pos = self.npos[eng]
            self.npos[eng] += 1
            op.know = dict(know)
            op.know[eng] = op.pos
        for t in reads:
            self.tok.setdefault(t, [None, []])[1].append(op.idx)
        for t in writes:
            self.tok[t] = [op.idx, []]
        self.ops.append(op)
        return op

    def flush(self):
        ops = self.ops[self.phase_start:]
        last = {}
        for op in ops:
            if not op.dma:
                last[op.eng] = op
        for op in last.values():
            op.signal = True
        for op in ops:
            if not op.dma and op.signal:
                self.sigcount[op.eng] += 1
                op.sigval = self.sigcount[op.eng]
        final_sig = dict(self.sigcount)
        final_d = {q: list(self.dq[q]["val"]) for q in self.QE}
        per = {e: [op for op in ops if op.eng == e] for e in self.ALLE}
        allops = self.ops
        sem, dsem = self.sem, self.dsem

        def mk(name):
            def body(e):
                for op in per[name]:
                    for d in op.waits:
                        Dp = allops[d]
                        if Dp.dma:
                            e.wait_ge(dsem[Dp.eng][Dp.dslot], Dp.dval)
                        else:
                            e.wait_ge(sem[Dp.eng], Dp.sigval)
                    ins = op.fn(e)
                    if op.dma:
                        ins.then_inc(dsem[op.eng][op.dslot], 16)
                    elif op.signal:
                        ins.then_inc(sem[op.eng], 1)
                for e2 in self.CE:
                    if final_sig[e2] > 0:
                        e.wait_ge(sem[e2], final_sig[e2])
                for q in self.QE:
                    for s in range(self.R):
                        if final_d[q][s] > 0:
                            e.wait_ge(dsem[q][s], final_d[q][s])
            return body

        with self.nc.Block() as blk:
            blk.tensor(mk("pe"))
            blk.vector(mk("dve"))
            blk.scalar(mk("act"))
            blk.gpsimd(mk("pool"))
            blk.sync(mk("sp"))
        self.phase_start = len(self.ops)
        for e in self.ALLE:
            self.know[e] = {c: self.npos[c] - 1 for c in self.CE}
            self.dknown[e] = set()

    def dma(self, out, in_, reads, writes, q="sp", **kw):
        return self.add(q, lambda e: e.dma_start(out=out, in_=in_, **kw), reads, writes, dma=True)

    def mm(self, out, lhsT, rhs, start, stop, reads, writes, r32=False):
        if r32 and lhsT.dtype == F32:
            lhsT = lhsT.bitcast(mybir.dt.float32r)
            rhs = rhs.bitcast(mybir.dt.float32r)
        return self.add("pe", lambda e: e.matmul(out, lhsT, rhs, start=start, stop=stop), reads, writes)

    def tr(self, out, in_, ident, reads, writes):
        if in_.dtype == F32:
            return self.add("pe", lambda e: e.matmul(out, in_, ident, start=True, stop=True), reads, writes)
        return self.add("pe", lambda e: e.transpose(out, in_, ident), reads, writes)

    def act(self, out, in_, func, reads, writes, **kw):
        return self.add("act", lambda e: e.activation(out=out, in_=in_, func=func, **kw), reads, writes)

    def tt(self, eng, out, in0, in1, op, reads, writes):
        return self.add(eng, lambda e: e.tensor_tensor(out=out, in0=in0, in1=in1, op=op), reads, writes)

    def ts(self, eng, out, in0, s1, s2, op0, op1, reads, writes):
        if s2 is None:
            return self.add(eng, lambda e: e.tensor_scalar(out=out, in0=in0, scalar1=s1, scalar2=None, op0=op0), reads, writes)
        return self.add(eng, lambda e: e.tensor_scalar(out=out, in0=in0, scalar1=s1, scalar2=s2, op0=op0, op1=op1), reads, writes)

    def stt(self, eng, out, in0, scalar, in1, op0, op1, reads, writes):
        return self.add(eng, lambda e: e.scalar_tensor_tensor(out=out, in0=in0, scalar=scalar, in1=in1, op0=op0, op1=op1), reads, writes)

    def recip(self, out, in_, reads, writes):
        return self.add("dve", lambda e: e.reciprocal(out=out, in_=in_), reads, writes)

    def copy(self, eng, out, in_, reads, writes):
        if eng == "act":
            return self.add("act", lambda e: e.activation(out=out, in_=in_, func=AF.Copy), reads, writes)
        return self.add(eng, lambda e: e.tensor_copy(out=out, in_=in_), reads, writes)


class Ctx:
    pass


def mod_phase(P, C):
    NB = 1152
    with P.phase():
        cc = P.sb("cc", [128, 8, 2], F32)
        sc = P.sb("sc", [128, 8, 2], F32)
        wb = [P.sb(f"mw{i}", [128, 8, NB], F32) for i in range(2)]
        bb = [P.sb(f"mb{i}", [2, NB], F32) for i in range(2)]
        rs = [P.sb(f"mr{i}", [2, NB], F32) for i in range(2)]
        pp = [P.ps(f"mp{i}", [2, 512]) for i in range(4)]
        P.dma(cc[:, :, 0], C.c.rearrange("o (k p) -> p (o k)", p=128), [], ["cc0"], allow_slow_non_contiguous=True)
        P.dma(cc[:, :, 1], C.c_ctx.rearrange("(k p) -> p k", p=128), [], ["cc1"], allow_slow_non_contiguous=True)
        P.act(sc[:], cc[:], AF.Silu, ["cc0", "cc1"], ["sc"])
        it = 0
        pi = 0
        for l in range(DEPTH):
            for cb in range(9216 // NB):
                w = wb[it % 2]
                wt = ("mw", it % 2)
                c0 = cb * NB
                P.dma(w[:], C.mod_w[l, :, c0:c0 + NB].rearrange("(k p) n -> p k n", p=128), [], [wt],
                      q=("sp" if it % 2 == 0 else "pool"))
                P.dma(bb[it % 2][:], C.mod_b[l:l + 1, c0:c0 + NB].broadcast_to([2, NB]), [], [("mb", it % 2)])
                for (o, n) in ((0, 512), (512, 512), (1024, 128)):
                    pt = pp[pi % 4]
                    ptk = ("mp", pi % 4)
                    for k in range(8):
                        P.mm(pt[:, 0:n], sc[:, k, :], w[:, k, o:o + n], k == 0, k == 7, ["sc", wt], [ptk])
                    P.tt("dve", rs[it % 2][:, o:o + n], pt[:, 0:n], bb[it % 2][:, o:o + n], ALU.add,
                         [ptk, ("mb", it % 2)], [("mr", it % 2, o)])
                    pi += 1
                P.dma(C.modv[l, :, c0:c0 + NB], rs[it % 2][:], [("mr", it % 2, 0), ("mr", it % 2, 512), ("mr", it % 2, 1024)],
                      [("modv", l, cb)])
                it += 1


def load_mod_tiles(P, C, l, sub, who, weight, A, Bt, G, tmp1, tmp2, key, k1, k2):
    base = sub * 3 * D
    def row(ap):
        return ap.broadcast_to([128, D])
    P.dma(Bt[:], row(C.modv[l, who:who + 1, base:base + D]), [], [("B", key)])
    P.dma(A[:], row(C.modv[l, who:who + 1, base + D:base + 2 * D]), [], [("A", key)])
    P.dma(G[:], row(C.modv[l, who:who + 1, base + 2 * D:base + 3 * D]), [], [("G", key)])
    P.dma(tmp1[:, 0:D], row(C.norm_pre[l, sub:sub + 1, :]), [], [k1])
    P.dma(tmp2[:, 0:D], row(C.norm_post[l, sub:sub + 1, :]), [], [k2])
    P.stt("dve", A[:], A[:], 1.0, tmp1[:, 0:D], ALU.add, ALU.mult, [("A", key), k1], [("A", key)])
    P.stt("dve", G[:], G[:], float(weight), tmp2[:, 0:D], ALU.mult, ALU.mult, [("G", key), k2], [("G", key)])


def rms_rstd(P, src, junk, ss, rstd, n, reads, key, eps=EPS):
    P.act(junk, src, AF.Square, reads + [], [("junk", key), ("ss", key)], accum_out=ss)
    P.ts("dve", rstd, ss, 1.0 / n, eps, ALU.mult, ALU.add, [("ss", key)], [("rstd", key)])
    P.act(rstd, rstd, AF.Sqrt, [("rstd", key)], [("rstd", key)])
    P.recip(rstd, rstd, [("rstd", key)], [("rstd", key)])


def ffn_phase(P, C, l, f, src, dst, ntiles_lat, do_ctx):
    sub = 0 if f == 0 else 2
    TB = 256
    with P.phase():
        Wg = P.sb("Wg", [128, 8, FF], BF16)
        Wu = P.sb("Wu", [128, 8, FF], BF16)
        Wd = P.sb("Wd", [128, NFC, D], BF16)
        stg = [P.sb(f"stg{i}", [128, 1024], F32) for i in range(2)]
        A = P.sb("A", [128, D], F32)
        Bt = P.sb("B", [128, D], F32)
        G = P.sb("G", [128, D], F32)
        xT = [P.sb(f"xT{i}", [128, 8, TB], BF16) for i in range(2)]
        hT = P.sb("hT", [128, NFC, TB], BF16)
        xb = [P.sb(f"xb{i}", [128, D], F32) for i in range(4)]
        xm = [P.sb(f"xm{i}", [128, D], BF16) for i in range(2)]
        tmp = P.sb("tmp", [128, D], F32)
        junk = P.sb("junk", [128, D], BF16)
        sg = [P.sb(f"sg{i}", [128, TB], F32) for i in range(2)]
        small = P.sb("small", [128, 16], F32)
        ident = P.sb("ident", [128, 128], BF16)
        psT = P.ps("psT", [128, 8, 128], BF16)
        psGU = [P.ps(f"psGU{i}", [128, 512]) for i in range(2)]
        psYs = [P.ps(f"psY{i}", [128, D]) for i in range(2)]
        P.dma(ident[:], C.ident_bf[:, :], [], ["ident"])
        cv = ["pool", "act", "dve"]
        qs = ["sp", "act", "sp"]
        n = 0
        NS = 2
        for (Wsrc, Wdst, nm) in ((C.ffn_w_gate[l, f], Wg, "Wg"), (C.ffn_w_up[l, f], Wu, "Wu")):
            for k in range(8):
                for (c0, cw) in ((0, 1024), (1024, 1024), (2048, 768)):
                    s_ = stg[n % NS]
                    P.dma(s_[:, 0:cw], Wsrc[k * 128:(k + 1) * 128, c0:c0 + cw], [], [("stg", n % NS)], q=qs[n % 3])
                    P.copy(cv[n % 3], Wdst[:, k, c0:c0 + cw], s_[:, 0:cw], [("stg", n % NS)], [(nm, k)])
                    n += 1
        for fc in range(NFC):
            s_ = stg[n % NS]
            P.dma(s_[:, 0:D], C.ffn_w_down[l, f, fc * 128:(fc + 1) * 128, :], [], [("stg", n % NS)], q=qs[n % 3])
            P.copy(cv[n % 3], Wd[:, fc, :], s_[:, 0:D], [("stg", n % NS)], [("Wd", fc)])
            n += 1
        Wg_r = [("Wg", k) for k in range(8)]
        Wu_r = [("Wu", k) for k in range(8)]
        blocks = [(0, ti) for ti in range(0, ntiles_lat, 2)]
        if do_ctx:
            blocks.append((1, 0))
        st_ = {"gi": 0, "who": None}
        tiles_of = {}

        def prologue(bi):
            who, t0 = blocks[bi]
            if who != st_["who"]:
                load_mod_tiles(P, C, l, sub, who, 0.5, A, Bt, G, stg[0], stg[1], "m", ("stg", 0), ("stg", 1))
                st_["who"] = who
            xTb = xT[bi % 2]
            xtk = ("xT", bi % 2)
            tiles = []
            for j in range(2):
                ti = t0 + j
                x = xb[st_["gi"] % 4]
                xk = ("xb", st_["gi"] % 4)
                st_["gi"] += 1
                tiles.append((ti, x, xk))
                P.dma(x[:], src(who, ti), [("h", who, ti)], [xk])
                ss, rstd = small[:, 0:1], small[:, 1:2]
                rms_rstd(P, x[:], junk[:], ss, rstd, D, [xk], "pre")
                P.stt("dve", tmp[:], x[:], rstd, A[:], ALU.mult, ALU.mult, [xk, ("rstd", "pre"), ("A", "m")], ["tmp"])
                m = xm[j]
                P.tt("dve", m[:], tmp[:], Bt[:], ALU.add, ["tmp", ("B", "m")], [("xm", j)])
                for k in range(8):
                    P.tr(psT[:, k, :], m[:, k * 128:(k + 1) * 128], ident[:], [("xm", j), "ident"], ["psT"])
                P.copy("act", xTb[:, :, j * 128:(j + 1) * 128], psT[:], ["psT"], [(xtk, j)])
            tiles_of[bi] = tiles

        prologue(0)
        for bi, (who, t0) in enumerate(blocks):
            xTb = xT[bi % 2]
            xtk = ("xT", bi % 2)
            tiles = tiles_of[bi]
            xr = [(xtk, 0), (xtk, 1)]
            nxt_same = bi + 1 < len(blocks) and blocks[bi + 1][0] == who
            for fc in range(NFC):
                if fc == 5 and nxt_same:
                    prologue(bi + 1)
                pgu, pk = psGU[fc % 2], ("psGU", fc % 2)
                for k in range(8):
                    P.mm(pgu[:, 0:TB], Wg[:, k, fc * 128:(fc + 1) * 128], xTb[:, k, :], k == 0, k == 7,
                         [("Wg", k)] + xr, [pk])
                for k in range(8):
                    P.mm(pgu[:, TB:2 * TB], Wu[:, k, fc * 128:(fc + 1) * 128], xTb[:, k, :], k == 0, k == 7,
                         [("Wu", k)] + xr, [pk])
                P.act(sg[fc % 2][:], pgu[:, 0:TB], AF.Silu, [], [pk, ("sg", fc % 2)])
                P.tt("dve", hT[:, fc, :], sg[fc % 2][:], pgu[:, TB:2 * TB], ALU.mult, [("sg", fc % 2)], [pk, ("hT", fc)])
            for j, (ti, x, xk) in enumerate(tiles):
                psY, pyk = psYs[j], ("psY", j)
                for nh in range(2):
                    for fc in range(NFC):
                        P.mm(psY[:, nh * 512:(nh + 1) * 512], hT[:, fc, j * 128:(j + 1) * 128],
                             Wd[:, fc, nh * 512:(nh + 1) * 512], fc == 0, fc == NFC - 1,
                             [("hT", fc), ("Wd", fc)], [pyk])
                ss, rstd = small[:, 2 + 2 * j:3 + 2 * j], small[:, 3 + 2 * j:4 + 2 * j]
                rms_rstd(P, psY[:], junk[:], ss, rstd, D, [pyk], ("post", j))
                P.stt("dve", tmp[:], psY[:], rstd, G[:], ALU.mult, ALU.mult, [pyk, ("rstd", ("post", j)), ("G", "m")], ["tmp"])
                P.tt("pool", x[:], tmp[:], x[:], ALU.add, ["tmp", xk], [xk])
                P.dma(dst(who, ti), x[:], [xk], [("h", who, ti)])
            if bi + 1 < len(blocks) and not nxt_same:
                prologue(bi + 1)


class CB:
    pass


def alloc_common(P, nxb=2):
    cb = CB()
    cb.A = P.sb("cA", [128, D], F32)
    cb.Bt = P.sb("cB", [128, D], F32)
    cb.G = P.sb("cG", [128, D], F32)
    cb.t1 = P.sb("ct1", [128, D], F32)
    cb.t2 = P.sb("ct2", [128, D], F32)
    cb.xb = [P.sb(f"cxb{i}", [128, D], F32) for i in range(nxb)]
    cb.xm = [P.sb(f"cxm{i}", [128, D], BF16) for i in range(2)]
    cb.tmp = P.sb("ctmp", [128, D], F32)
    cb.junk = P.sb("cjunk", [128, D], BF16)
    cb.small = P.sb("csmall", [128, 16], F32)
    cb.ident = P.sb("cident", [128, 128], BF16)
    cb.psT = P.ps("cpsT", [128, 8, 128], BF16)
    cb.gi = 0
    cb.mi = 0
    P.dma(cb.ident[:], P.C.ident_bf[:, :], [], ["ident"])
    return cb


def mod_tiles(P, cb, l, sub, who, weight):
    load_mod_tiles(P, P.C, l, sub, who, weight, cb.A, cb.Bt, cb.G, cb.t1, cb.t2, "m", "ct1", "ct2")


def pro_tile(P, cb, src_ap, htok):
    x = cb.xb[cb.gi % len(cb.xb)]
    xk = ("cxb", cb.gi % len(cb.xb))
    cb.gi += 1
    P.dma(x[:], src_ap, [htok], [xk])
    ss, rstd = cb.small[:, 0:1], cb.small[:, 1:2]
    rms_rstd(P, x[:], cb.junk[:], ss, rstd, D, [xk], "pre")
    P.stt("dve", cb.tmp[:], x[:], rstd, cb.A[:], ALU.mult, ALU.mult, [xk, ("rstd", "pre"), ("A", "m")], ["ctmp"])
    j = cb.mi % 2
    cb.mi += 1
    m = cb.xm[j]
    P.tt("pool", m[:], cb.tmp[:], cb.Bt[:], ALU.add, ["ctmp", ("B", "m")], [("cxm", j)])
    for k in range(8):
        P.tr(cb.psT[:, k, :], m[:, k * 128:(k + 1) * 128], cb.ident[:], [("cxm", j), "ident"], ["cpsT"])
    return x, xk


def epi_tile(P, cb, psY, pkeys, x, xk, dst_ap, htok, bias=None):
    ss, rstd = cb.small[:, 2:3], cb.small[:, 3:4]
    src = psY
    sk = list(pkeys)
    if bias is not None:
        P.tt("dve", cb.t1[:], psY, bias[0], ALU.add, sk + [bias[1]], ["ct1"])
        src = cb.t1[:]
        sk = ["ct1"]
    rms_rstd(P, src, cb.junk[:], ss, rstd, D, sk, "post")
    P.stt("dve", cb.tmp[:], src, rstd, cb.G[:], ALU.mult, ALU.mult, sk + [("rstd", "post"), ("G", "m")], ["ctmp"])
    P.tt("pool", x[:], cb.tmp[:], x[:], ALU.add, ["ctmp", xk], [xk])
    P.dma(dst_ap, x[:], [xk], [htok])


def load_w_bf16(P, dst, src_ap, stg, nk, ncols, name, cnt):
    cv = ["pool", "act", "dve"]
    for k in range(nk):
        n = cnt[0]
        s = stg[n % 2]
        P.dma(s[:, 0:ncols], src_ap[k * 128:(k + 1) * 128, :], [], [("cstg", n % 2)], q=("sp" if n % 2 == 0 else "act"))
        P.copy(cv[n % 3], dst[:, k, 0:ncols], s[:, 0:ncols], [("cstg", n % 2)], [(name, k)])
        cnt[0] += 1


MLA_SCALE = 192.0 ** -0.5


def mla_proj_phase(P, C, l, j, ctx_q):
    with P.phase():
        cb = alloc_common(P)
        stg = [P.sb(f"stg{i}", [128, 1024], F32) for i in range(2)]
        Wdq = P.sb("Wdq", [128, 8, 512], BF16)
        Wdc = P.sb("Wdc", [128, 8, 256], BF16)
        Wdr = P.sb("Wdr", [128, 8, 128], BF16)
        Wdrp = P.sb("Wdrp", [128, 8, 128], BF16)
        Wqn = P.sb("Wqn", [128, 4, 1024], BF16)
        Wqr = P.sb("Wqr", [128, 4, 512], BF16)
        Wqrp = P.sb("Wqrp", [128, 4, 512], BF16)
        Wk = P.sb("Wk", [128, 2, 1024], BF16)
        Wv = P.sb("Wv", [128, 2, 1024], BF16)
        cnt = [0]
        load_w_bf16(P, Wdq, C.mla_w_dq[j], stg, 8, 512, "Wdq", cnt)
        load_w_bf16(P, Wdc, C.mla_dkv_c[j], stg, 8, 256, "Wdc", cnt)
        load_w_bf16(P, Wdr, C.mla_dkv_r[j], stg, 8, 128, "Wdr", cnt)
        load_w_bf16(P, Wdrp, C.mla_dkv_rp[j], stg, 8, 128, "Wdrp", cnt)
        load_w_bf16(P, Wqn, C.mla_uq_n[j], stg, 4, 1024, "Wqn", cnt)
        load_w_bf16(P, Wqr, C.mla_uq_r[j], stg, 4, 512, "Wqr", cnt)
        load_w_bf16(P, Wqrp, C.mla_uq_rp[j], stg, 4, 512, "Wqrp", cnt)
        load_w_bf16(P, Wk, C.mla_ukv_k[j], stg, 2, 1024, "Wk", cnt)
        load_w_bf16(P, Wv, C.mla_ukv_v[j], stg, 2, 1024, "Wv", cnt)
        gq = P.sb("gq", [128, 512], F32)
        gkv = P.sb("gkv", [128, 256], F32)
        P.dma(gq[:], C.mla_q_norm[j:j + 1, :].broadcast_to([128, 512]), [], ["gq"])
        P.dma(gkv[:], C.mla_kv_norm[j:j + 1, :].broadcast_to([128, 256]), [], ["gkv"])
        cos2 = P.sb("cos2", [128, T], F32)
        sin2 = P.sb("sin2", [128, T], F32)
        P.dma(cos2[:], C.rope_cos[:, :], [], ["cos2"])
        P.dma(sin2[:], C.rope_sin[:, :], [], ["sin2"], q="act")
        uT = P.sb("uT", [128, 8, 512], BF16)
        cqnT = P.sb("cqnT", [128, 4, 512], BF16)
        ckvnT = P.sb("ckvnT", [128, 2, 512], BF16)
        cqn = P.sb("cqn", [128, 512], BF16)
        ckvn = P.sb("ckvn", [128, 256], BF16)
        Vt = [P.sb(f"Vt{i}", [128, 1024], BF16) for i in range(2)]
        QTb = P.sb("QTb", [128, 8, 512], BF16)
        KTb = P.sb("KTb", [128, 8, 512], BF16)
        QrTb = P.sb("QrTb", [128, 4, 512], BF16)
        KrTb = P.sb("KrTb", [128, 512], BF16)
        r1 = P.sb("r1", [128, 512], F32)
        r2 = P.sb("r2", [128, 512], F32)
        sm2 = P.sb("sm2", [128, 8], F32)
        psq = P.ps("psq", [128, 512])
        pskv = P.ps("pskv", [128, 512])
        psT2 = P.ps("psT2", [128, 8, 128], BF16)
        psV = P.ps("psV", [128, 1024])
        psB = [P.ps(f"psB{i}", [128, 512]) for i in range(2)]
        blocks = [(0, t0, 4) for t0 in range(0, 16, 4)] + [(1, 0, 2)]
        cur = None
        bctr = 0
        for (who, t0, nt) in blocks:
            if who != cur:
                mod_tiles(P, cb, l, 1, who, 1.0)
                cur = who
            nq = nt * 128
            c0 = t0 * 128 + (T if who == 1 else 0)
            for jt in range(nt):
                ti = t0 + jt
                gt = ti + (16 if who == 1 else 0)
                src = (C.hL if who == 0 else C.hC)[ti * 128:(ti + 1) * 128, :]
                pro_tile(P, cb, src, ("h", who, ti))
                P.copy("act", uT[:, :, jt * 128:(jt + 1) * 128], cb.psT[:], ["cpsT"], [("uT", jt)])
                for k in range(8):
                    P.mm(psq[:, :], uT[:, k, jt * 128:(jt + 1) * 128], Wdq[:, k, :], k == 0, k == 7, [("uT", jt), ("Wdq", k)], ["psq"])
                for k in range(8):
                    P.mm(pskv[:, 0:256], uT[:, k, jt * 128:(jt + 1) * 128], Wdc[:, k, :], k == 0, k == 7, [("uT", jt), ("Wdc", k)], ["pskv"])
                P.act(cb.junk[:, 0:512], psq[:, :], AF.Square, ["psq"], [("junk", "q"), "ssq"], accum_out=sm2[:, 0:1])
                P.ts("dve", sm2[:, 1:2], sm2[:, 0:1], 1.0 / 512, EPS, ALU.mult, ALU.add, ["ssq"], ["rq"])
                P.act(sm2[:, 1:2], sm2[:, 1:2], AF.Sqrt, ["rq"], ["rq"])
                P.recip(sm2[:, 1:2], sm2[:, 1:2], ["rq"], ["rq"])
                P.stt("dve", cqn[:], psq[:, :], sm2[:, 1:2], gq[:], ALU.mult, ALU.mult, ["psq", "rq", "gq"], ["cqn"])
                P.act(cb.junk[:, 512:768], pskv[:, 0:256], AF.Square, ["pskv"], [("junk", "kv"), "sskv"], accum_out=sm2[:, 2:3])
                P.ts("dve", sm2[:, 3:4], sm2[:, 2:3], 1.0 / 256, EPS, ALU.mult, ALU.add, ["sskv"], ["rkv"])
                P.act(sm2[:, 3:4], sm2[:, 3:4], AF.Sqrt, ["rkv"], ["rkv"])
                P.recip(sm2[:, 3:4], sm2[:, 3:4], ["rkv"], ["rkv"])
                P.stt("dve", ckvn[:], pskv[:, 0:256], sm2[:, 3:4], gkv[:], ALU.mult, ALU.mult, ["pskv", "rkv", "gkv"], ["ckvn"])
                for r in range(4):
                    P.tr(psT2[:, r, :], cqn[:, r * 128:(r + 1) * 128], cb.ident[:], ["cqn", "ident"], ["psT2"])
                for r in range(2):
                    P.tr(psT2[:, 4 + r, :], ckvn[:, r * 128:(r + 1) * 128], cb.ident[:], ["ckvn", "ident"], ["psT2"])
                P.copy("act", cqnT[:, :, jt * 128:(jt + 1) * 128], psT2[:, 0:4, :], [], ["psT2", ("cqnT", jt)])
                P.copy("act", ckvnT[:, :, jt * 128:(jt + 1) * 128], psT2[:, 4:6, :], [], ["psT2", ("ckvnT", jt)])
                for nh in range(2):
                    for r in range(2):
                        P.mm(psV[:, nh * 512:(nh + 1) * 512], ckvnT[:, r, jt * 128:(jt + 1) * 128], Wv[:, r, nh * 512:(nh + 1) * 512],
                             r == 0, r == 1, [("ckvnT", jt), ("Wv", r)], ["psV"])
                vt = Vt[gt % 2]
                P.copy("act", vt[:], psV[:], ["psV"], [("Vt", gt % 2)])
                P.dma(C.sV[:, gt, :], vt[:], [("Vt", gt % 2)], [("sV", gt)])
            uTr = [("uT", jt) for jt in range(nt)]
            cqr = [("cqnT", jt) for jt in range(nt)]
            ckr = [("ckvnT", jt) for jt in range(nt)]
            for h in range(8):
                pb = psB[bctr % 2]; pk = ("psB", bctr % 2); bctr += 1
                for r in range(4):
                    P.mm(pb[:, 0:nq], Wqn[:, r, h * 128:(h + 1) * 128], cqnT[:, r, 0:nq], r == 0, r == 3, cqr + [("Wqn", r)], [pk])
                P.copy("act" if h % 2 == 0 else "dve", QTb[:, h, 0:nq], pb[:, 0:nq], [pk], [("QTb", h)])
                pb = psB[bctr % 2]; pk = ("psB", bctr % 2); bctr += 1
                for r in range(2):
                    P.mm(pb[:, 0:nq], Wk[:, r, h * 128:(h + 1) * 128], ckvnT[:, r, 0:nq], r == 0, r == 1, ckr + [("Wk", r)], [pk])
                P.copy("dve" if h % 2 == 0 else "act", KTb[:, h, 0:nq], pb[:, 0:nq], [pk], [("KTb", h)])
            jobs = [(Wqr, Wqrp, 4, hp, cqnT, cqr, "Wqr", "Wqrp", QrTb[:, hp, 0:nq], ("QrTb", hp)) for hp in range(4)]
            jobs.append((Wdr, Wdrp, 8, 0, uT, uTr, "Wdr", "Wdrp", KrTb[:, 0:nq], ("KrTb", 0)))
            for (Wa, Wb, nk, hp, rhsT, rk, na, nb_, dst, dk) in jobs:
                pa = psB[bctr % 2]; pka = ("psB", bctr % 2); bctr += 1
                for r in range(nk):
                    P.mm(pa[:, 0:nq], Wa[:, r, hp * 128:(hp + 1) * 128], rhsT[:, r, 0:nq], r == 0, r == nk - 1, rk + [(na, r)], [pka])
                if who == 1:
                    P.copy("act", dst, pa[:, 0:nq], [pka], [dk])
                    continue
                P.tt("dve", r1[:, 0:nq], pa[:, 0:nq], cos2[:, c0:c0 + nq], ALU.mult, [pka, "cos2"], ["r1"])
                pb = psB[bctr % 2]; pkb = ("psB", bctr % 2); bctr += 1
                for r in range(nk):
                    P.mm(pb[:, 0:nq], Wb[:, r, hp * 128:(hp + 1) * 128], rhsT[:, r, 0:nq], r == 0, r == nk - 1, rk + [(nb_, r)], [pkb])
                P.tt("dve", r2[:, 0:nq], pb[:, 0:nq], sin2[:, c0:c0 + nq], ALU.mult, [pkb, "sin2"], ["r2"])
                P.tt("pool", dst, r1[:, 0:nq], r2[:, 0:nq], ALU.add, ["r1", "r2"], [dk])
            P.dma(C.sQT[:, :, c0:c0 + nq], QTb[:, :, 0:nq], [("QTb", h) for h in range(8)], [("sQT", c0)])
            P.dma(C.sKT[:, :, c0:c0 + nq], KTb[:, :, 0:nq], [("KTb", h) for h in range(8)], [("sKT", c0)], q="act")
            P.dma(C.sQrT[:, :, c0:c0 + nq], QrTb[:, :, 0:nq], [("QrTb", hp) for hp in range(4)], [("sQrT", c0)])
            P.dma(C.sKrT[:, c0:c0 + nq], KrTb[:, 0:nq], [("KrTb", 0)], [("sKrT", c0)], q="act")


def mla_attn_phase(P, C, l, j, ctx_q):
    with P.phase():
        cb = alloc_common(P)
        stg = [P.sb(f"stg{i}", [128, 1024], F32) for i in range(2)]
        Wo = P.sb("Wo", [128, 8, 1024], BF16)
        cnt = [0]
        load_w_bf16(P, Wo, C.mla_w_o[j], stg, 8, 1024, "Wo", cnt)
        KT = P.sb("KT", [128, 8, T + TC], BF16)
        KrT = P.sb("KrT", [128, T + TC], BF16)
        V = P.sb("V", [128, NT, 1024], BF16)
        P.dma(KT[:], C.sKT[:, :, :], [], ["KT"])
        P.dma(KrT[:], C.sKrT[:, :], [], ["KrT"], q="act")
        P.dma(V[:, 0:9, :], C.sV[:, 0:9, :], [], ["V0"])
        P.dma(V[:, 9:18, :], C.sV[:, 9:18, :], [], ["V1"], q="act")
        ones = P.sb("ones", [128, 128], BF16)
        P.add("pool", lambda e: e.memset(ones[:], 1.0), [], ["ones"])
        QTb = [P.sb(f"QTb{i}", [128, 8, 512], BF16) for i in range(2)]
        QrTb = [P.sb(f"QrTb{i}", [128, 4, 512], BF16) for i in range(2)]
        OTb = P.sb("OTb", [128, 8, 512], BF16)
        PT = [P.sb(f"PT{i}", [128, 512], BF16) for i in range(4)]
        rz = P.sb("rz", [128, 512], F32)
        psS = [P.ps(f"psS{i}", [128, 512]) for i in range(3)]
        psO = P.ps("psO", [128, 512])
        psZ = P.ps("psZ", [128, 512])
        psY = P.ps("psY", [128, 1024])
        blocks = [(0, t0, 4, list(range(NT))) for t0 in range(0, 16, 4)]
        if ctx_q:
            blocks.append((1, 0, 2, [16, 17]))
        cur = None
        sc_ = 0
        pc_ = 0
        for bi, (who, t0, nt, kcs) in enumerate(blocks):
            if who != cur:
                mod_tiles(P, cb, l, 1, who, 1.0)
                cur = who
            nq = nt * 128
            c0 = t0 * 128 + (T if who == 1 else 0)
            qt, qr = QTb[bi % 2], QrTb[bi % 2]
            qk, qrk = ("QTb", bi % 2), ("QrTb", bi % 2)
            P.dma(qt[:, :, 0:nq], C.sQT[:, :, c0:c0 + nq], [], [qk])
            P.dma(qr[:, :, 0:nq], C.sQrT[:, :, c0:c0 + nq], [], [qrk], q="act")
            for h in range(8):
                pbase = (h % 2) * 64
                pend = []
                n = len(kcs)
                SK = 2
                for i in range(n + SK):
                    if i < n:
                        kc = kcs[i]
                        ps_ = psS[sc_ % 3]; psk = ("psS", sc_ % 3); sc_ += 1
                        P.mm(ps_[:, 0:nq], KT[:, h, kc * 128:(kc + 1) * 128], qt[:, h, 0:nq], True, False, ["KT", qk], [psk])
                        P.mm(ps_[:, 0:nq], KrT[pbase:pbase + 64, kc * 128:(kc + 1) * 128], qr[pbase:pbase + 64, h // 2, 0:nq],
                             False, True, ["KrT", qrk], [psk])
                        pt = PT[pc_ % 4]; ptk = ("PT", pc_ % 4); pc_ += 1
                        P.act(pt[:, 0:nq], ps_[:, 0:nq], AF.Exp, [psk], [ptk], scale=MLA_SCALE)
                        pend.append((kc, pt, ptk))
                    if i >= SK:
                        kc, pt, ptk = pend[i - SK]
                        P.mm(psO[:, 0:nq], V[:, kc, h * 128:(h + 1) * 128], pt[:, 0:nq], i == SK, i == n + SK - 1, ["V0", "V1", ptk], ["psO"])
                        P.mm(psZ[:, 0:nq], ones[:], pt[:, 0:nq], i == SK, i == n + SK - 1, ["ones", ptk], ["psZ"])
                P.recip(rz[:, 0:nq], psZ[:, 0:nq], ["psZ"], ["rz"])
                P.tt("dve", OTb[:, h, 0:nq], psO[:, 0:nq], rz[:, 0:nq], ALU.mult, ["psO", "rz"], [("OTb", h)])
            for jt in range(nt):
                ti = t0 + jt
                hap = (C.hL if who == 0 else C.hC)[ti * 128:(ti + 1) * 128, :]
                x = cb.xb[cb.gi % 2]; xk = ("cxb", cb.gi % 2); cb.gi += 1
                P.dma(x[:], hap, [("h", who, ti)], [xk])
                for nh in range(2):
                    for h in range(8):
                        P.mm(psY[:, nh * 512:(nh + 1) * 512], OTb[:, h, jt * 128:(jt + 1) * 128], Wo[:, h, nh * 512:(nh + 1) * 512],
                             h == 0, h == 7, [("OTb", h), ("Wo", h)], ["psY"])
                epi_tile(P, cb, psY[:], ["psY"], x, xk, hap, ("h", who, ti))


def _perm64():
    p = np.arange(64)
    return np.where((p % 32) < 16, p + 16, p - 16)


def host_layout(inputs):
    o = {}
    perm = _perm64()
    wdkv = inputs["mla_w_dkv"]
    o["mla_dkv_c"] = np.ascontiguousarray(wdkv[:, :, :256])
    r = wdkv[:, :, 256:320]
    o["mla_dkv_r"] = np.ascontiguousarray(np.concatenate([r, r], axis=-1))
    rp = r[:, :, perm]
    o["mla_dkv_rp"] = np.ascontiguousarray(np.concatenate([rp, rp], axis=-1))
    wuq = inputs["mla_w_uq"].reshape(-1, 512, 8, 192)
    o["mla_uq_n"] = np.ascontiguousarray(wuq[:, :, :, :128].reshape(-1, 512, 1024))
    o["mla_uq_r"] = np.ascontiguousarray(wuq[:, :, :, 128:].reshape(-1, 512, 512))
    o["mla_uq_rp"] = np.ascontiguousarray(wuq[:, :, :, 128:][:, :, :, perm].reshape(-1, 512, 512))
    wukv = inputs["mla_w_ukv"].reshape(-1, 256, 8, 256)
    o["mla_ukv_k"] = np.ascontiguousarray(wukv[:, :, :, :128].reshape(-1, 256, 1024))
    o["mla_ukv_v"] = np.ascontiguousarray(wukv[:, :, :, 128:].reshape(-1, 256, 1024))
    for k in ("mla_w_dq", "mla_q_norm", "mla_kv_norm", "mla_w_o", "c_ctx", "mod_w", "mod_b", "norm_pre", "norm_post",
              "ffn_w_gate", "ffn_w_up", "ffn_w_down"):
        o[k] = np.ascontiguousarray(inputs[k])
    o["ident_bf"] = np.eye(128, dtype=np.float32).astype(ml_dtypes.bfloat16)
    t = np.arange(T)
    p = np.arange(64)
    axis, half, pair = p // 32, (p % 32) // 16, p % 16
    inv = (np.float32(10000.0) ** (-(pair.astype(np.float32)) / np.float32(16.0))).astype(np.float32)
    pos = np.where(axis[:, None] == 0, (t // 64)[None, :], (t % 64)[None, :]).astype(np.float32)
    ang = (pos * inv[:, None]).astype(np.float32)
    cs = np.cos(ang).astype(np.float32)
    sn = (np.sin(ang) * np.where(half == 0, -1.0, 1.0)[:, None]).astype(np.float32)
    def dft(n):
        k = (np.arange(n)[:, None] * np.arange(n)[None, :]) % n
        a = 2.0 * np.pi * k.astype(np.float64) / n
        return np.cos(a), np.sin(a)
    bf = ml_dtypes.bfloat16
    c_, s_ = dft(T)
    o["dft_ct"] = c_.astype(np.float32).astype(bf); o["dft_st"] = s_.astype(np.float32).astype(bf)
    c_, s_ = dft(TC)
    o["dft_ct_c"] = c_.astype(np.float32).astype(bf); o["dft_st_c"] = s_.astype(np.float32).astype(bf)
    c_, s_ = dft(128)
    o["dft_cc"] = c_.astype(np.float32).astype(bf); o["dft_scn"] = (-s_).astype(np.float32).astype(bf)
    for k in ("fnet_w_o", "fnet_b_o", "rwkv_mix", "rwkv_w_r", "rwkv_w_k", "rwkv_w_v", "rwkv_w0", "rwkv_w2", "rwkv_a0", "rwkv_a2",
              "rwkv_g1", "rwkv_g2", "rwkv_k_k", "rwkv_k_a", "rwkv_r_k", "rwkv_ln_w", "rwkv_ln_b", "rwkv_w_o"):
        o[k] = np.ascontiguousarray(inputs[k])
    o["rwkv_w1c"] = np.ascontiguousarray(np.concatenate([inputs["rwkv_w1"][:, 0], inputs["rwkv_w1"][:, 1]], axis=-1))
    o["rwkv_a1c"] = np.ascontiguousarray(np.concatenate([inputs["rwkv_a1"][:, 0], inputs["rwkv_a1"][:, 1]], axis=-1))
    o["ident_f"] = np.eye(128, dtype=np.float32)
    ii = np.arange(128)
    s_, t_ = ii[:, None], ii[None, :]
    tri0 = (s_ <= t_).astype(np.float32); tri1 = (s_ >= t_).astype(np.float32)
    st0 = (s_ < t_).astype(np.float32); st1 = (s_ > t_).astype(np.float32)
    o["rw_tri"] = np.stack([tri0, tri1])
    rep4 = lambda m: np.ascontiguousarray(np.concatenate([m] * 4, axis=1))
    o["rw_mS"] = np.stack([rep4(st0), rep4(st1)])
    o["rw_mI"] = np.stack([rep4(tri0), rep4(tri1)])
    bd = np.kron(np.eye(4, dtype=np.float32), np.ones((32, 32), np.float32))
    o["rw_mSd"] = np.stack([rep4(st0 * bd), rep4(st1 * bd)])
    o["rw_mSo"] = np.stack([rep4(st0 * (1 - bd)), rep4(st1 * (1 - bd))])
    o["rw_mTd"] = np.stack([rep4(st0.T * bd), rep4(st1.T * bd)])
    o["rw_I4"] = rep4(np.eye(128, dtype=np.float32))
    o["rope_cos"] = np.ascontiguousarray(np.concatenate([cs, cs], axis=0))
    o["rope_sin"] = np.ascontiguousarray(np.concatenate([sn, sn], axis=0))
    return o


IN_SHAPES = {
    "c_ctx": ([D], F32), "mod_w": ([DEPTH, D, 9 * D], F32), "mod_b": ([DEPTH, 9 * D], F32),
    "norm_pre": ([DEPTH, 3, D], F32), "norm_post": ([DEPTH, 3, D], F32),
    "ffn_w_gate": ([DEPTH, 2, D, FF], F32), "ffn_w_up": ([DEPTH, 2, D, FF], F32), "ffn_w_down": ([DEPTH, 2, FF, D], F32),
    "ident_bf": ([128, 128], BF16), "rope_cos": ([128, T], F32), "rope_sin": ([128, T], F32),
    "mla_w_dq": ([2, D, 512], F32), "mla_q_norm": ([2, 512], F32), "mla_kv_norm": ([2, 256], F32), "mla_w_o": ([2, D, D], F32),
    "mla_dkv_c": ([2, D, 256], F32), "mla_dkv_r": ([2, D, 128], F32), "mla_dkv_rp": ([2, D, 128], F32),
    "mla_uq_n": ([2, 512, 1024], F32), "mla_uq_r": ([2, 512, 512], F32), "mla_uq_rp": ([2, 512, 512], F32),
    "mla_ukv_k": ([2, 256, 1024], F32), "mla_ukv_v": ([2, 256, 1024], F32),
    "fnet_w_o": ([1, D, D], F32), "fnet_b_o": ([1, D], F32),
    "dft_ct": ([T, T], BF16), "dft_st": ([T, T], BF16), "dft_ct_c": ([TC, TC], BF16), "dft_st_c": ([TC, TC], BF16),
    "dft_cc": ([128, 128], BF16), "dft_scn": ([128, 128], BF16),
    "rwkv_mix": ([1, 6, D], F32), "rwkv_w_r": ([1, D, D], F32), "rwkv_w_k": ([1, D, D], F32), "rwkv_w_v": ([1, D, D], F32),
    "rwkv_w0": ([1, 2, D], F32), "rwkv_w2": ([1, 2, 64, D], F32), "rwkv_a0": ([1, 2, D], F32), "rwkv_a2": ([1, 2, 64, D], F32),
    "rwkv_g1": ([1, D, 160], F32), "rwkv_g2": ([1, 160, D], F32), "rwkv_k_k": ([1, D], F32), "rwkv_k_a": ([1, D], F32),
    "rwkv_r_k": ([1, 16, 64], F32), "rwkv_ln_w": ([1, D], F32), "rwkv_ln_b": ([1, D], F32), "rwkv_w_o": ([1, D, D], F32),
    "rwkv_w1c": ([1, D, 128], F32), "rwkv_a1c": ([1, D, 128], F32), "ident_f": ([128, 128], F32),
    "rw_tri": ([2, 128, 128], F32), "rw_mS": ([2, 128, 512], F32), "rw_mI": ([2, 128, 512], F32), "rw_mSd": ([2, 128, 512], F32), "rw_mSo": ([2, 128, 512], F32),
    "rw_mTd": ([2, 128, 512], F32), "rw_I4": ([128, 512], F32),
}


def build(nsteps=None, dbg=False):
    nc = bass.Bass("TRN2", target_bir_lowering=False)
    C = Ctx()

    def din(name, shape, dt=F32):
        return nc.dram_tensor(name, list(shape), dt, kind="ExternalInput").ap()

    def scratch(name, shape, dt=F32):
        return nc.dram_tensor(name, list(shape), dt, kind="Internal").ap()

    C.x = din("x", [T, D])
    C.c = din("c", [1, D])
    C.ctx = din("ctx", [TC, D])
    for k, (shp, dt) in IN_SHAPES.items():
        setattr(C, k, din(k, shp, dt))
    C.out = nc.dram_tensor("out", [T, D], F32, kind="ExternalOutput").ap()
    C.modv = scratch("modv", [DEPTH, 2, 9 * D])
    C.hL = scratch("hL", [T, D])
    C.hC = scratch("hC", [TC, D])
    C.sQT = scratch("sQT", [128, 8, T + TC], BF16)
    C.sQrT = scratch("sQrT", [128, 4, T + TC], BF16)
    C.sKT = scratch("sKT", [128, 8, T + TC], BF16)
    C.sKrT = scratch("sKrT", [128, T + TC], BF16)
    C.sV = scratch("sV", [128, NT, 1024], BF16)
    for nm in ("sR", "sK", "sVv", "sKK", "sG"):
        setattr(C, nm, scratch(nm, [NTOK, D]))
    C.sLW = scratch("sLW", [2, NTOK, D])
    C.sA = scratch("sA", [2, NTOK, D])
    C.sY = scratch("sY", [2, NTOK, D])
    if dbg:
        C.dbg_hC = nc.dram_tensor("dbg_hC", [TC, D], F32, kind="ExternalOutput").ap()

    def tile_ap(base_l, base_c):
        def f(who, ti):
            b = base_l if who == 0 else base_c
            return b[ti * 128:(ti + 1) * 128, :]
        return f

    steps = []
    for l in range(DEPTH):
        steps += [("ffn", l, 0), ("mix", l), ("ffn", l, 1)]
    if nsteps is not None:
        steps = steps[:nsteps]
    with ExitStack() as st:
        P = Prog(nc, st)
        P.C = C
        mod_phase(P, C)
        for si, stp in enumerate(steps):
            l = stp[1]
            kind, j, last = l % 3, l // 3, l == DEPTH - 1
            final = (nsteps is None and si == len(steps) - 1)
            if stp[0] == "ffn":
                f = stp[2]
                src = tile_ap(C.x, C.ctx) if si == 0 else tile_ap(C.hL, C.hC)
                dst = tile_ap(C.out, C.hC) if final else tile_ap(C.hL, C.hC)
                ffn_phase(P, C, l, f, src, dst, T // 128, (f == 0) or (not last))
            else:
                if kind == 0:
                    mla_proj_phase(P, C, l, j, not last)
                    mla_attn_phase(P, C, l, j, not last)
                elif kind == 1:
                    fnet_phase(P, C, l, j, not last)
                else:
                    rwkv_phases(P, C, l, j, not last)
        if nsteps is not None:
            with P.phase():
                P.dma(C.out[:, :], C.hL[:, :], [], ["dm"])
                if dbg:
                    P.dma(C.dbg_hC[:, :], C.hC[:, :], [], ["dh"])
    return nc


_NC_CACHE = {}


def make_in_maps(inputs, ncores=8):
    shared = host_layout(inputs)
    in_maps = []
    for b in range(ncores):
        m = dict(shared)
        m["x"] = np.ascontiguousarray(inputs["x"][b])
        m["c"] = np.ascontiguousarray(inputs["c"][b:b + 1])
        m["ctx"] = np.ascontiguousarray(inputs["ctx"][b])
        in_maps.append(m)
    return in_maps


def kernel(**inputs):
    inputs = {k: np.asarray(v) for k, v in inputs.items()}
    if "nc" not in _NC_CACHE:
        _NC_CACHE["nc"] = build()
    nc = _NC_CACHE["nc"]
    in_maps = make_in_maps(inputs, 8)
    res = run_bass_kernel_spmd(nc, in_maps, core_ids=list(range(8)))
    return np.stack([np.asarray(r["out"]) for r in res.results], axis=0).astype(np.float32)


def fnet_phase(P, C, l, j, ctx_out):
    with P.phase():
        cb = alloc_common(P)
        stg = [P.sb(f"stg{i}", [128, 1024], F32) for i in range(2)]
        Wo = P.sb("Wo", [128, 8, 1024], BF16)
        cnt = [0]
        load_w_bf16(P, Wo, C.fnet_w_o[j], stg, 8, 1024, "Wo", cnt)
        bo = P.sb("bo", [128, D], F32)
        P.dma(bo[:], C.fnet_b_o[j:j + 1, :].broadcast_to([128, D]), [], ["bo"])
        cc = P.sb("cc", [128, 128], BF16)
        scn = P.sb("scn", [128, 128], BF16)
        P.dma(cc[:], C.dft_cc[:, :], [], ["cc"])
        P.dma(scn[:], C.dft_scn[:, :], [], ["scn"])
        Zc = P.sb("Zc", [128, 16, 1024], BF16)
        Zs = P.sb("Zs", [128, 16, 1024], BF16)
        NQ = 256
        CTb = [P.sb(f"CTb{i}", [128, 16, NQ], BF16) for i in range(2)]
        STb = [P.sb(f"STb{i}", [128, 16, NQ], BF16) for i in range(2)]
        uT = P.sb("uT", [128, 8, 128], BF16)
        FT = P.sb("FT", [128, 8, NQ], BF16)
        psZ = [P.ps(f"psZ{i}", [128, 1024]) for i in range(2)]
        psF = [P.ps(f"psF{i}", [128, 512]) for i in range(1)]
        psY = P.ps("psY", [128, 1024])
        groups = [(0, T, C.hL, C.dft_ct, C.dft_st)]
        if ctx_out:
            groups.append((1, TC, C.hC, C.dft_ct_c, C.dft_st_c))
        bi = 0
        fi = 0
        for (who, Tt, hbase, ct, st_) in groups:
            ntl = Tt // 128
            mod_tiles(P, cb, l, 1, who, 1.0)
            scale = float((Tt * 128) ** -0.5)
            for ti in range(ntl):
                pro_tile(P, cb, hbase[ti * 128:(ti + 1) * 128, :], ("h", who, ti))
                P.copy("act", uT[:], cb.psT[:], ["cpsT"], ["uT"])
                for (tab, tk, pz, pzk, Zd, zk, ce) in ((cc, "cc", psZ[0], "psZ0", Zc, "Zc", "act"), (scn, "scn", psZ[1], "psZ1", Zs, "Zs", "dve")):
                    for g in range(8):
                        P.mm(pz[:, g * 128:(g + 1) * 128], uT[:, g, :], tab[:], True, True, ["uT", tk], [pzk])
                    P.copy(ce, Zd[:, ti, :], pz[:], [pzk], [(zk, ti)])
            zr = [("Zc", ti) for ti in range(ntl)] + [("Zs", ti) for ti in range(ntl)]
            for b0 in range(0, Tt, NQ):
                cbuf, sbuf_ = CTb[bi % 2], STb[bi % 2]
                ck, sk = ("CTb", bi % 2), ("STb", bi % 2)
                bi += 1
                P.dma(cbuf[:, 0:ntl, :], ct[:, b0:b0 + NQ].rearrange("(c p) n -> p c n", p=128), [], [ck])
                P.dma(sbuf_[:, 0:ntl, :], st_[:, b0:b0 + NQ].rearrange("(c p) n -> p c n", p=128), [], [sk], q="act")
                for g in range(8):
                    pf = psF[0]; pfk = ("psF", 0); fi += 1
                    for tc_ in range(ntl):
                        P.mm(pf[:, 0:NQ], Zc[:, tc_, g * 128:(g + 1) * 128], cbuf[:, tc_, :], tc_ == 0, False, zr + [ck], [pfk])
                        P.mm(pf[:, 0:NQ], Zs[:, tc_, g * 128:(g + 1) * 128], sbuf_[:, tc_, :], False, tc_ == ntl - 1, zr + [sk], [pfk])
                    P.act(FT[:, g, :], pf[:, 0:NQ], AF.Copy, [pfk], [("FT", g)], scale=scale)
                for jt in range(NQ // 128):
                    ti = b0 // 128 + jt
                    hap = hbase[ti * 128:(ti + 1) * 128, :]
                    x = cb.xb[cb.gi % 2]; xk = ("cxb", cb.gi % 2); cb.gi += 1
                    P.dma(x[:], hap, [("h", who, ti)], [xk])
                    for nh in range(2):
                        for g in range(8):
                            P.mm(psY[:, nh * 512:(nh + 1) * 512], FT[:, g, jt * 128:(jt + 1) * 128], Wo[:, g, nh * 512:(nh + 1) * 512],
                                 g == 0, g == 7, [("FT", g), ("Wo", g)], ["psY"])
                    epi_tile(P, cb, psY[:], ["psY"], x, xk, hap, ("h", who, ti), bias=(bo[:], "bo"))


NTOK = T + TC
DECAY_C = float(np.exp(-0.5))


def rwkv_feat_phase(P, C, l, j):
    UW = 2308
    with P.phase():
        cb = alloc_common(P)
        stg = [P.sb(f"stg{i}", [128, 1024], F32) for i in range(2)]
        uT = P.sb("uT", [128, 8, UW], BF16)
        xxT = P.sb("xxT", [128, 8, 256], BF16)
        tmpw = P.sb("tmpw", [128, 256], F32)
        P.add("pool", lambda e: e.memset(uT[:], 0.0), [], ["uTall"])
        Wr = P.sb("Wr", [128, 8, 1024], BF16)
        Wk = P.sb("Wk", [128, 8, 1024], BF16)
        Wv = P.sb("Wv", [128, 8, 1024], BF16)
        W1 = P.sb("W1", [128, 8, 128], BF16)
        A1 = P.sb("A1", [128, 8, 128], BF16)
        G1 = P.sb("G1", [128, 8, 160], BF16)
        W2 = P.sb("W2", [128, 1, 1024], BF16)
        A2 = P.sb("A2", [128, 1, 1024], BF16)
        G2 = P.sb("G2", [128, 2, 1024], BF16)
        cnt = [0]
        load_w_bf16(P, Wr, C.rwkv_w_r[j], stg, 8, 1024, "Wr", cnt)
        load_w_bf16(P, Wk, C.rwkv_w_k[j], stg, 8, 1024, "Wk", cnt)
        load_w_bf16(P, Wv, C.rwkv_w_v[j], stg, 8, 1024, "Wv", cnt)
        load_w_bf16(P, W1, C.rwkv_w1c[j], stg, 8, 128, "W1", cnt)
        load_w_bf16(P, A1, C.rwkv_a1c[j], stg, 8, 128, "A1", cnt)
        load_w_bf16(P, G1, C.rwkv_g1[j], stg, 8, 160, "G1", cnt)
        load_w_bf16(P, W2, C.rwkv_w2[j].rearrange("e r d -> (e r) d"), stg, 1, 1024, "W2", cnt)
        load_w_bf16(P, A2, C.rwkv_a2[j].rearrange("e r d -> (e r) d"), stg, 1, 1024, "A2", cnt)
        load_w_bf16(P, G2, C.rwkv_g2[j, 0:128, :], stg, 1, 1024, "G2", cnt)
        n = cnt[0]; s = stg[n % 2]
        P.dma(s[0:32, :], C.rwkv_g2[j, 128:160, :], [], [("cstg", n % 2)])
        P.copy("dve", G2[0:32, 1, :], s[0:32, :], [("cstg", n % 2)], [("G2", 1)])
        cnt[0] += 1
        mixc = P.sb("mixc", [128, 6, 8], F32)
        for m_ in range(6):
            P.dma(mixc[:, m_, :], C.rwkv_mix[j, m_].rearrange("(k p) -> p k", p=128), [], ["mixc"], allow_slow_non_contiguous=True)
        w0t = [P.sb(f"w0t{e}", [128, D], F32) for e in range(2)]
        a0t = [P.sb(f"a0t{e}", [128, D], F32) for e in range(2)]
        kkt = cb.t1
        for e in range(2):
            P.dma(w0t[e][:], C.rwkv_w0[j, e:e + 1, :].broadcast_to([128, D]), [], [("w0t", e)])
            P.dma(a0t[e][:], C.rwkv_a0[j, e:e + 1, :].broadcast_to([128, D]), [], [("a0t", e)], q="act")
        def ucol(who, ti):
            return (1 if who == 0 else 2051) + ti * 128
        for (who, ntl, hb) in ((0, 16, C.hL), (1, 2, C.hC)):
            mod_tiles(P, cb, l, 1, who, 1.0)
            for ti in range(ntl):
                pro_tile(P, cb, hb[ti * 128:(ti + 1) * 128, :], ("h", who, ti))
                c0 = ucol(who, ti)
                P.copy("act", uT[:, :, c0:c0 + 128], cb.psT[:], ["cpsT", "uTall"], [("uTt", who, ti)])
        allu = [("uTt", 0, ti) for ti in range(16)] + [("uTt", 1, ti) for ti in range(2)]
        P.dma(kkt[:], C.rwkv_k_k[j:j + 1, :].broadcast_to([128, D]), [], ["kkt", "ct1"])
        xm = [P.sb(f"xm{m}", [128, 8, 256], BF16) for m in range(6)]
        hW = P.sb("hW", [128, 256], BF16)
        hA = P.sb("hA", [128, 256], BF16)
        hG = P.sb("hG", [128, 2, 256], BF16)
        ot = [P.sb(f"ot{i}", [128, D], F32) for i in range(2)]
        sq = cb.tmp
        s16 = P.sb("s16", [128, 32], F32)
        psH = [P.ps(f"psH{i}", [128, 512]) for i in range(2)]
        psO = [P.ps(f"psO{i}", [128, 1024]) for i in range(2)]
        oc = [0]
        pc = [0]

        def out_tile(name_key):
            i = oc[0] % 2; oc[0] += 1
            return ot[i], ("ot", i)

        def ps_tile():
            i = pc[0] % 2; pc[0] += 1
            return psO[i], ("psO", i)

        blocks = [(0, t0, 2) for t0 in range(0, 16, 2)] + [(1, 0, 2)]
        hc_ = 0
        for (who, t0, nt) in blocks:
            nq = nt * 128
            c0 = ucol(who, t0)
            g0 = (t0 + (16 if who == 1 else 0)) * 128
            for k in range(8):
                P.tt("dve", tmpw[:, 0:nq], uT[:, k, c0 - 1:c0 - 1 + nq], uT[:, k, c0 + 1:c0 + 1 + nq], ALU.add, allu, ["tmpw"])
                P.stt("dve", xxT[:, k, 0:nq], tmpw[:, 0:nq], 0.5, uT[:, k, c0:c0 + nq], ALU.mult, ALU.subtract, ["tmpw"] + allu, [("xxT", k)])
            allx = [("xxT", k) for k in range(8)]
            for m in range(6):
                for k in range(8):
                    P.stt("dve" if (m + k) % 2 == 0 else "dve", xm[m][:, k, 0:nq], xxT[:, k, 0:nq], mixc[:, m, k:k + 1], uT[:, k, c0:c0 + nq],
                          ALU.mult, ALU.add, allx + allu + ["mixc"], [("xm", m, k)])
            xr_ = lambda m: [("xm", m, k) for k in range(8)]
            ph = psH[hc_ % 2]; phk = ("psH", hc_ % 2); hc_ += 1
            for k in range(8):
                P.mm(ph[:, 0:nq], W1[:, k, :], xm[1][:, k, 0:nq], k == 0, k == 7, xr_(1) + [("W1", k)], [phk])
            P.act(hW[:, 0:nq], ph[:, 0:nq], AF.Tanh, [phk], ["hW"])
            ph = psH[hc_ % 2]; phk = ("psH", hc_ % 2); hc_ += 1
            for k in range(8):
                P.mm(ph[:, 0:nq], A1[:, k, :], xm[4][:, k, 0:nq], k == 0, k == 7, xr_(4) + [("A1", k)], [phk])
            P.copy("dve", hA[:, 0:nq], ph[:, 0:nq], [phk], ["hA"])
            for (gi_, lo, hi) in ((0, 0, 128), (1, 128, 160)):
                ph = psH[hc_ % 2]; phk = ("psH", hc_ % 2); hc_ += 1
                for k in range(8):
                    P.mm(ph[0:hi - lo, 0:nq], G1[:, k, lo:hi], xm[5][:, k, 0:nq], k == 0, k == 7, xr_(5) + [("G1", k)], [phk])
                P.act(hG[0:hi - lo, gi_, 0:nq], ph[0:hi - lo, 0:nq], AF.Sigmoid, [phk], [("hG", gi_)])
            for jt in range(nt):
                r0 = g0 + jt * 128
                cs = slice(jt * 128, (jt + 1) * 128)
                for (m, Wm, wn, dstA) in ((0, Wr, "Wr", C.sR), (3, Wv, "Wv", C.sVv)):
                    pt, ptk = ps_tile()
                    for nh in range(2):
                        for k in range(8):
                            P.mm(pt[:, nh * 512:(nh + 1) * 512], xm[m][:, k, cs], Wm[:, k, nh * 512:(nh + 1) * 512], k == 0, k == 7,
                                 xr_(m) + [(wn, k)], [ptk])
                    o, ok = out_tile(0)
                    P.copy("act", o[:], pt[:], [ptk], [ok])
                    P.dma(dstA[r0:r0 + 128, :], o[:], [ok], [(wn, "out", r0)])
                pt, ptk = ps_tile()
                for nh in range(2):
                    for k in range(8):
                        P.mm(pt[:, nh * 512:(nh + 1) * 512], xm[2][:, k, cs], Wk[:, k, nh * 512:(nh + 1) * 512], k == 0, k == 7,
                             xr_(2) + [("Wk", k)], [ptk])
                o, ok = out_tile(0)
                P.copy("act", o[:], pt[:], [ptk], [ok])
                P.dma(C.sK[r0:r0 + 128, :], o[:], [ok], [("k", "out", r0)])
                o2, ok2 = out_tile(0)
                P.tt("dve", o2[:], o[:], kkt[:], ALU.mult, [ok, "kkt"], [ok2])
                P.tt("pool", sq[:], o2[:], o2[:], ALU.mult, [ok2], ["ctmp"])
                P.add("dve", lambda e, o_=s16[:, 0:16], i_=sq[:].rearrange("p (h n) -> p h n", n=64): e.tensor_reduce(out=o_, in_=i_, axis=AX.X, op=ALU.add), ["ctmp"], ["s16"])
                P.act(s16[:, 0:16], s16[:, 0:16], AF.Sqrt, ["s16"], ["s16"])
                P.ts("dve", s16[:, 0:16], s16[:, 0:16], 1e-12, None, ALU.max, None, ["s16"], ["s16"])
                P.recip(s16[:, 16:32], s16[:, 0:16], ["s16"], ["s16r"])
                for h in range(16):
                    P.ts("dve", o2[:, h * 64:(h + 1) * 64], o2[:, h * 64:(h + 1) * 64], s16[:, 16 + h:17 + h], None, ALU.mult, None,
                         [ok2, "s16r"], [ok2])
                P.dma(C.sKK[r0:r0 + 128, :], o2[:], [ok2], [("kk", "out", r0)])
                pt, ptk = ps_tile()
                for nh in range(2):
                    P.mm(pt[:, nh * 512:(nh + 1) * 512], hG[:, 0, cs], G2[:, 0, nh * 512:(nh + 1) * 512], True, False, [("hG", 0), ("hG", 1), ("G2", 0)], [ptk])
                    P.mm(pt[:, nh * 512:(nh + 1) * 512], hG[0:32, 1, cs], G2[0:32, 1, nh * 512:(nh + 1) * 512], False, True, [("hG", 1), ("G2", 1)], [ptk])
                o, ok = out_tile(0)
                P.copy("act", o[:], pt[:], [ptk], [ok])
                P.dma(C.sG[r0:r0 + 128, :], o[:], [ok], [("g", "out", r0)])
                for e in range(2):
                    pt, ptk = ps_tile()
                    for nh in range(2):
                        P.mm(pt[:, nh * 512:(nh + 1) * 512], hW[e * 64:(e + 1) * 64, cs], W2[e * 64:(e + 1) * 64, 0, nh * 512:(nh + 1) * 512], True, True,
                             ["hW", ("W2", 0)], [ptk])
                    o, ok = out_tile(0)
                    P.tt("dve", o[:], pt[:], w0t[e][:], ALU.add, [ptk, ("w0t", e)], [ok])
                    P.act(o[:], o[:], AF.Sigmoid, [ok], [ok])
                    P.ts("dve", o[:], o[:], -DECAY_C, None, ALU.mult, None, [ok], [ok])
                    P.dma(C.sLW[e, r0:r0 + 128, :], o[:], [ok], [("lw", e, r0)])
                    pt, ptk = ps_tile()
                    for nh in range(2):
                        P.mm(pt[:, nh * 512:(nh + 1) * 512], hA[e * 64:(e + 1) * 64, cs], A2[e * 64:(e + 1) * 64, 0, nh * 512:(nh + 1) * 512], True, True,
                             ["hA", ("A2", 0)], [ptk])
                    o, ok = out_tile(0)
                    P.tt("dve", o[:], pt[:], a0t[e][:], ALU.add, [ptk, ("a0t", e)], [ok])
                    P.act(o[:], o[:], AF.Sigmoid, [ok], [ok])
                    P.dma(C.sA[e, r0:r0 + 128, :], o[:], [ok], [("a", e, r0)])


class _Cut(Exception):
    pass


def rwkv_scan_phase(P, C, l, j, st_lim=NT, g_lim=4, stage=99):
    with P.phase():
        try:
            _rwkv_scan_body(P, C, l, j, st_lim, g_lim, stage)
        except _Cut:
            pass


def _rwkv_scan_body(P, C, l, j, st_lim, g_lim, stage):
    def cut(n):
        if stage == n:
            raise _Cut()
    NSLOT = 4
    identf = P.sb("identf", [128, 128], F32)
    ones = P.sb("onesf", [128, 128], F32)
    tri = [P.sb(f"tri{e}", [128, 128], F32) for e in range(2)]
    mSd = [P.sb(f"mSd{e}", [128, 512], F32) for e in range(2)]
    mSo = [P.sb(f"mSo{e}", [128, 512], F32) for e in range(2)]
    mS = [P.sb(f"mS{e}", [128, 512], F32) for e in range(2)]
    mI = [P.sb(f"mI{e}", [128, 512], F32) for e in range(2)]
    mTd = [P.sb(f"mTd{e}", [128, 512], F32) for e in range(2)]
    I4 = P.sb("I4", [128, 512], F32)
    P.dma(I4[:], C.rw_I4[:, :], [], ["I4"])
    P.dma(identf[:], C.ident_f[:, :], [], ["identf"])
    P.add("pool", lambda e_: e_.memset(ones[:], 1.0), [], ["ones"])
    for e in range(2):
        P.dma(tri[e][:], C.rw_tri[e], [], [("tri", e)])
        P.dma(mS[e][:], C.rw_mS[e], [], [("mS", e)])
        P.dma(mI[e][:], C.rw_mI[e], [], [("mI", e)], q="act")
        P.dma(mSd[e][:], C.rw_mSd[e], [], [("mSd", e)])
        P.dma(mSo[e][:], C.rw_mSo[e], [], [("mSo", e)], q="act")
        P.dma(mTd[e][:], C.rw_mTd[e], [], [("mTd", e)])
    kat = P.sb("kat", [128, D], F32)
    P.dma(kat[:], C.rwkv_k_a[j:j + 1, :].broadcast_to([128, D]), [], ["kat"])
    ST = [P.sb(f"ST{e}", [128, 8, 64], F32) for e in range(2)]
    for e in range(2):
        P.add("pool", lambda e_, t_=ST[e]: e_.memset(t_[:], 0.0), [], [("ST", e, hp) for hp in range(8)])
    tr_, tk_, tkk, tlw, ta = [P.sb(n, [128, D], F32) for n in ("tr_", "tk_", "tkk", "tlw", "ta")]
    tvs = [P.sb(f"tv{i}", [128, D], F32) for i in range(2)]
    tb, tkd, cumS, E = [P.sb(n, [128, D], F32) for n in ("tb", "tkd", "cumS", "E")]
    Be, Ke = [P.sb(n, [128, D], F32) for n in ("Be", "Ke")]
    FT = P.sb("FT", [128, 8, 4, 128], F32)
    Dg = P.sb("Dg", [128, 8, 128], F32)
    Yt = P.sb("Yt", [128, D], F32)

    class Slot:
        pass
    slots = []
    for si in range(NSLOT):
        S = Slot()
        S.i = si
        S.Gb = [P.sb(f"Gb{si}_{i}", [128, 4, 128], F32) for i in range(2)]
        S.Lb = [P.sb(f"Lb{si}_{i}", [128, 4, 128], F32) for i in range(2)]
        S.NTb = [P.sb(f"NTb{si}_{i}", [128, 4, 128], F32) for i in range(2)]
        S.Go, S.LakT, S.MrbT, S.MrkT = [P.sb(f"{n}{si}", [128, 4, 128], F32) for n in ("Go", "LakT", "MrbT", "MrkT")]
        S.Xs, S.Zs, S.Ws, S.Us = [P.sb(f"{n}{si}", [128, 4, 64], F32) for n in ("Xs", "Zs", "Ws", "Us")]
        slots.append(S)
    banks = [P.ps(f"bk{i}", [128, 512]) for i in range(8)]
    bc = [0]

    def bank():
        i = bc[0] % 8
        bc[0] += 1
        return banks[i], ("bank", i)

    fl = lambda t_: t_[:].rearrange("p s t -> p (s t)")

    def chain(e, hg, S, tv_, tvk):
        si = S.i
        T_ = lambda n: (n, si)
        gq, hp0 = hg % 2, (hg // 2) * 4
        heads = [2 * (hp0 + i) + gq for i in range(4)]
        hps = [hp0 + i for i in range(4)]
        ftk = [("FT", hp) for hp in hps]
        stk = [("ST", e, hp) for hp in hps]

        def ft(h, si_):
            q = h % 2
            return FT[q * 64:q * 64 + 64, h // 2, si_, :]

        def grp(a_si, b_si, outs):
            bk, bkk = bank()
            for i, h in enumerate(heads):
                P.mm(bk[:, i * 128:(i + 1) * 128], ft(h, a_si), ft(h, b_si), True, True, ftk, [bkk])
            for (dst, dk, mask, mk) in outs:
                P.tt("dve", fl(dst), bk[:, :], mask[:], ALU.mult, [mk], [bkk, dk])

        grp(2, 0, [(S.Gb[0], T_("Gb0"), mSd[e], ("mSd", e)), (S.Go, T_("Go"), mSo[e], ("mSo", e))])
        yield
        grp(0, 2, [(S.Lb[0], T_("Lb0"), mTd[e], ("mTd", e))])
        yield
        grp(3, 0, [(S.LakT, T_("LakT"), mS[e], ("mS", e))])
        yield
        grp(2, 1, [(S.MrbT, T_("MrbT"), mI[e], ("mI", e))])
        yield
        grp(3, 1, [(S.MrkT, T_("MrkT"), mI[e], ("mI", e))])
        yield
        bk, bkk = bank()
        for i, h in enumerate(heads):
            q, hp = h % 2, h // 2
            P.mm(bk[:, i * 64:(i + 1) * 64], ft(h, 0), ST[e][q * 64:q * 64 + 64, hp, :], True, False, ftk + stk, [bkk])
            P.mm(bk[:, i * 64:(i + 1) * 64], S.LakT[:, i, :], tv_[:, h * 64:(h + 1) * 64], False, True, [T_("LakT"), tvk], [bkk])
        P.copy("act", fl(S.Xs), bk[:, 0:256], [], [bkk, T_("Xs")])
        P.tt("pool", fl(S.NTb[0]), I4[:], fl(S.Gb[0]), ALU.add, ["I4", T_("Gb0")], [T_("NT0")])
        yield
        for k in range(1, 5):
            gp, lp = S.Gb[(k - 1) % 2], S.Lb[(k - 1) % 2]
            gpk, lpk = T_("Gb%d" % ((k - 1) % 2)), T_("Lb%d" % ((k - 1) % 2))
            gn, ln = S.Gb[k % 2], S.Lb[k % 2]
            gnk, lnk = T_("Gb%d" % (k % 2)), T_("Lb%d" % (k % 2))
            ntp, ntn = T_("NT%d" % ((k - 1) % 2)), T_("NT%d" % (k % 2))
            bl, blk = bank()
            for i in range(4):
                P.mm(bl[:, i * 128:(i + 1) * 128], gp[:, i, :], lp[:, i, :], True, True, [gpk, lpk], [blk])
            if k < 4:
                bg, bgk = bank()
                for i in range(4):
                    P.mm(bg[:, i * 128:(i + 1) * 128], lp[:, i, :], gp[:, i, :], True, True, [gpk, lpk], [bgk])
            P.copy("act", fl(ln), bl[:, :], [], [blk, lnk])
            if k < 4:
                P.copy("dve", fl(gn), bg[:, :], [], [bgk, gnk])
            yield
            bn, bnk = bank()
            for i in range(4):
                P.mm(bn[:, i * 128:(i + 1) * 128], ln[:, i, :], S.NTb[(k - 1) % 2][:, i, :], True, True, [lnk, ntp], [bnk])
            P.tt("dve", fl(S.NTb[k % 2]), fl(S.NTb[(k - 1) % 2]), bn[:, :], ALU.add, [ntp], [bnk, ntn])
            yield
        NT, NTk = S.NTb[0], T_("NT0")
        bk, bkk = bank()
        for i in range(4):
            P.mm(bk[:, i * 64:(i + 1) * 64], NT[:, i, :], S.Xs[:, i, :], True, True, [NTk, T_("Xs")], [bkk])
        P.copy("act", fl(S.Zs), bk[:, 0:256], [], [bkk, T_("Zs")])
        yield
        ucur, uk = S.Zs, T_("Zs")
        for it in range(3):
            bk, bkk = bank()
            for i in range(4):
                P.mm(bk[:, i * 64:(i + 1) * 64], S.Go[:, i, :], ucur[:, i, :], True, True, [T_("Go"), uk], [bkk])
            P.copy("act", fl(S.Ws), bk[:, 0:256], [], [bkk, T_("Ws")])
            yield
            bk, bkk = bank()
            for i in range(4):
                P.mm(bk[:, i * 64:(i + 1) * 64], NT[:, i, :], S.Ws[:, i, :], True, True, [NTk, T_("Ws")], [bkk])
            P.tt("dve", fl(S.Us), fl(S.Zs), bk[:, 0:256], ALU.add, [T_("Zs")], [bkk, T_("Us")])
            yield
            ucur, uk = S.Us, T_("Us")
        bk, bkk = bank()
        for i, h in enumerate(heads):
            q, hp = h % 2, h // 2
            P.mm(bk[:, i * 64:(i + 1) * 64], ft(h, 1), ST[e][q * 64:q * 64 + 64, hp, :], True, False, ftk + stk, [bkk])
            P.mm(bk[:, i * 64:(i + 1) * 64], S.MrbT[:, i, :], S.Us[:, i, :], False, False, [T_("MrbT"), T_("Us")], [bkk])
            P.mm(bk[:, i * 64:(i + 1) * 64], S.MrkT[:, i, :], tv_[:, h * 64:(h + 1) * 64], False, True, [T_("MrkT"), tvk], [bkk])
        P.copy("act", Yt[:].rearrange("p (a q n) -> p a q n", q=2, n=64)[:, hp0:hp0 + 4, gq, :],
               bk[:, 0:256].rearrange("p (s t) -> p s t", s=4), [], [bkk, ("Yt", hg)])
        yield
        bk, bkk = bank()
        for i, h in enumerate(heads):
            hp = h // 2
            cs = slice(hp * 128, (hp + 1) * 128)
            P.mm(bk[:, i * 64:(i + 1) * 64], Dg[:, hp, :], ST[e][:, hp, :], True, False, [("Dg", hp)] + stk, [bkk])
            P.mm(bk[:, i * 64:(i + 1) * 64], Be[:, cs], S.Us[:, i, :], False, False, ["Be", T_("Us")], [bkk])
            P.mm(bk[:, i * 64:(i + 1) * 64], Ke[:, cs], tv_[:, h * 64:(h + 1) * 64], False, True, ["Ke", tvk], [bkk])
        b4 = bk[:, 0:256].rearrange("p (s t) -> p s t", s=4)
        P.copy("dve", ST[e][gq * 64:gq * 64 + 64, hp0:hp0 + 4, :], b4[gq * 64:gq * 64 + 64, :, :], [], [bkk] + stk)

    order = {0: [16, 17] + list(range(16)), 1: [17, 16] + list(range(15, -1, -1))}
    cut(-1)
    iters = [(st, e) for st in range(st_lim) for e in range(2)]

    def issue_loads(n):
        st_, e_ = iters[n]
        r0_ = order[e_][st_] * 128
        for (tile_, src, nm) in ((tr_, C.sR, "tr"), (tk_, C.sK, "tk"), (tkk, C.sKK, "tkk"), (tlw, C.sLW[e_], "tlw"), (ta, C.sA[e_], "ta"),
                                 (tvs[n % 2], C.sVv, ("tv", n % 2))):
            P.dma(tile_[:], src[r0_:r0_ + 128, :], [], [nm])

    issue_loads(0)
    pending_store = None
    for n, (st, e) in enumerate(iters):
        if True:
            ti = order[e][st]
            r0 = ti * 128
            tv_, tvk = tvs[n % 2], ("tv", n % 2)
            P.tt("pool", tb[:], tkk[:], ta[:], ALU.mult, ["tkk", "ta"], ["tb"])
            P.tt("dve", tkd[:], ta[:], kat[:], ALU.mult, ["ta", "kat"], ["tkd"])
            P.tt("dve", tkd[:], tkd[:], kat[:], ALU.subtract, ["tkd", "kat"], ["tkd"])
            P.stt("dve", tkd[:], tkd[:], 1.0, tk_[:], ALU.add, ALU.mult, ["tkd", "tk"], ["tkd"])
            for nh in range(2):
                cs = slice(nh * 512, (nh + 1) * 512)
                bk, bkk = bank()
                P.mm(bk[:, :], tri[e][:], tlw[:, cs], True, True, [("tri", e), "tlw"], [bkk])
                P.copy("act", cumS[:, cs], bk[:, :], [], [bkk, ("cumS", nh)])
                bk, bkk = bank()
                P.mm(bk[:, :], ones[:], tlw[:, cs], True, True, ["ones", "tlw"], [bkk])
                P.act(E[:, cs], bk[:, :], AF.Exp, [], [bkk, ("Etot", nh)])
                P.tt("dve", Be[:, cs], bk[:, :], cumS[:, cs], ALU.subtract, [("cumS", nh)], [bkk, ("Be4", nh)])
            cS = [("cumS", 0), ("cumS", 1)]
            for hp in range(8):
                P.tt("pool", Dg[:, hp, :], identf[:], E[:, hp * 128:(hp + 1) * 128], ALU.mult, ["identf", ("Etot", hp // 4)], [("Dg", hp)])
            P.act(Be[:], Be[:], AF.Exp, [("Be4", 0), ("Be4", 1)], ["Be"])
            P.tt("pool", Ke[:], tkd[:], Be[:], ALU.mult, ["tkd", "Be"], ["Ke"])
            P.tt("dve", Be[:], tb[:], Be[:], ALU.mult, ["tb", "Be"], ["Be"])
            dgk = [("Dg", hp) for hp in range(8)]
            P.act(E[:], cumS[:], AF.Exp, cS + dgk, ["E"])
            P.tt("dve", tr_[:], tr_[:], E[:], ALU.mult, ["tr", "E"], ["tr"])
            P.tt("dve", E[:], cumS[:], tlw[:], ALU.subtract, cS + ["tlw", "tr"], ["E"])
            P.act(E[:], E[:], AF.Exp, ["E"], ["E"])
            P.stt("dve", tkk[:], tkk[:], -1.0, E[:], ALU.mult, ALU.mult, ["tkk", "E", "tb"], ["tkk"])
            P.act(E[:], cumS[:], AF.Exp, cS + ["tkk"], ["E"], scale=-1.0)
            P.tt("dve", tb[:], tb[:], E[:], ALU.mult, ["tb", "E", "Be"], ["tb"])
            P.tt("pool", tkd[:], tkd[:], E[:], ALU.mult, ["tkd", "E", "Ke"], ["tkd"])
            cut(4)
            for hp in range(8):
                bk, bkk = bank()
                for si, (srcT, sk) in enumerate(((tkk, "tkk"), (tr_, "tr"), (tb, "tb"), (tkd, "tkd"))):
                    P.mm(bk[:, si * 128:(si + 1) * 128], srcT[:, hp * 128:(hp + 1) * 128], identf[:], True, True, [sk, "identf"], [bkk])
                P.copy("act" if hp % 2 else "dve", FT[:, hp, :, :], bk[:, :].rearrange("p (s t) -> p s t", s=4), [], [bkk, ("FT", hp)])
            cut(5)
            if n + 1 < len(iters):
                issue_loads(n + 1)
            if pending_store is not None:
                pending_store()
                pending_store = None
            gens = [chain(e, hg, slots[hg % NSLOT], tv_, tvk) for hg in range(g_lim)]
            while gens:
                for g in list(gens):
                    try:
                        next(g)
                    except StopIteration:
                        gens.remove(g)
            pending_store = (lambda e=e, r0=r0, ti=ti: P.dma(C.sY[e, r0:r0 + 128, :], Yt[:], [("Yt", hg) for hg in range(4)], [("sY", e, ti)]))
    if pending_store is not None:
        pending_store()


def rwkv_out_phase(P, C, l, j):
    with P.phase():
        cb = alloc_common(P)
        stg = [P.sb(f"stg{i}", [128, 1024], F32) for i in range(2)]
        Wo = P.sb("Wo", [128, 8, 1024], BF16)
        cnt = [0]
        load_w_bf16(P, Wo, C.rwkv_w_o[j], stg, 8, 1024, "Wo", cnt)
        kat, rkt, lnw, lnb = [P.sb(n, [128, D], F32) for n in ("kat", "rkt", "lnw", "lnb")]
        P.dma(kat[:], C.rwkv_k_a[j:j + 1, :].broadcast_to([128, D]), [], ["kat"])
        P.dma(rkt[:], C.rwkv_r_k[j].rearrange("h n -> (h n)").rearrange("(o d) -> o d", o=1).broadcast_to([128, D]), [], ["rkt"])
        P.dma(lnw[:], C.rwkv_ln_w[j:j + 1, :].broadcast_to([128, D]), [], ["lnw"])
        P.dma(lnb[:], C.rwkv_ln_b[j:j + 1, :].broadcast_to([128, D]), [], ["lnb"])
        y0, y1, tr_, tk_, tv_, a0, a1, tg = [P.sb(n, [128, D], F32) for n in ("y0", "y1", "tr_", "tk_", "tv_", "a0", "a1", "tg")]
        s16 = P.sb("s16", [128, 64], F32)
        ob = P.sb("ob", [128, D], BF16)
        oT = P.sb("oT", [128, 8, 128], BF16)
        psY = P.ps("psY", [128, D])
        for (who, ntl, hb) in ((0, 16, C.hL), (1, 2, C.hC)):
            mod_tiles(P, cb, l, 1, who, 1.0)
            for ti in range(ntl):
                r0 = (ti + (16 if who == 1 else 0)) * 128
                for (tile_, src, nm, q) in ((y0, C.sY[0], "y0", "sp"), (y1, C.sY[1], "y1", "act"), (tr_, C.sR, "tr", "sp"), (tk_, C.sK, "tk", "act"),
                                            (tv_, C.sVv, "tv", "sp"), (a0, C.sA[0], "a0", "act"), (a1, C.sA[1], "a1", "sp"), (tg, C.sG, "tg", "act")):
                    P.dma(tile_[:], src[r0:r0 + 128, :], [], [nm], q=q)
                v3 = lambda t_: t_[:].rearrange("p (h n) -> p h n", n=64)
                P.tt("dve", y0[:], y0[:], y1[:], ALU.add, ["y0", "y1"], ["y0"])
                P.add("dve", lambda e, o_=s16[:, 0:16], i_=v3(y0): e.tensor_reduce(out=o_, in_=i_, axis=AX.X, op=ALU.add), ["y0"], ["mean"])
                P.ts("dve", s16[:, 0:16], s16[:, 0:16], 1.0 / 64, None, ALU.mult, None, ["mean"], ["mean"])
                for h in range(16):
                    P.ts("dve", y0[:, h * 64:(h + 1) * 64], y0[:, h * 64:(h + 1) * 64], s16[:, h:h + 1], None, ALU.subtract, None,
                         ["y0", "mean"], ["y0"])
                P.tt("pool", y1[:], y0[:], y0[:], ALU.mult, ["y0", "y1"], ["y1"])
                P.add("dve", lambda e, o_=s16[:, 16:32], i_=v3(y1): e.tensor_reduce(out=o_, in_=i_, axis=AX.X, op=ALU.add), ["y1"], ["var"])
                P.ts("dve", s16[:, 16:32], s16[:, 16:32], 1.0 / 64, 64e-5, ALU.mult, ALU.add, ["var"], ["var"])
                P.act(s16[:, 16:32], s16[:, 16:32], AF.Sqrt, ["var"], ["var"])
                P.recip(s16[:, 16:32], s16[:, 16:32], ["var"], ["var"])
                for h in range(16):
                    P.ts("dve", y0[:, h * 64:(h + 1) * 64], y0[:, h * 64:(h + 1) * 64], s16[:, 16 + h:17 + h], None, ALU.mult, None,
                         ["y0", "var"], ["y0"])
                P.tt("dve", y0[:], y0[:], lnw[:], ALU.mult, ["y0", "lnw"], ["y0"])
                P.tt("pool", y0[:], y0[:], lnb[:], ALU.add, ["y0", "lnb"], ["y0"])
                P.tt("dve", a0[:], a0[:], a1[:], ALU.add, ["a0", "a1"], ["a0"])
                P.stt("dve", a0[:], a0[:], -2.0, kat[:], ALU.add, ALU.mult, ["a0", "kat"], ["a0"])
                P.stt("dve", a0[:], a0[:], 2.0, tk_[:], ALU.add, ALU.mult, ["a0", "tk"], ["a0"])
                P.tt("pool", a0[:], a0[:], tr_[:], ALU.mult, ["a0", "tr"], ["a0"])
                P.tt("dve", a0[:], a0[:], rkt[:], ALU.mult, ["a0", "rkt"], ["a0"])
                P.add("dve", lambda e, o_=s16[:, 32:48], i_=v3(a0): e.tensor_reduce(out=o_, in_=i_, axis=AX.X, op=ALU.add), ["a0"], ["coef"])
                for h in range(16):
                    P.stt("dve", y0[:, h * 64:(h + 1) * 64], tv_[:, h * 64:(h + 1) * 64], s16[:, 32 + h:33 + h], y0[:, h * 64:(h + 1) * 64],
                          ALU.mult, ALU.add, ["tv", "coef", "y0"], ["y0"])
                P.tt("dve", ob[:], y0[:], tg[:], ALU.mult, ["y0", "tg"], ["ob"])
                for k in range(8):
                    P.tr(cb.psT[:, k, :], ob[:, k * 128:(k + 1) * 128], cb.ident[:], ["ob", "ident"], ["cpsT"])
                P.copy("act", oT[:], cb.psT[:], ["cpsT"], ["oT"])
                hap = hb[ti * 128:(ti + 1) * 128, :]
                x = cb.xb[cb.gi % 2]; xk = ("cxb", cb.gi % 2); cb.gi += 1
                P.dma(x[:], hap, [("h", who, ti)], [xk])
                for nh in range(2):
                    for k in range(8):
                        P.mm(psY[:, nh * 512:(nh + 1) * 512], oT[:, k, :], Wo[:, k, nh * 512:(nh + 1) * 512], k == 0, k == 7, ["oT", ("Wo", k)], ["psY"])
                epi_tile(P, cb, psY[:], ["psY"], x, xk, hap, ("h", who, ti))


def rwkv_phases(P, C, l, j, ctx_out):
    import os
    stop = int(os.environ.get("RW_STOP", "3"))
    rwkv_feat_phase(P, C, l, j)
    if stop >= 2:
        rwkv_scan_phase(P, C, l, j)
    if stop >= 3:
        rwkv_out_phase(P, C, l, j)
```

```python
import numpy as np
import ml_dtypes
from contextlib import ExitStack, contextmanager
import concourse.bass as bass
import concourse.mybir as mybir
from concourse.bass_utils import run_bass_kernel_spmd

F32 = mybir.dt.float32
BF16 = mybir.dt.bfloat16
AF = mybir.ActivationFunctionType
ALU = mybir.AluOpType
AX = mybir.AxisListType

D = 1024
T = 2048
TC = 256
NT = (T + TC) // 128
FF = 2816
NFC = FF // 128
DEPTH = 4
EPS = 1e-6


class Op:
    __slots__ = ("eng", "fn", "waits", "signal", "pos", "dma", "dslot", "dval", "know", "idx", "sigval")


class Prog:
    R = 14
    CE = ("pe", "dve", "act", "pool")
    QE = ("sp", "act", "pool")
    ALLE = ("pe", "dve", "act", "pool", "sp")

    def __init__(self, nc, stack):
        self.nc = nc
        self.sem = {e: stack.enter_context(nc.semaphore("s_" + e)) for e in self.CE}
        self.dsem = {q: [stack.enter_context(nc.semaphore(f"d_{q}{i}")) for i in range(self.R)] for q in self.QE}
        self.sigcount = {e: 0 for e in self.CE}
        self.dq = {q: {"n": 0, "hist": [None] * self.R, "val": [0] * self.R} for q in self.QE}
        self.ops = []
        self.tok = {}
        self.npos = {e: 0 for e in self.CE}
        self.know = {e: {} for e in self.ALLE}
        self.dknown = {e: set() for e in self.ALLE}
        self.phase_start = 0
        self.pstack = None
        self.nphase = 0
        self.same_engine_sync = True

    def sb(self, name, shape, dtype):
        return self.pstack.enter_context(self.nc.sbuf_tensor(f"{name}_{self.nphase}", list(shape), dtype))

    def ps(self, name, shape, dtype=F32):
        return self.pstack.enter_context(self.nc.psum_tensor(f"{name}_{self.nphase}", list(shape), dtype))

    @contextmanager
    def phase(self):
        with ExitStack() as st:
            self.pstack = st
            yield self
            self.flush()
            self.pstack = None
        self.nphase += 1

    def add(self, eng, fn, reads=(), writes=(), dma=False):
        op = Op()
        op.eng, op.fn, op.dma, op.signal, op.idx, op.sigval = eng, fn, dma, False, len(self.ops), None
        deps = set()
        for t in reads:
            s = self.tok.get(t)
            if s is not None and s[0] is not None:
                deps.add(s[0])
        for t in writes:
            s = self.tok.get(t)
            if s is not None:
                if s[0] is not None:
                    deps.add(s[0])
                deps.update(s[1])
        if dma:
            q = self.dq[eng]
            n = q["n"]
            q["n"] += 1
            slot = n % self.R
            op.dslot, op.dval = slot, 16 * (n // self.R + 1)
            q["val"][slot] = op.dval
            if q["hist"][slot] is not None:
                deps.add(q["hist"][slot])
            q["hist"][slot] = op.idx
        know, dk = self.know[eng], self.dknown[eng]
        waits = []
        for d in sorted(deps):
            if d < self.phase_start:
                continue
            Dp = self.ops[d]
            if Dp.dma:
                if d in dk:
                    continue
                dk.add(d)
                waits.append(d)
            else:
                if Dp.eng == eng and (eng == "pe" or not self.same_engine_sync):
                    continue
                if know.get(Dp.eng, -1) >= Dp.pos:
                    continue
                Dp.signal = True
                waits.append(d)
            for e2, p2 in Dp.know.items():
                if know.get(e2, -1) < p2:
                    know[e2] = p2
        op.waits = waits
        if dma:
            op.pos = None
            op.know = dict(know)
        else:
            op.pos = self.npos[eng]
            self.npos[eng] += 1
            op.know = dict(know)
            op.know[eng] = op.pos
        for t in reads:
            self.tok.setdefault(t, [None, []])[1].append(op.idx)
        for t in writes:
            self.tok[t] = [op.idx, []]
        self.ops.append(op)
        return op

    def flush(self):
        ops = self.ops[self.phase_start:]
        last = {}
        for op in ops:
            if not op.dma:
                last[op.eng] = op
        for op in last.values():
            op.signal = True
        for op in ops:
            if not op.dma and op.signal:
                self.sigcount[op.eng] += 1
                op.sigval = self.sigcount[op.eng]
        final_sig = dict(self.sigcount)
        final_d = {q: list(self.dq[q]["val"]) for q in self.QE}
        per = {e: [op for op in ops if op.eng == e] for e in self.ALLE}
        allops = self.ops
        sem, dsem = self.sem, self.dsem

        def mk(name):
            def body(e):
                for op in per[name]:
                    for d in op.waits:
                        Dp = allops[d]
                        if Dp.dma:
                            e.wait_ge(dsem[Dp.eng][Dp.dslot], Dp.dval)
                        else:
                            e.wait_ge(sem[Dp.eng], Dp.sigval)
                    ins = op.fn(e)
                    if op.dma:
                        ins.then_inc(dsem[op.eng][op.dslot], 16)
                    elif op.signal:
                        ins.then_inc(sem[op.eng], 1)
                for e2 in self.CE:
                    if final_sig[e2] > 0:
                        e.wait_ge(sem[e2], final_sig[e2])
                for q in self.QE:
                    for s in range(self.R):
                        if final_d[q][s] > 0:
                            e.wait_ge(dsem[q][s], final_d[q][s])
            return body

        with self.nc.Block() as blk:
            blk.tensor(mk("pe"))
            blk.vector(mk("dve"))
            blk.scalar(mk("act"))
            blk.gpsimd(mk("pool"))
            blk.sync(mk("sp"))
        self.phase_start = len(self.ops)
        for e in self.ALLE:
            self.know[e] = {c: self.npos[c] - 1 for c in self.CE}
            self.dknown[e] = set()

    def dma(self, out, in_, reads, writes, q="sp", **kw):
        return self.add(q, lambda e: e.dma_start(out=out, in_=in_, **kw), reads, writes, dma=True)

    def mm(self, out, lhsT, rhs, start, stop, reads, writes, r32=False):
        if r32 and lhsT.dtype == F32:
            lhsT = lhsT.bitcast(mybir.dt.float32r)
            rhs = rhs.bitcast(mybir.dt.float32r)
        return self.add("pe", lambda e: e.matmul(out, lhsT, rhs, start=start, stop=stop), reads, writes)

    def tr(self, out, in_, ident, reads, writes):
        if in_.dtype == F32:
            return self.add("pe", lambda e: e.matmul(out, in_, ident, start=True, stop=True), reads, writes)
        return self.add("pe", lambda e: e.transpose(out, in_, ident), reads, writes)

    def act(self, out, in_, func, reads, writes, **kw):
        return self.add("act", lambda e: e.activation(out=out, in_=in_, func=func, **kw), reads, writes)

    def tt(self, eng, out, in0, in1, op, reads, writes):
        return self.add(eng, lambda e: e.tensor_tensor(out=out, in0=in0, in1=in1, op=op), reads, writes)

    def ts(self, eng, out, in0, s1, s2, op0, op1, reads, writes):
        if s2 is None:
            return self.add(eng, lambda e: e.tensor_scalar(out=out, in0=in0, scalar1=s1, scalar2=None, op0=op0), reads, writes)
        return self.add(eng, lambda e: e.tensor_scalar(out=out, in0=in0, scalar1=s1, scalar2=s2, op0=op0, op1=op1), reads, writes)

    def stt(self, eng, out, in0, scalar, in1, op0, op1, reads, writes):
        return self.add(eng, lambda e: e.scalar_tensor_tensor(out=out, in0=in0, scalar=scalar, in1=in1, op0=op0, op1=op1), reads, writes)

    def recip(self, out, in_, reads, writes):
        return self.add("dve", lambda e: e.reciprocal(out=out, in_=in_), reads, writes)

    def copy(self, eng, out, in_, reads, writes):
        if eng == "act":
            return self.add("act", lambda e: e.activation(out=out, in_=in_, func=AF.Copy), reads, writes)
        return self.add(eng, lambda e: e.tensor_copy(out=out, in_=in_), reads, writes)


class Ctx:
    pass


def mod_phase(P, C):
    NB = 1152
    with P.phase():
        cc = P.sb("cc", [128, 8, 2], F32)
        sc = P.sb("sc", [128, 8, 2], F32)
        wb = [P.sb(f"mw{i}", [128, 8, NB], F32) for i in range(2)]
        bb = [P.sb(f"mb{i}", [2, NB], F32) for i in range(2)]
        rs = [P.sb(f"mr{i}", [2, NB], F32) for i in range(2)]
        pp = [P.ps(f"mp{i}", [2, 512]) for i in range(4)]
        P.dma(cc[:, :, 0], C.c.rearrange("o (k p) -> p (o k)", p=128), [], ["cc0"], allow_slow_non_contiguous=True)
        P.dma(cc[:, :, 1], C.c_ctx.rearrange("(k p) -> p k", p=128), [], ["cc1"], allow_slow_non_contiguous=True)
        P.act(sc[:], cc[:], AF.Silu, ["cc0", "cc1"], ["sc"])
        it = 0
        pi = 0
        for l in range(DEPTH):
            for cb in range(9216 // NB):
                w = wb[it % 2]
                wt = ("mw", it % 2)
                c0 = cb * NB
                P.dma(w[:], C.mod_w[l, :, c0:c0 + NB].rearrange("(k p) n -> p k n", p=128), [], [wt],
                      q=("sp" if it % 2 == 0 else "pool"))
                P.dma(bb[it % 2][:], C.mod_b[l:l + 1, c0:c0 + NB].broadcast_to([2, NB]), [], [("mb", it % 2)])
                for (o, n) in ((0, 512), (512, 512), (1024, 128)):
                    pt = pp[pi % 4]
                    ptk = ("mp", pi % 4)
                    for k in range(8):
                        P.mm(pt[:, 0:n], sc[:, k, :], w[:, k, o:o + n], k == 0, k == 7, ["sc", wt], [ptk])
                    P.tt("dve", rs[it % 2][:, o:o + n], pt[:, 0:n], bb[it % 2][:, o:o + n], ALU.add,
                         [ptk, ("mb", it % 2)], [("mr", it % 2, o)])
                    pi += 1
                P.dma(C.modv[l, :, c0:c0 + NB], rs[it % 2][:], [("mr", it % 2, 0), ("mr", it % 2, 512), ("mr", it % 2, 1024)],
                      [("modv", l, cb)])
                it += 1


def load_mod_tiles(P, C, l, sub, who, weight, A, Bt, G, tmp1, tmp2, key, k1, k2):
    base = sub * 3 * D
    def row(ap):
        return ap.broadcast_to([128, D])
    P.dma(Bt[:], row(C.modv[l, who:who + 1, base:base + D]), [], [("B", key)])
    P.dma(A[:], row(C.modv[l, who:who + 1, base + D:base + 2 * D]), [], [("A", key)])
    P.dma(G[:], row(C.modv[l, who:who + 1, base + 2 * D:base + 3 * D]), [], [("G", key)])
    P.dma(tmp1[:, 0:D], row(C.norm_pre[l, sub:sub + 1, :]), [], [k1])
    P.dma(tmp2[:, 0:D], row(C.norm_post[l, sub:sub + 1, :]), [], [k2])
    P.stt("dve", A[:], A[:], 1.0, tmp1[:, 0:D], ALU.add, ALU.mult, [("A", key), k1], [("A", key)])
    P.stt("dve", G[:], G[:], float(weight), tmp2[:, 0:D], ALU.mult, ALU.mult, [("G", key), k2], [("G", key)])


def rms_rstd(P, src, junk, ss, rstd, n, reads, key, eps=EPS):
    P.act(junk, src, AF.Square, reads + [], [("junk", key), ("ss", key)], accum_out=ss)
    P.ts("dve", rstd, ss, 1.0 / n, eps, ALU.mult, ALU.add, [("ss", key)], [("rstd", key)])
    P.act(rstd, rstd, AF.Sqrt, [("rstd", key)], [("rstd", key)])
    P.recip(rstd, rstd, [("rstd", key)], [("rstd", key)])


def ffn_phase(P, C, l, f, src, dst, ntiles_lat, do_ctx):
    sub = 0 if f == 0 else 2
    TB = 256
    with P.phase():
        Wg = P.sb("Wg", [128, 8, FF], BF16)
        Wu = P.sb("Wu", [128, 8, FF], BF16)
        Wd = P.sb("Wd", [128, NFC, D], BF16)
        stg = [P.sb(f"stg{i}", [128, 1024], F32) for i in range(2)]
        A = P.sb("A", [128, D], F32)
        Bt = P.sb("B", [128, D], F32)
        G = P.sb("G", [128, D], F32)
        xT = [P.sb(f"xT{i}", [128, 8, TB], BF16) for i in range(2)]
        hT = P.sb("hT", [128, NFC, TB], BF16)
        xb = [P.sb(f"xb{i}", [128, D], F32) for i in range(4)]
        xm = [P.sb(f"xm{i}", [128, D], BF16) for i in range(2)]
        tmp = P.sb("tmp", [128, D], F32)
        junk = P.sb("junk", [128, D], BF16)
        sg = [P.sb(f"sg{i}", [128, TB], F32) for i in range(2)]
        small = P.sb("small", [128, 16], F32)
        ident = P.sb("ident", [128, 128], BF16)
        psT = P.ps("psT", [128, 8, 128], BF16)
        psGU = [P.ps(f"psGU{i}", [128, 512]) for i in range(2)]
        psYs = [P.ps(f"psY{i}", [128, D]) for i in range(2)]
        P.dma(ident[:], C.ident_bf[:, :], [], ["ident"])
        cv = ["pool", "act", "dve"]
        qs = ["sp", "act", "sp"]
        n = 0
        NS = 2
        for (Wsrc, Wdst, nm) in ((C.ffn_w_gate[l, f], Wg, "Wg"), (C.ffn_w_up[l, f], Wu, "Wu")):
            for k in range(8):
                for (c0, cw) in ((0, 1024), (1024, 1024), (2048, 768)):
                    s_ = stg[n % NS]
                    P.dma(s_[:, 0:cw], Wsrc[k * 128:(k + 1) * 128, c0:c0 + cw], [], [("stg", n % NS)], q=qs[n % 3])
                    P.copy(cv[n % 3], Wdst[:, k, c0:c0 + cw], s_[:, 0:cw], [("stg", n % NS)], [(nm, k)])
                    n += 1
        for fc in range(NFC):
            s_ = stg[n % NS]
            P.dma(s_[:, 0:D], C.ffn_w_down[l, f, fc * 128:(fc + 1) * 128, :], [], [("stg", n % NS)], q=qs[n % 3])
            P.copy(cv[n % 3], Wd[:, fc, :], s_[:, 0:D], [("stg", n % NS)], [("Wd", fc)])
            n += 1
        Wg_r = [("Wg", k) for k in range(8)]
        Wu_r = [("Wu", k) for k in range(8)]
        blocks = [(0, ti) for ti in range(0, ntiles_lat, 2)]
        if do_ctx:
            blocks.append((1, 0))
        st_ = {"gi": 0, "who": None}
        tiles_of = {}

        def prologue(bi):
            who, t0 = blocks[bi]
            if who != st_["who"]:
                load_mod_tiles(P, C, l, sub, who, 0.5, A, Bt, G, stg[0], stg[1], "m", ("stg", 0), ("stg", 1))
                st_["who"] = who
            xTb = xT[bi % 2]
            xtk = ("xT", bi % 2)
            tiles = []
            for j in range(2):
                ti = t0 + j
                x = xb[st_["gi"] % 4]
                xk = ("xb", st_["gi"] % 4)
                st_["gi"] += 1
                tiles.append((ti, x, xk))
                P.dma(x[:], src(who, ti), [("h", who, ti)], [xk])
                ss, rstd = small[:, 0:1], small[:, 1:2]
                rms_rstd(P, x[:], junk[:], ss, rstd, D, [xk], "pre")
                P.stt("dve", tmp[:], x[:], rstd, A[:], ALU.mult, ALU.mult, [xk, ("rstd", "pre"), ("A", "m")], ["tmp"])
                m = xm[j]
                P.tt("dve", m[:], tmp[:], Bt[:], ALU.add, ["tmp", ("B", "m")], [("xm", j)])
                for k in range(8):
                    P.tr(psT[:, k, :], m[:, k * 128:(k + 1) * 128], ident[:], [("xm", j), "ident"], ["psT"])
                P.copy("act", xTb[:, :, j * 128:(j + 1) * 128], psT[:], ["psT"], [(xtk, j)])
            tiles_of[bi] = tiles

        prologue(0)
        for bi, (who, t0) in enumerate(blocks):
            xTb = xT[bi % 2]
            xtk = ("xT", bi % 2)
            tiles = tiles_of[bi]
            xr = [(xtk, 0), (xtk, 1)]
            nxt_same = bi + 1 < len(blocks) and blocks[bi + 1][0] == who
            for fc in range(NFC):
                if fc == 5 and nxt_same:
                    prologue(bi + 1)
                pgu, pk = psGU[fc % 2], ("psGU", fc % 2)
                for k in range(8):
                    P.mm(pgu[:, 0:TB], Wg[:, k, fc * 128:(fc + 1) * 128], xTb[:, k, :], k == 0, k == 7,
                         [("Wg", k)] + xr, [pk])
                for k in range(8):
                    P.mm(pgu[:, TB:2 * TB], Wu[:, k, fc * 128:(fc + 1) * 128], xTb[:, k, :], k == 0, k == 7,
                         [("Wu", k)] + xr, [pk])
                P.act(sg[fc % 2][:], pgu[:, 0:TB], AF.Silu, [], [pk, ("sg", fc % 2)])
                P.tt("dve", hT[:, fc, :], sg[fc % 2][:], pgu[:, TB:2 * TB], ALU.mult, [("sg", fc % 2)], [pk, ("hT", fc)])
            for j, (ti, x, xk) in enumerate(tiles):
                psY, pyk = psYs[j], ("psY", j)
                for nh in range(2):
                    for fc in range(NFC):
                        P.mm(psY[:, nh * 512:(nh + 1) * 512], hT[:, fc, j * 128:(j + 1) * 128],
                             Wd[:, fc, nh * 512:(nh + 1) * 512], fc == 0, fc == NFC - 1,
                             [("hT", fc), ("Wd", fc)], [pyk])
                ss, rstd = small[:, 2 + 2 * j:3 + 2 * j], small[:, 3 + 2 * j:4 + 2 * j]
                rms_rstd(P, psY[:], junk[:], ss, rstd, D, [pyk], ("post", j))
                P.stt("dve", tmp[:], psY[:], rstd, G[:], ALU.mult, ALU.mult, [pyk, ("rstd", ("post", j)), ("G", "m")], ["tmp"])
                P.tt("pool", x[:], tmp[:], x[:], ALU.add, ["tmp", xk], [xk])
                P.dma(dst(who, ti), x[:], [xk], [("h", who, ti)])
            if bi + 1 < len(blocks) and not nxt_same:
                prologue(bi + 1)


class CB:
    pass


def alloc_common(P, nxb=2):
    cb = CB()
    cb.A = P.sb("cA", [128, D], F32)
    cb.Bt = P.sb("cB", [128, D], F32)
    cb.G = P.sb("cG", [128, D], F32)
    cb.t1 = P.sb("ct1", [128, D], F32)
    cb.t2 = P.sb("ct2", [128, D], F32)
    cb.xb = [P.sb(f"cxb{i}", [128, D], F32) for i in range(nxb)]
    cb.xm = [P.sb(f"cxm{i}", [128, D], BF16) for i in range(2)]
    cb.tmp = P.sb("ctmp", [128, D], F32)
    cb.junk = P.sb("cjunk", [128, D], BF16)
    cb.small = P.sb("csmall", [128, 16], F32)
    cb.ident = P.sb("cident", [128, 128], BF16)
    cb.psT = P.ps("cpsT", [128, 8, 128], BF16)
    cb.gi = 0
    cb.mi = 0
    P.dma(cb.ident[:], P.C.ident_bf[:, :], [], ["ident"])
    return cb


def mod_tiles(P, cb, l, sub, who, weight):
    load_mod_tiles(P, P.C, l, sub, who, weight, cb.A, cb.Bt, cb.G, cb.t1, cb.t2, "m", "ct1", "ct2")


def pro_tile(P, cb, src_ap, htok):
    x = cb.xb[cb.gi % len(cb.xb)]
    xk = ("cxb", cb.gi % len(cb.xb))
    cb.gi += 1
    P.dma(x[:], src_ap, [htok], [xk])
    ss, rstd = cb.small[:, 0:1], cb.small[:, 1:2]
    rms_rstd(P, x[:], cb.junk[:], ss, rstd, D, [xk], "pre")
    P.stt("dve", cb.tmp[:], x[:], rstd, cb.A[:], ALU.mult, ALU.mult, [xk, ("rstd", "pre"), ("A", "m")], ["ctmp"])
    j = cb.mi % 2
    cb.mi += 1
    m = cb.xm[j]
    P.tt("pool", m[:], cb.tmp[:], cb.Bt[:], ALU.add, ["ctmp", ("B", "m")], [("cxm", j)])
    for k in range(8):
        P.tr(cb.psT[:, k, :], m[:, k * 128:(k + 1) * 128], cb.ident[:], [("cxm", j), "ident"], ["cpsT"])
    return x, xk


def epi_tile(P, cb, psY, pkeys, x, xk, dst_ap, htok, bias=None):
    ss, rstd = cb.small[:, 2:3], cb.small[:, 3:4]
    src = psY
    sk = list(pkeys)
    if bias is not None:
        P.tt("dve", cb.t1[:], psY, bias[0], ALU.add, sk + [bias[1]], ["ct1"])
        src = cb.t1[:]
        sk = ["ct1"]
    rms_rstd(P, src, cb.junk[:], ss, rstd, D, sk, "post")
    P.stt("dve", cb.tmp[:], src, rstd, cb.G[:], ALU.mult, ALU.mult, sk + [("rstd", "post"), ("G", "m")], ["ctmp"])
    P.tt("pool", x[:], cb.tmp[:], x[:], ALU.add, ["ctmp", xk], [xk])
    P.dma(dst_ap, x[:], [xk], [htok])


def load_w_bf16(P, dst, src_ap, stg, nk, ncols, name, cnt):
    cv = ["pool", "act", "dve"]
    for k in range(nk):
        n = cnt[0]
        s = stg[n % 2]
        P.dma(s[:, 0:ncols], src_ap[k * 128:(k + 1) * 128, :], [], [("cstg", n % 2)], q=("sp" if n % 2 == 0 else "act"))
        P.copy(cv[n % 3], dst[:, k, 0:ncols], s[:, 0:ncols], [("cstg", n % 2)], [(name, k)])
        cnt[0] += 1


MLA_SCALE = 192.0 ** -0.5


def mla_proj_phase(P, C, l, j, ctx_q):
    with P.phase():
        cb = alloc_common(P)
        stg = [P.sb(f"stg{i}", [128, 1024], F32) for i in range(2)]
        Wdq = P.sb("Wdq", [128, 8, 512], BF16)
        Wdc = P.sb("Wdc", [128, 8, 256], BF16)
        Wdr = P.sb("Wdr", [128, 8, 128], BF16)
        Wdrp = P.sb("Wdrp", [128, 8, 128], BF16)
        Wqn = P.sb("Wqn", [128, 4, 1024], BF16)
        Wqr = P.sb("Wqr", [128, 4, 512], BF16)
        Wqrp = P.sb("Wqrp", [128, 4, 512], BF16)
        Wk = P.sb("Wk", [128, 2, 1024], BF16)
        Wv = P.sb("Wv", [128, 2, 1024], BF16)
        cnt = [0]
        load_w_bf16(P, Wdq, C.mla_w_dq[j], stg, 8, 512, "Wdq", cnt)
        load_w_bf16(P, Wdc, C.mla_dkv_c[j], stg, 8, 256, "Wdc", cnt)
        load_w_bf16(P, Wdr, C.mla_dkv_r[j], stg, 8, 128, "Wdr", cnt)
        load_w_bf16(P, Wdrp, C.mla_dkv_rp[j], stg, 8, 128, "Wdrp", cnt)
        load_w_bf16(P, Wqn, C.mla_uq_n[j], stg, 4, 1024, "Wqn", cnt)
        load_w_bf16(P, Wqr, C.mla_uq_r[j], stg, 4, 512, "Wqr", cnt)
        load_w_bf16(P, Wqrp, C.mla_uq_rp[j], stg, 4, 512, "Wqrp", cnt)
        load_w_bf16(P, Wk, C.mla_ukv_k[j], stg, 2, 1024, "Wk", cnt)
        load_w_bf16(P, Wv, C.mla_ukv_v[j], stg, 2, 1024, "Wv", cnt)
        gq = P.sb("gq", [128, 512], F32)
        gkv = P.sb("gkv", [128, 256], F32)
        P.dma(gq[:], C.mla_q_norm[j:j + 1, :].broadcast_to([128, 512]), [], ["gq"])
        P.dma(gkv[:], C.mla_kv_norm[j:j + 1, :].broadcast_to([128, 256]), [], ["gkv"])
        cos2 = P.sb("cos2", [128, T], F32)
        sin2 = P.sb("sin2", [128, T], F32)
        P.dma(cos2[:], C.rope_cos[:, :], [], ["cos2"])
        P.dma(sin2[:], C.rope_sin[:, :], [], ["sin2"], q="act")
        uT = P.sb("uT", [128, 8, 512], BF16)
        cqnT = P.sb("cqnT", [128, 4, 512], BF16)
        ckvnT = P.sb("ckvnT", [128, 2, 512], BF16)
        cqn = P.sb("cqn", [128, 512], BF16)
        ckvn = P.sb("ckvn", [128, 256], BF16)
        Vt = [P.sb(f"Vt{i}", [128, 1024], BF16) for i in range(2)]
        QTb = P.sb("QTb", [128, 8, 512], BF16)
        KTb = P.sb("KTb", [128, 8, 512], BF16)
        QrTb = P.sb("QrTb", [128, 4, 512], BF16)
        KrTb = P.sb("KrTb", [128, 512], BF16)
        r1 = P.sb("r1", [128, 512], F32)
        r2 = P.sb("r2", [128, 512], F32)
        sm2 = P.sb("sm2", [128, 8], F32)
        psq = P.ps("psq", [128, 512])
        pskv = P.ps("pskv", [128, 512])
        psT2 = P.ps("psT2", [128, 8, 128], BF16)
        psV = P.ps("psV", [128, 1024])
        psB = [P.ps(f"psB{i}", [128, 512]) for i in range(2)]
        blocks = [(0, t0, 4) for t0 in range(0, 16, 4)] + [(1, 0, 2)]
        cur = None
        bctr = 0
        for (who, t0, nt) in blocks:
            if who != cur:
                mod_tiles(P, cb, l, 1, who, 1.0)
                cur = who
            nq = nt * 128
            c0 = t0 * 128 + (T if who == 1 else 0)
            for jt in range(nt):
                ti = t0 + jt
                gt = ti + (16 if who == 1 else 0)
                src = (C.hL if who == 0 else C.hC)[ti * 128:(ti + 1) * 128, :]
                pro_tile(P, cb, src, ("h", who, ti))
                P.copy("act", uT[:, :, jt * 128:(jt + 1) * 128], cb.psT[:], ["cpsT"], [("uT", jt)])
                for k in range(8):
                    P.mm(psq[:, :], uT[:, k, jt * 128:(jt + 1) * 128], Wdq[:, k, :], k == 0, k == 7, [("uT", jt), ("Wdq", k)], ["psq"])
                for k in range(8):
                    P.mm(pskv[:, 0:256], uT[:, k, jt * 128:(jt + 1) * 128], Wdc[:, k, :], k == 0, k == 7, [("uT", jt), ("Wdc", k)], ["pskv"])
                P.act(cb.junk[:, 0:512], psq[:, :], AF.Square, ["psq"], [("junk", "q"), "ssq"], accum_out=sm2[:, 0:1])
                P.ts("dve", sm2[:, 1:2], sm2[:, 0:1], 1.0 / 512, EPS, ALU.mult, ALU.add, ["ssq"], ["rq"])
                P.act(sm2[:, 1:2], sm2[:, 1:2], AF.Sqrt, ["rq"], ["rq"])
                P.recip(sm2[:, 1:2], sm2[:, 1:2], ["rq"], ["rq"])
                P.stt("dve", cqn[:], psq[:, :], sm2[:, 1:2], gq[:], ALU.mult, ALU.mult, ["psq", "rq", "gq"], ["cqn"])
                P.act(cb.junk[:, 512:768], pskv[:, 0:256], AF.Square, ["pskv"], [("junk", "kv"), "sskv"], accum_out=sm2[:, 2:3])
                P.ts("dve", sm2[:, 3:4], sm2[:, 2:3], 1.0 / 256, EPS, ALU.mult, ALU.add, ["sskv"], ["rkv"])
                P.act(sm2[:, 3:4], sm2[:, 3:4], AF.Sqrt, ["rkv"], ["rkv"])
                P.recip(sm2[:, 3:4], sm2[:, 3:4], ["rkv"], ["rkv"])
                P.stt("dve", ckvn[:], pskv[:, 0:256], sm2[:, 3:4], gkv[:], ALU.mult, ALU.mult, ["pskv", "rkv", "gkv"], ["ckvn"])
                for r in range(4):
                    P.tr(psT2[:, r, :], cqn[:, r * 128:(r + 1) * 128], cb.ident[:], ["cqn", "ident"], ["psT2"])
                for r in range(2):
                    P.tr(psT2[:, 4 + r, :], ckvn[:, r * 128:(r + 1) * 128], cb.ident[:], ["ckvn", "ident"], ["psT2"])
                P.copy("act", cqnT[:, :, jt * 128:(jt + 1) * 128], psT2[:, 0:4, :], [], ["psT2", ("cqnT", jt)])
                P.copy("act", ckvnT[:, :, jt * 128:(jt + 1) * 128], psT2[:, 4:6, :], [], ["psT2", ("ckvnT", jt)])
                for nh in range(2):
                    for r in range(2):
                        P.mm(psV[:, nh * 512:(nh + 1) * 512], ckvnT[:, r, jt * 128:(jt + 1) * 128], Wv[:, r, nh * 512:(nh + 1) * 512],
                             r == 0, r == 1, [("ckvnT", jt), ("Wv", r)], ["psV"])
                vt = Vt[gt % 2]
                P.copy("act", vt[:], psV[:], ["psV"], [("Vt", gt % 2)])
                P.dma(C.sV[:, gt, :], vt[:], [("Vt", gt % 2)], [("sV", gt)])
            uTr = [("uT", jt) for jt in range(nt)]
            cqr = [("cqnT", jt) for jt in range(nt)]
            ckr = [("ckvnT", jt) for jt in range(nt)]
            for h in range(8):
                pb = psB[bctr % 2]; pk = ("psB", bctr % 2); bctr += 1
                for r in range(4):
                    P.mm(pb[:, 0:nq], Wqn[:, r, h * 128:(h + 1) * 128], cqnT[:, r, 0:nq], r == 0, r == 3, cqr + [("Wqn", r)], [pk])
                P.copy("act" if h % 2 == 0 else "dve", QTb[:, h, 0:nq], pb[:, 0:nq], [pk], [("QTb", h)])
                pb = psB[bctr % 2]; pk = ("psB", bctr % 2); bctr += 1
                for r in range(2):
                    P.mm(pb[:, 0:nq], Wk[:, r, h * 128:(h + 1) * 128], ckvnT[:, r, 0:nq], r == 0, r == 1, ckr + [("Wk", r)], [pk])
                P.copy("dve" if h % 2 == 0 else "act", KTb[:, h, 0:nq], pb[:, 0:nq], [pk], [("KTb", h)])
            jobs = [(Wqr, Wqrp, 4, hp, cqnT, cqr, "Wqr", "Wqrp", QrTb[:, hp, 0:nq], ("QrTb", hp)) for hp in range(4)]
            jobs.append((Wdr, Wdrp, 8, 0, uT, uTr, "Wdr", "Wdrp", KrTb[:, 0:nq], ("KrTb", 0)))
            for (Wa, Wb, nk, hp, rhsT, rk, na, nb_, dst, dk) in jobs:
                pa = psB[bctr % 2]; pka = ("psB", bctr % 2); bctr += 1
                for r in range(nk):
                    P.mm(pa[:, 0:nq], Wa[:, r, hp * 128:(hp + 1) * 128], rhsT[:, r, 0:nq], r == 0, r == nk - 1, rk + [(na, r)], [pka])
                if who == 1:
                    P.copy("act", dst, pa[:, 0:nq], [pka], [dk])
                    continue
                P.tt("dve", r1[:, 0:nq], pa[:, 0:nq], cos2[:, c0:c0 + nq], ALU.mult, [pka, "cos2"], ["r1"])
                pb = psB[bctr % 2]; pkb = ("psB", bctr % 2); bctr += 1
                for r in range(nk):
                    P.mm(pb[:, 0:nq], Wb[:, r, hp * 128:(hp + 1) * 128], rhsT[:, r, 0:nq], r == 0, r == nk - 1, rk + [(nb_, r)], [pkb])
                P.tt("dve", r2[:, 0:nq], pb[:, 0:nq], sin2[:, c0:c0 + nq], ALU.mult, [pkb, "sin2"], ["r2"])
                P.tt("pool", dst, r1[:, 0:nq], r2[:, 0:nq], ALU.add, ["r1", "r2"], [dk])
            P.dma(C.sQT[:, :, c0:c0 + nq], QTb[:, :, 0:nq], [("QTb", h) for h in range(8)], [("sQT", c0)])
            P.dma(C.sKT[:, :, c0:c0 + nq], KTb[:, :, 0:nq], [("KTb", h) for h in range(8)], [("sKT", c0)], q="act")
            P.dma(C.sQrT[:, :, c0:c0 + nq], QrTb[:, :, 0:nq], [("QrTb", hp) for hp in range(4)], [("sQrT", c0)])
            P.dma(C.sKrT[:, c0:c0 + nq], KrTb[:, 0:nq], [("KrTb", 0)], [("sKrT", c0)], q="act")


def mla_attn_phase(P, C, l, j, ctx_q):
    with P.phase():
        cb = alloc_common(P)
        stg = [P.sb(f"stg{i}", [128, 1024], F32) for i in range(2)]
        Wo = P.sb("Wo", [128, 8, 1024], BF16)
        cnt = [0]
        load_w_bf16(P, Wo, C.mla_w_o[j], stg, 8, 1024, "Wo", cnt)
        KT = P.sb("KT", [128, 8, T + TC], BF16)
        KrT = P.sb("KrT", [128, T + TC], BF16)
        V = P.sb("V", [128, NT, 1024], BF16)
        P.dma(KT[:], C.sKT[:, :, :], [], ["KT"])
        P.dma(KrT[:], C.sKrT[:, :], [], ["KrT"], q="act")
        P.dma(V[:, 0:9, :], C.sV[:, 0:9, :], [], ["V0"])
        P.dma(V[:, 9:18, :], C.sV[:, 9:18, :], [], ["V1"], q="act")
        ones = P.sb("ones", [128, 128], BF16)
        P.add("pool", lambda e: e.memset(ones[:], 1.0), [], ["ones"])
        QTb = [P.sb(f"QTb{i}", [128, 8, 512], BF16) for i in range(2)]
        QrTb = [P.sb(f"QrTb{i}", [128, 4, 512], BF16) for i in range(2)]
        OTb = P.sb("OTb", [128, 8, 512], BF16)
        PT = [P.sb(f"PT{i}", [128, 512], BF16) for i in range(4)]
        rz = P.sb("rz", [128, 512], F32)
        psS = [P.ps(f"psS{i}", [128, 512]) for i in range(3)]
        psO = P.ps("psO", [128, 512])
        psZ = P.ps("psZ", [128, 512])
        psY = P.ps("psY", [128, 1024])
        blocks = [(0, t0, 4, list(range(NT))) for t0 in range(0, 16, 4)]
        if ctx_q:
            blocks.append((1, 0, 2, [16, 17]))
        cur = None
        sc_ = 0
        pc_ = 0
        for bi, (who, t0, nt, kcs) in enumerate(blocks):
            if who != cur:
                mod_tiles(P, cb, l, 1, who, 1.0)
                cur = who
            nq = nt * 128
            c0 = t0 * 128 + (T if who == 1 else 0)
            qt, qr = QTb[bi % 2], QrTb[bi % 2]
            qk, qrk = ("QTb", bi % 2), ("QrTb", bi % 2)
            P.dma(qt[:, :, 0:nq], C.sQT[:, :, c0:c0 + nq], [], [qk])
            P.dma(qr[:, :, 0:nq], C.sQrT[:, :, c0:c0 + nq], [], [qrk], q="act")
            for h in range(8):
                pbase = (h % 2) * 64
                pend = []
                n = len(kcs)
                SK = 2
                for i in range(n + SK):
                    if i < n:
                        kc = kcs[i]
                        ps_ = psS[sc_ % 3]; psk = ("psS", sc_ % 3); sc_ += 1
                        P.mm(ps_[:, 0:nq], KT[:, h, kc * 128:(kc + 1) * 128], qt[:, h, 0:nq], True, False, ["KT", qk], [psk])
                        P.mm(ps_[:, 0:nq], KrT[pbase:pbase + 64, kc * 128:(kc + 1) * 128], qr[pbase:pbase + 64, h // 2, 0:nq],
                             False, True, ["KrT", qrk], [psk])
                        pt = PT[pc_ % 4]; ptk = ("PT", pc_ % 4); pc_ += 1
                        P.act(pt[:, 0:nq], ps_[:, 0:nq], AF.Exp, [psk], [ptk], scale=MLA_SCALE)
                        pend.append((kc, pt, ptk))
                    if i >= SK:
                        kc, pt, ptk = pend[i - SK]
                        P.mm(psO[:, 0:nq], V[:, kc, h * 128:(h + 1) * 128], pt[:, 0:nq], i == SK, i == n + SK - 1, ["V0", "V1", ptk], ["psO"])
                        P.mm(psZ[:, 0:nq], ones[:], pt[:, 0:nq], i == SK, i == n + SK - 1, ["ones", ptk], ["psZ"])
                P.recip(rz[:, 0:nq], psZ[:, 0:nq], ["psZ"], ["rz"])
                P.tt("dve", OTb[:, h, 0:nq], psO[:, 0:nq], rz[:, 0:nq], ALU.mult, ["psO", "rz"], [("OTb", h)])
            for jt in range(nt):
                ti = t0 + jt
                hap = (C.hL if who == 0 else C.hC)[ti * 128:(ti + 1) * 128, :]
                x = cb.xb[cb.gi % 2]; xk = ("cxb", cb.gi % 2); cb.gi += 1
                P.dma(x[:], hap, [("h", who, ti)], [xk])
                for nh in range(2):
                    for h in range(8):
                        P.mm(psY[:, nh * 512:(nh + 1) * 512], OTb[:, h, jt * 128:(jt + 1) * 128], Wo[:, h, nh * 512:(nh + 1) * 512],
                             h == 0, h == 7, [("OTb", h), ("Wo", h)], ["psY"])
                epi_tile(P, cb, psY[:], ["psY"], x, xk, hap, ("h", who, ti))


def _perm64():
    p = np.arange(64)
    return np.where((p % 32) < 16, p + 16, p - 16)


def host_layout(inputs):
    o = {}
    perm = _perm64()
    wdkv = inputs["mla_w_dkv"]
    o["mla_dkv_c"] = np.ascontiguousarray(wdkv[:, :, :256])
    r = wdkv[:, :, 256:320]
    o["mla_dkv_r"] = np.ascontiguousarray(np.concatenate([r, r], axis=-1))
    rp = r[:, :, perm]
    o["mla_dkv_rp"] = np.ascontiguousarray(np.concatenate([rp, rp], axis=-1))
    wuq = inputs["mla_w_uq"].reshape(-1, 512, 8, 192)
    o["mla_uq_n"] = np.ascontiguousarray(wuq[:, :, :, :128].reshape(-1, 512, 1024))
    o["mla_uq_r"] = np.ascontiguousarray(wuq[:, :, :, 128:].reshape(-1, 512, 512))
    o["mla_uq_rp"] = np.ascontiguousarray(wuq[:, :, :, 128:][:, :, :, perm].reshape(-1, 512, 512))
    wukv = inputs["mla_w_ukv"].reshape(-1, 256, 8, 256)
    o["mla_ukv_k"] = np.ascontiguousarray(wukv[:, :, :, :128].reshape(-1, 256, 1024))
    o["mla_ukv_v"] = np.ascontiguousarray(wukv[:, :, :, 128:].reshape(-1, 256, 1024))
    for k in ("mla_w_dq", "mla_q_norm", "mla_kv_norm", "mla_w_o", "c_ctx", "mod_w", "mod_b", "norm_pre", "norm_post",
              "ffn_w_gate", "ffn_w_up", "ffn_w_down"):
        o[k] = np.ascontiguousarray(inputs[k])
    o["ident_bf"] = np.eye(128, dtype=np.float32).astype(ml_dtypes.bfloat16)
    t = np.arange(T)
    p = np.arange(64)
    axis, half, pair = p // 32, (p % 32) // 16, p % 16
    inv = (np.float32(10000.0) ** (-(pair.astype(np.float32)) / np.float32(16.0))).astype(np.float32)
    pos = np.where(axis[:, None] == 0, (t // 64)[None, :], (t % 64)[None, :]).astype(np.float32)
    ang = (pos * inv[:, None]).astype(np.float32)
    cs = np.cos(ang).astype(np.float32)
    sn = (np.sin(ang) * np.where(half == 0, -1.0, 1.0)[:, None]).astype(np.float32)
    def dft(n):
        k = (np.arange(n)[:, None] * np.arange(n)[None, :]) % n
        a = 2.0 * np.pi * k.astype(np.float64) / n
        return np.cos(a), np.sin(a)
    bf = ml_dtypes.bfloat16
    c_, s_ = dft(T)
    o["dft_ct"] = c_.astype(np.float32).astype(bf); o["dft_st"] = s_.astype(np.float32).astype(bf)
    c_, s_ = dft(TC)
    o["dft_ct_c"] = c_.astype(np.float32).astype(bf); o["dft_st_c"] = s_.astype(np.float32).astype(bf)
    c_, s_ = dft(128)
    o["dft_cc"] = c_.astype(np.float32).astype(bf); o["dft_scn"] = (-s_).astype(np.float32).astype(bf)
    for k in ("fnet_w_o", "fnet_b_o", "rwkv_mix", "rwkv_w_r", "rwkv_w_k", "rwkv_w_v", "rwkv_w0", "rwkv_w2", "rwkv_a0", "rwkv_a2",
              "rwkv_g1", "rwkv_g2", "rwkv_k_k", "rwkv_k_a", "rwkv_r_k", "rwkv_ln_w", "rwkv_ln_b", "rwkv_w_o"):
        o[k] = np.ascontiguousarray(inputs[k])
    o["rwkv_w1c"] = np.ascontiguousarray(np.concatenate([inputs["rwkv_w1"][:, 0], inputs["rwkv_w1"][:, 1]], axis=-1))
    o["rwkv_a1c"] = np.ascontiguousarray(np.concatenate([inputs["rwkv_a1"][:, 0], inputs["rwkv_a1"][:, 1]], axis=-1))
    o["ident_f"] = np.eye(128, dtype=np.float32)
    ii = np.arange(128)
    s_, t_ = ii[:, None], ii[None, :]
    tri0 = (s_ <= t_).astype(np.float32); tri1 = (s_ >= t_).astype(np.float32)
    st0 = (s_ < t_).astype(np.float32); st1 = (s_ > t_).astype(np.float32)
    o["rw_tri"] = np.stack([tri0, tri1])
    rep4 = lambda m: np.ascontiguousarray(np.concatenate([m] * 4, axis=1))
    o["rw_mS"] = np.stack([rep4(st0), rep4(st1)])
    o["rw_mI"] = np.stack([rep4(tri0), rep4(tri1)])
    bd = np.kron(np.eye(4, dtype=np.float32), np.ones((32, 32), np.float32))
    o["rw_mSd"] = np.stack([rep4(st0 * bd), rep4(st1 * bd)])
    o["rw_mSo"] = np.stack([rep4(st0 * (1 - bd)), rep4(st1 * (1 - bd))])
    o["rw_mTd"] = np.stack([rep4(st0.T * bd), rep4(st1.T * bd)])
    o["rw_I4"] = rep4(np.eye(128, dtype=np.float32))
    o["rope_cos"] = np.ascontiguousarray(np.concatenate([cs, cs], axis=0))
    o["rope_sin"] = np.ascontiguousarray(np.concatenate([sn, sn], axis=0))
    return o


IN_SHAPES = {
    "c_ctx": ([D], F32), "mod_w": ([DEPTH, D, 9 * D], F32), "mod_b": ([DEPTH, 9 * D], F32),
    "norm_pre": ([DEPTH, 3, D], F32), "norm_post": ([DEPTH, 3, D], F32),
    "ffn_w_gate": ([DEPTH, 2, D, FF], F32), "ffn_w_up": ([DEPTH, 2, D, FF], F32), "ffn_w_down": ([DEPTH, 2, FF, D], F32),
    "ident_bf": ([128, 128], BF16), "rope_cos": ([128, T], F32), "rope_sin": ([128, T], F32),
    "mla_w_dq": ([2, D, 512], F32), "mla_q_norm": ([2, 512], F32), "mla_kv_norm": ([2, 256], F32), "mla_w_o": ([2, D, D], F32),
    "mla_dkv_c": ([2, D, 256], F32), "mla_dkv_r": ([2, D, 128], F32), "mla_dkv_rp": ([2, D, 128], F32),
    "mla_uq_n": ([2, 512, 1024], F32), "mla_uq_r": ([2, 512, 512], F32), "mla_uq_rp": ([2, 512, 512], F32),
    "mla_ukv_k": ([2, 256, 1024], F32), "mla_ukv_v": ([2, 256, 1024], F32),
    "fnet_w_o": ([1, D, D], F32), "fnet_b_o": ([1, D], F32),
    "dft_ct": ([T, T], BF16), "dft_st": ([T, T], BF16), "dft_ct_c": ([TC, TC], BF16), "dft_st_c": ([TC, TC], BF16),
    "dft_cc": ([128, 128], BF16), "dft_scn": ([128, 128], BF16),
    "rwkv_mix": ([1, 6, D], F32), "rwkv_w_r": ([1, D, D], F32), "rwkv_w_k": ([1, D, D], F32), "rwkv_w_v": ([1, D, D], F32),
    "rwkv_w0": ([1, 2, D], F32), "rwkv_w2": ([1, 2, 64, D], F32), "rwkv_a0": ([1, 2, D], F32), "rwkv_a2": ([1, 2, 64, D], F32),
    "rwkv_g1": ([1, D, 160], F32), "rwkv_g2": ([1, 160, D], F32), "rwkv_k_k": ([1, D], F32), "rwkv_k_a": ([1, D], F32),
    "rwkv_r_k": ([1, 16, 64], F32), "rwkv_ln_w": ([1, D], F32), "rwkv_ln_b": ([1, D], F32), "rwkv_w_o": ([1, D, D], F32),
    "rwkv_w1c": ([1, D, 128], F32), "rwkv_a1c": ([1, D, 128], F32), "ident_f": ([128, 128], F32),
    "rw_tri": ([2, 128, 128], F32), "rw_mS": ([2, 128, 512], F32), "rw_mI": ([2, 128, 512], F32), "rw_mSd": ([2, 128, 512], F32), "rw_mSo": ([2, 128, 512], F32),
    "rw_mTd": ([2, 128, 512], F32), "rw_I4": ([128, 512], F32),
}


def build(nsteps=None, dbg=False):
    nc = bass.Bass("TRN2", target_bir_lowering=False)
    C = Ctx()

    def din(name, shape, dt=F32):
        return nc.dram_tensor(name, list(shape), dt, kind="ExternalInput").ap()

    def scratch(name, shape, dt=F32):
        return nc.dram_tensor(name, list(shape), dt, kind="Internal").ap()

    C.x = din("x", [T, D])
    C.c = din("c", [1, D])
    C.ctx = din("ctx", [TC, D])
    for k, (shp, dt) in IN_SHAPES.items():
        setattr(C, k, din(k, shp, dt))
    C.out = nc.dram_tensor("out", [T, D], F32, kind="ExternalOutput").ap()
    C.modv = scratch("modv", [DEPTH, 2, 9 * D])
    C.hL = scratch("hL", [T, D])
    C.hC = scratch("hC", [TC, D])
    C.sQT = scratch("sQT", [128, 8, T + TC], BF16)
    C.sQrT = scratch("sQrT", [128, 4, T + TC], BF16)
    C.sKT = scratch("sKT", [128, 8, T + TC], BF16)
    C.sKrT = scratch("sKrT", [128, T + TC], BF16)
    C.sV = scratch("sV", [128, NT, 1024], BF16)
    for nm in ("sR", "sK", "sVv", "sKK", "sG"):
        setattr(C, nm, scratch(nm, [NTOK, D]))
    C.sLW = scratch("sLW", [2, NTOK, D])
    C.sA = scratch("sA", [2, NTOK, D])
    C.sY = scratch("sY", [2, NTOK, D])
    if dbg:
        C.dbg_hC = nc.dram_tensor("dbg_hC", [TC, D], F32, kind="ExternalOutput").ap()

    def tile_ap(base_l, base_c):
        def f(who, ti):
            b = base_l if who == 0 else base_c
            return b[ti * 128:(ti + 1) * 128, :]
        return f

    steps = []
    for l in range(DEPTH):
        steps += [("ffn", l, 0), ("mix", l), ("ffn", l, 1)]
    if nsteps is not None:
        steps = steps[:nsteps]
    with ExitStack() as st:
        P = Prog(nc, st)
        P.C = C
        mod_phase(P, C)
        for si, stp in enumerate(steps):
            l = stp[1]
            kind, j, last = l % 3, l // 3, l == DEPTH - 1
            final = (nsteps is None and si == len(steps) - 1)
            if stp[0] == "ffn":
                f = stp[2]
                src = tile_ap(C.x, C.ctx) if si == 0 else tile_ap(C.hL, C.hC)
                dst = tile_ap(C.out, C.hC) if final else tile_ap(C.hL, C.hC)
                ffn_phase(P, C, l, f, src, dst, T // 128, (f == 0) or (not last))
            else:
                if kind == 0:
                    mla_proj_phase(P, C, l, j, not last)
                    mla_attn_phase(P, C, l, j, not last)
                elif kind == 1:
                    fnet_phase(P, C, l, j, not last)
                else:
                    rwkv_phases(P, C, l, j, not last)
        if nsteps is not None:
            with P.phase():
                P.dma(C.out[:, :], C.hL[:, :], [], ["dm"])
                if dbg:
                    P.dma(C.dbg_hC[:, :], C.hC[:, :], [], ["dh"])
    return nc


_NC_CACHE = {}


def make_in_maps(inputs, ncores=8):
    shared = host_layout(inputs)
    in_maps = []
    for b in range(ncores):
        m = dict(shared)
        m["x"] = np.ascontiguousarray(inputs["x"][b])
        m["c"] = np.ascontiguousarray(inputs["c"][b:b + 1])
        m["ctx"] = np.ascontiguousarray(inputs["ctx"][b])
        in_maps.append(m)
    return in_maps


def kernel(**inputs):
    inputs = {k: np.asarray(v) for k, v in inputs.items()}
    if "nc" not in _NC_CACHE:
        _NC_CACHE["nc"] = build()
    nc = _NC_CACHE["nc"]
    in_maps = make_in_maps(inputs, 8)
    res = run_bass_kernel_spmd(nc, in_maps, core_ids=list(range(8)))
    return np.stack([np.asarray(r["out"]) for r in res.results], axis=0).astype(np.float32)


def fnet_phase(P, C, l, j, ctx_out):
    with P.phase():
        cb = alloc_common(P)
        stg = [P.sb(f"stg{i}", [128, 1024], F32) for i in range(2)]
        Wo = P.sb("Wo", [128, 8, 1024], BF16)
        cnt = [0]
        load_w_bf16(P, Wo, C.fnet_w_o[j], stg, 8, 1024, "Wo", cnt)
        bo = P.sb("bo", [128, D], F32)
        P.dma(bo[:], C.fnet_b_o[j:j + 1, :].broadcast_to([128, D]), [], ["bo"])
        cc = P.sb("cc", [128, 128], BF16)
        scn = P.sb("scn", [128, 128], BF16)
        P.dma(cc[:], C.dft_cc[:, :], [], ["cc"])
        P.dma(scn[:], C.dft_scn[:, :], [], ["scn"])
        Zc = P.sb("Zc", [128, 16, 1024], BF16)
        Zs = P.sb("Zs", [128, 16, 1024], BF16)
        NQ = 256
        CTb = [P.sb(f"CTb{i}", [128, 16, NQ], BF16) for i in range(2)]
        STb = [P.sb(f"STb{i}", [128, 16, NQ], BF16) for i in range(2)]
        uT = P.sb("uT", [128, 8, 128], BF16)
        FT = P.sb("FT", [128, 8, NQ], BF16)
        psZ = [P.ps(f"psZ{i}", [128, 1024]) for i in range(2)]
        psF = [P.ps(f"psF{i}", [128, 512]) for i in range(1)]
        psY = P.ps("psY", [128, 1024])
        groups = [(0, T, C.hL, C.dft_ct, C.dft_st)]
        if ctx_out:
            groups.append((1, TC, C.hC, C.dft_ct_c, C.dft_st_c))
        bi = 0
        fi = 0
        for (who, Tt, hbase, ct, st_) in groups:
            ntl = Tt // 128
            mod_tiles(P, cb, l, 1, who, 1.0)
            scale = float((Tt * 128) ** -0.5)
            for ti in range(ntl):
                pro_tile(P, cb, hbase[ti * 128:(ti + 1) * 128, :], ("h", who, ti))
                P.copy("act", uT[:], cb.psT[:], ["cpsT"], ["uT"])
                for (tab, tk, pz, pzk, Zd, zk, ce) in ((cc, "cc", psZ[0], "psZ0", Zc, "Zc", "act"), (scn, "scn", psZ[1], "psZ1", Zs, "Zs", "dve")):
                    for g in range(8):
                        P.mm(pz[:, g * 128:(g + 1) * 128], uT[:, g, :], tab[:], True, True, ["uT", tk], [pzk])
                    P.copy(ce, Zd[:, ti, :], pz[:], [pzk], [(zk, ti)])
            zr = [("Zc", ti) for ti in range(ntl)] + [("Zs", ti) for ti in range(ntl)]
            for b0 in range(0, Tt, NQ):
                cbuf, sbuf_ = CTb[bi % 2], STb[bi % 2]
                ck, sk = ("CTb", bi % 2), ("STb", bi % 2)
                bi += 1
                P.dma(cbuf[:, 0:ntl, :], ct[:, b0:b0 + NQ].rearrange("(c p) n -> p c n", p=128), [], [ck])
                P.dma(sbuf_[:, 0:ntl, :], st_[:, b0:b0 + NQ].rearrange("(c p) n -> p c n", p=128), [], [sk], q="act")
                for g in range(8):
                    pf = psF[0]; pfk = ("psF", 0); fi += 1
                    for tc_ in range(ntl):
                        P.mm(pf[:, 0:NQ], Zc[:, tc_, g * 128:(g + 1) * 128], cbuf[:, tc_, :], tc_ == 0, False, zr + [ck], [pfk])
                        P.mm(pf[:, 0:NQ], Zs[:, tc_, g * 128:(g + 1) * 128], sbuf_[:, tc_, :], False, tc_ == ntl - 1, zr + [sk], [pfk])
                    P.act(FT[:, g, :], pf[:, 0:NQ], AF.Copy, [pfk], [("FT", g)], scale=scale)
                for jt in range(NQ // 128):
                    ti = b0 // 128 + jt
                    hap = hbase[ti * 128:(ti + 1) * 128, :]
                    x = cb.xb[cb.gi % 2]; xk = ("cxb", cb.gi % 2); cb.gi += 1
                    P.dma(x[:], hap, [("h", who, ti)], [xk])
                    for nh in range(2):
                        for g in range(8):
                            P.mm(psY[:, nh * 512:(nh + 1) * 512], FT[:, g, jt * 128:(jt + 1) * 128], Wo[:, g, nh * 512:(nh + 1) * 512],
                                 g == 0, g == 7, [("FT", g), ("Wo", g)], ["psY"])
                    epi_tile(P, cb, psY[:], ["psY"], x, xk, hap, ("h", who, ti), bias=(bo[:], "bo"))


NTOK = T + TC
DECAY_C = float(np.exp(-0.5))


def rwkv_feat_phase(P, C, l, j):
    UW = 2308
    with P.phase():
        cb = alloc_common(P)
        stg = [P.sb(f"stg{i}", [128, 1024], F32) for i in range(2)]
        uT = P.sb("uT", [128, 8, UW], BF16)
        xxT = P.sb("xxT", [128, 8, 256], BF16)
        tmpw = P.sb("tmpw", [128, 256], F32)
        P.add("pool", lambda e: e.memset(uT[:], 0.0), [], ["uTall"])
        Wr = P.sb("Wr", [128, 8, 1024], BF16)
        Wk = P.sb("Wk", [128, 8, 1024], BF16)
        Wv = P.sb("Wv", [128, 8, 1024], BF16)
        W1 = P.sb("W1", [128, 8, 128], BF16)
        A1 = P.sb("A1", [128, 8, 128], BF16)
        G1 = P.sb("G1", [128, 8, 160], BF16)
        W2 = P.sb("W2", [128, 1, 1024], BF16)
        A2 = P.sb("A2", [128, 1, 1024], BF16)
        G2 = P.sb("G2", [128, 2, 1024], BF16)
        cnt = [0]
        load_w_bf16(P, Wr, C.rwkv_w_r[j], stg, 8, 1024, "Wr", cnt)
        load_w_bf16(P, Wk, C.rwkv_w_k[j], stg, 8, 1024, "Wk", cnt)
        load_w_bf16(P, Wv, C.rwkv_w_v[j], stg, 8, 1024, "Wv", cnt)
        load_w_bf16(P, W1, C.rwkv_w1c[j], stg, 8, 128, "W1", cnt)
        load_w_bf16(P, A1, C.rwkv_a1c[j], stg, 8, 128, "A1", cnt)
        load_w_bf16(P, G1, C.rwkv_g1[j], stg, 8, 160, "G1", cnt)
        load_w_bf16(P, W2, C.rwkv_w2[j].rearrange("e r d -> (e r) d"), stg, 1, 1024, "W2", cnt)
        load_w_bf16(P, A2, C.rwkv_a2[j].rearrange("e r d -> (e r) d"), stg, 1, 1024, "A2", cnt)
        load_w_bf16(P, G2, C.rwkv_g2[j, 0:128, :], stg, 1, 1024, "G2", cnt)
        n = cnt[0]; s = stg[n % 2]
        P.dma(s[0:32, :], C.rwkv_g2[j, 128:160, :], [], [("cstg", n % 2)])
        P.copy("dve", G2[0:32, 1, :], s[0:32, :], [("cstg", n % 2)], [("G2", 1)])
        cnt[0] += 1
        mixc = P.sb("mixc", [128, 6, 8], F32)
        for m_ in range(6):
            P.dma(mixc[:, m_, :], C.rwkv_mix[j, m_].rearrange("(k p) -> p k", p=128), [], ["mixc"], allow_slow_non_contiguous=True)
        w0t = [P.sb(f"w0t{e}", [128, D], F32) for e in range(2)]
        a0t = [P.sb(f"a0t{e}", [128, D], F32) for e in range(2)]
        kkt = cb.t1
        for e in range(2):
            P.dma(w0t[e][:], C.rwkv_w0[j, e:e + 1, :].broadcast_to([128, D]), [], [("w0t", e)])
            P.dma(a0t[e][:], C.rwkv_a0[j, e:e + 1, :].broadcast_to([128, D]), [], [("a0t", e)], q="act")
        def ucol(who, ti):
            return (1 if who == 0 else 2051) + ti * 128
        for (who, ntl, hb) in ((0, 16, C.hL), (1, 2, C.hC)):
            mod_tiles(P, cb, l, 1, who, 1.0)
            for ti in range(ntl):
                pro_tile(P, cb, hb[ti * 128:(ti + 1) * 128, :], ("h", who, ti))
                c0 = ucol(who, ti)
                P.copy("act", uT[:, :, c0:c0 + 128], cb.psT[:], ["cpsT", "uTall"], [("uTt", who, ti)])
        allu = [("uTt", 0, ti) for ti in range(16)] + [("uTt", 1, ti) for ti in range(2)]
        P.dma(kkt[:], C.rwkv_k_k[j:j + 1, :].broadcast_to([128, D]), [], ["kkt", "ct1"])
        xm = [P.sb(f"xm{m}", [128, 8, 256], BF16) for m in range(6)]
        hW = P.sb("hW", [128, 256], BF16)
        hA = P.sb("hA", [128, 256], BF16)
        hG = P.sb("hG", [128, 2, 256], BF16)
        ot = [P.sb(f"ot{i}", [128, D], F32) for i in range(2)]
        sq = cb.tmp
        s16 = P.sb("s16", [128, 32], F32)
        psH = [P.ps(f"psH{i}", [128, 512]) for i in range(2)]
        psO = [P.ps(f"psO{i}", [128, 1024]) for i in range(2)]
        oc = [0]
        pc = [0]

        def out_tile(name_key):
            i = oc[0] % 2; oc[0] += 1
            return ot[i], ("ot", i)

        def ps_tile():
            i = pc[0] % 2; pc[0] += 1
            return psO[i], ("psO", i)

        blocks = [(0, t0, 2) for t0 in range(0, 16, 2)] + [(1, 0, 2)]
        hc_ = 0
        for (who, t0, nt) in blocks:
            nq = nt * 128
            c0 = ucol(who, t0)
            g0 = (t0 + (16 if who == 1 else 0)) * 128
            for k in range(8):
                P.tt("dve", tmpw[:, 0:nq], uT[:, k, c0 - 1:c0 - 1 + nq], uT[:, k, c0 + 1:c0 + 1 + nq], ALU.add, allu, ["tmpw"])
                P.stt("dve", xxT[:, k, 0:nq], tmpw[:, 0:nq], 0.5, uT[:, k, c0:c0 + nq], ALU.mult, ALU.subtract, ["tmpw"] + allu, [("xxT", k)])
            allx = [("xxT", k) for k in range(8)]
            for m in range(6):
                for k in range(8):
                    P.stt("dve" if (m + k) % 2 == 0 else "dve", xm[m][:, k, 0:nq], xxT[:, k, 0:nq], mixc[:, m, k:k + 1], uT[:, k, c0:c0 + nq],
                          ALU.mult, ALU.add, allx + allu + ["mixc"], [("xm", m, k)])
            xr_ = lambda m: [("xm", m, k) for k in range(8)]
            ph = psH[hc_ % 2]; phk = ("psH", hc_ % 2); hc_ += 1
            for k in range(8):
                P.mm(ph[:, 0:nq], W1[:, k, :], xm[1][:, k, 0:nq], k == 0, k == 7, xr_(1) + [("W1", k)], [phk])
            P.act(hW[:, 0:nq], ph[:, 0:nq], AF.Tanh, [phk], ["hW"])
            ph = psH[hc_ % 2]; phk = ("psH", hc_ % 2); hc_ += 1
            for k in range(8):
                P.mm(ph[:, 0:nq], A1[:, k, :], xm[4][:, k, 0:nq], k == 0, k == 7, xr_(4) + [("A1", k)], [phk])
            P.copy("dve", hA[:, 0:nq], ph[:, 0:nq], [phk], ["hA"])
            for (gi_, lo, hi) in ((0, 0, 128), (1, 128, 160)):
                ph = psH[hc_ % 2]; phk = ("psH", hc_ % 2); hc_ += 1
                for k in range(8):
                    P.mm(ph[0:hi - lo, 0:nq], G1[:, k, lo:hi], xm[5][:, k, 0:nq], k == 0, k == 7, xr_(5) + [("G1", k)], [phk])
                P.act(hG[0:hi - lo, gi_, 0:nq], ph[0:hi - lo, 0:nq], AF.Sigmoid, [phk], [("hG", gi_)])
            for jt in range(nt):
                r0 = g0 + jt * 128
                cs = slice(jt * 128, (jt + 1) * 128)
                for (m, Wm, wn, dstA) in ((0, Wr, "Wr", C.sR), (3, Wv, "Wv", C.sVv)):
                    pt, ptk = ps_tile()
                    for nh in range(2):
                        for k in range(8):
                            P.mm(pt[:, nh * 512:(nh + 1) * 512], xm[m][:, k, cs], Wm[:, k, nh * 512:(nh + 1) * 512], k == 0, k == 7,
                                 xr_(m) + [(wn, k)], [ptk])
                    o, ok = out_tile(0)
                    P.copy("act", o[:], pt[:], [ptk], [ok])
                    P.dma(dstA[r0:r0 + 128, :], o[:], [ok], [(wn, "out", r0)])
                pt, ptk = ps_tile()
                for nh in range(2):
                    for k in range(8):
                        P.mm(pt[:, nh * 512:(nh + 1) * 512], xm[2][:, k, cs], Wk[:, k, nh * 512:(nh + 1) * 512], k == 0, k == 7,
                             xr_(2) + [("Wk", k)], [ptk])
                o, ok = out_tile(0)
                P.copy("act", o[:], pt[:], [ptk], [ok])
                P.dma(C.sK[r0:r0 + 128, :], o[:], [ok], [("k", "out", r0)])
                o2, ok2 = out_tile(0)
                P.tt("dve", o2[:], o[:], kkt[:], ALU.mult, [ok, "kkt"], [ok2])
                P.tt("pool", sq[:], o2[:], o2[:], ALU.mult, [ok2], ["ctmp"])
                P.add("dve", lambda e, o_=s16[:, 0:16], i_=sq[:].rearrange("p (h n) -> p h n", n=64): e.tensor_reduce(out=o_, in_=i_, axis=AX.X, op=ALU.add), ["ctmp"], ["s16"])
                P.act(s16[:, 0:16], s16[:, 0:16], AF.Sqrt, ["s16"], ["s16"])
                P.ts("dve", s16[:, 0:16], s16[:, 0:16], 1e-12, None, ALU.max, None, ["s16"], ["s16"])
                P.recip(s16[:, 16:32], s16[:, 0:16], ["s16"], ["s16r"])
                o23 = o2[:].rearrange("p (h n) -> p h n", n=64)
                P.tt("dve", o23, o23, s16[:, 16:32].unsqueeze(2).broadcast_to([128, 16, 64]), ALU.mult, [ok2, "s16r"], [ok2])
                P.dma(C.sKK[r0:r0 + 128, :], o2[:], [ok2], [("kk", "out", r0)])
                pt, ptk = ps_tile()
                for nh in range(2):
                    P.mm(pt[:, nh * 512:(nh + 1) * 512], hG[:, 0, cs], G2[:, 0, nh * 512:(nh + 1) * 512], True, False, [("hG", 0), ("hG", 1), ("G2", 0)], [ptk])
                    P.mm(pt[:, nh * 512:(nh + 1) * 512], hG[0:32, 1, cs], G2[0:32, 1, nh * 512:(nh + 1) * 512], False, True, [("hG", 1), ("G2", 1)], [ptk])
                o, ok = out_tile(0)
                P.copy("act", o[:], pt[:], [ptk], [ok])
                P.dma(C.sG[r0:r0 + 128, :], o[:], [ok], [("g", "out", r0)])
                for e in range(2):
                    pt, ptk = ps_tile()
                    for nh in range(2):
                        P.mm(pt[:, nh * 512:(nh + 1) * 512], hW[e * 64:(e + 1) * 64, cs], W2[e * 64:(e + 1) * 64, 0, nh * 512:(nh + 1) * 512], True, True,
                             ["hW", ("W2", 0)], [ptk])
                    o, ok = out_tile(0)
                    P.tt("dve", o[:], pt[:], w0t[e][:], ALU.add, [ptk, ("w0t", e)], [ok])
                    P.act(o[:], o[:], AF.Sigmoid, [ok], [ok])
                    P.ts("dve", o[:], o[:], -DECAY_C, None, ALU.mult, None, [ok], [ok])
                    P.dma(C.sLW[e, r0:r0 + 128, :], o[:], [ok], [("lw", e, r0)])
                    pt, ptk = ps_tile()
                    for nh in range(2):
                        P.mm(pt[:, nh * 512:(nh + 1) * 512], hA[e * 64:(e + 1) * 64, cs], A2[e * 64:(e + 1) * 64, 0, nh * 512:(nh + 1) * 512], True, True,
                             ["hA", ("A2", 0)], [ptk])
                    o, ok = out_tile(0)
                    P.tt("dve", o[:], pt[:], a0t[e][:], ALU.add, [ptk, ("a0t", e)], [ok])
                    P.act(o[:], o[:], AF.Sigmoid, [ok], [ok])
                    P.dma(C.sA[e, r0:r0 + 128, :], o[:], [ok], [("a", e, r0)])


class _Cut(Exception):
    pass


def rwkv_scan_phase(P, C, l, j, st_lim=NT, g_lim=4, stage=99):
    with P.phase():
        try:
            _rwkv_scan_body(P, C, l, j, st_lim, g_lim, stage)
        except _Cut:
            pass


def _rwkv_scan_body(P, C, l, j, st_lim, g_lim, stage):
    def cut(n):
        if stage == n:
            raise _Cut()
    NSLOT = 4
    identf = P.sb("identf", [128, 128], F32)
    ones = P.sb("onesf", [128, 128], F32)
    tri = [P.sb(f"tri{e}", [128, 128], F32) for e in range(2)]
    mSd = [P.sb(f"mSd{e}", [128, 512], F32) for e in range(2)]
    mSo = [P.sb(f"mSo{e}", [128, 512], F32) for e in range(2)]
    mS = [P.sb(f"mS{e}", [128, 512], F32) for e in range(2)]
    mI = [P.sb(f"mI{e}", [128, 512], F32) for e in range(2)]
    mTd = [P.sb(f"mTd{e}", [128, 512], F32) for e in range(2)]
    I4 = P.sb("I4", [128, 512], F32)
    P.dma(I4[:], C.rw_I4[:, :], [], ["I4"])
    P.dma(identf[:], C.ident_f[:, :], [], ["identf"])
    P.add("pool", lambda e_: e_.memset(ones[:], 1.0), [], ["ones"])
    for e in range(2):
        P.dma(tri[e][:], C.rw_tri[e], [], [("tri", e)])
        P.dma(mS[e][:], C.rw_mS[e], [], [("mS", e)])
        P.dma(mI[e][:], C.rw_mI[e], [], [("mI", e)], q="act")
        P.dma(mSd[e][:], C.rw_mSd[e], [], [("mSd", e)])
        P.dma(mSo[e][:], C.rw_mSo[e], [], [("mSo", e)], q="act")
        P.dma(mTd[e][:], C.rw_mTd[e], [], [("mTd", e)])
    kat = P.sb("kat", [128, D], F32)
    P.dma(kat[:], C.rwkv_k_a[j:j + 1, :].broadcast_to([128, D]), [], ["kat"])
    ST = [P.sb(f"ST{e}", [128, 8, 64], F32) for e in range(2)]
    for e in range(2):
        P.add("pool", lambda e_, t_=ST[e]: e_.memset(t_[:], 0.0), [], [("ST", e, hp) for hp in range(8)])
    tr_, tk_, tkk, tlw, ta = [P.sb(n, [128, D], F32) for n in ("tr_", "tk_", "tkk", "tlw", "ta")]
    tvs = [P.sb(f"tv{i}", [128, D], F32) for i in range(2)]
    tb, tkd, cumS, E = [P.sb(n, [128, D], F32) for n in ("tb", "tkd", "cumS", "E")]
    Be, Ke = [P.sb(n, [128, D], F32) for n in ("Be", "Ke")]
    FT = P.sb("FT", [128, 8, 4, 128], F32)
    Dg = P.sb("Dg", [128, 8, 128], F32)
    Yt = P.sb("Yt", [128, D], F32)

    class Slot:
        pass
    slots = []
    for si in range(NSLOT):
        S = Slot()
        S.i = si
        S.Gb = [P.sb(f"Gb{si}_{i}", [128, 4, 128], F32) for i in range(2)]
        S.Lb = [P.sb(f"Lb{si}_{i}", [128, 4, 128], F32) for i in range(2)]
        S.NTb = [P.sb(f"NTb{si}_{i}", [128, 4, 128], F32) for i in range(2)]
        S.Go, S.LakT, S.MrbT, S.MrkT = [P.sb(f"{n}{si}", [128, 4, 128], F32) for n in ("Go", "LakT", "MrbT", "MrkT")]
        S.Xs, S.Zs, S.Ws, S.Us = [P.sb(f"{n}{si}", [128, 4, 64], F32) for n in ("Xs", "Zs", "Ws", "Us")]
        slots.append(S)
    banks = [P.ps(f"bk{i}", [128, 512]) for i in range(8)]
    bc = [0]

    def bank():
        i = bc[0] % 8
        bc[0] += 1
        return banks[i], ("bank", i)

    fl = lambda t_: t_[:].rearrange("p s t -> p (s t)")

    def chain(e, hg, S, tv_, tvk):
        si = S.i
        T_ = lambda n: (n, si)
        gq, hp0 = hg % 2, (hg // 2) * 4
        heads = [2 * (hp0 + i) + gq for i in range(4)]
        hps = [hp0 + i for i in range(4)]
        ftk = [("FT", hp) for hp in hps]
        stk = [("ST", e, hp) for hp in hps]

        def ft(h, si_):
            q = h % 2
            return FT[q * 64:q * 64 + 64, h // 2, si_, :]

        def grp(a_si, b_si, outs):
            bk, bkk = bank()
            for i, h in enumerate(heads):
                P.mm(bk[:, i * 128:(i + 1) * 128], ft(h, a_si), ft(h, b_si), True, True, ftk, [bkk])
            for (dst, dk, mask, mk) in outs:
                P.tt("dve", fl(dst), bk[:, :], mask[:], ALU.mult, [mk], [bkk, dk])

        grp(2, 0, [(S.Gb[0], T_("Gb0"), mSd[e], ("mSd", e)), (S.Go, T_("Go"), mSo[e], ("mSo", e))])
        yield
        grp(0, 2, [(S.Lb[0], T_("Lb0"), mTd[e], ("mTd", e))])
        yield
        grp(3, 0, [(S.LakT, T_("LakT"), mS[e], ("mS", e))])
        yield
        grp(2, 1, [(S.MrbT, T_("MrbT"), mI[e], ("mI", e))])
        yield
        grp(3, 1, [(S.MrkT, T_("MrkT"), mI[e], ("mI", e))])
        yield
        bk, bkk = bank()
        for i, h in enumerate(heads):
            q, hp = h % 2, h // 2
            P.mm(bk[:, i * 64:(i + 1) * 64], ft(h, 0), ST[e][q * 64:q * 64 + 64, hp, :], True, False, ftk + stk, [bkk])
            P.mm(bk[:, i * 64:(i + 1) * 64], S.LakT[:, i, :], tv_[:, h * 64:(h + 1) * 64], False, True, [T_("LakT"), tvk], [bkk])
        P.copy("act", fl(S.Xs), bk[:, 0:256], [], [bkk, T_("Xs")])
        P.tt("pool", fl(S.NTb[0]), I4[:], fl(S.Gb[0]), ALU.add, ["I4", T_("Gb0")], [T_("NT0")])
        yield
        for k in range(1, 5):
            gp, lp = S.Gb[(k - 1) % 2], S.Lb[(k - 1) % 2]
            gpk, lpk = T_("Gb%d" % ((k - 1) % 2)), T_("Lb%d" % ((k - 1) % 2))
            gn, ln = S.Gb[k % 2], S.Lb[k % 2]
            gnk, lnk = T_("Gb%d" % (k % 2)), T_("Lb%d" % (k % 2))
            ntp, ntn = T_("NT%d" % ((k - 1) % 2)), T_("NT%d" % (k % 2))
            bl, blk = bank()
            for i in range(4):
                P.mm(bl[:, i * 128:(i + 1) * 128], gp[:, i, :], lp[:, i, :], True, True, [gpk, lpk], [blk])
            if k < 4:
                bg, bgk = bank()
                for i in range(4):
                    P.mm(bg[:, i * 128:(i + 1) * 128], lp[:, i, :], gp[:, i, :], True, True, [gpk, lpk], [bgk])
            P.copy("act", fl(ln), bl[:, :], [], [blk, lnk])
            if k < 4:
                P.copy("dve", fl(gn), bg[:, :], [], [bgk, gnk])
            yield
            bn, bnk = bank()
            for i in range(4):
                P.mm(bn[:, i * 128:(i + 1) * 128], ln[:, i, :], S.NTb[(k - 1) % 2][:, i, :], True, True, [lnk, ntp], [bnk])
            P.tt("dve", fl(S.NTb[k % 2]), fl(S.NTb[(k - 1) % 2]), bn[:, :], ALU.add, [ntp], [bnk, ntn])
            yield
        NT, NTk = S.NTb[0], T_("NT0")
        bk, bkk = bank()
        for i in range(4):
            P.mm(bk[:, i * 64:(i + 1) * 64], NT[:, i, :], S.Xs[:, i, :], True, True, [NTk, T_("Xs")], [bkk])
        P.copy("act", fl(S.Zs), bk[:, 0:256], [], [bkk, T_("Zs")])
        yield
        ucur, uk = S.Zs, T_("Zs")
        for it in range(3):
            bk, bkk = bank()
            for i in range(4):
                P.mm(bk[:, i * 64:(i + 1) * 64], S.Go[:, i, :], ucur[:, i, :], True, True, [T_("Go"), uk], [bkk])
            P.copy("act", fl(S.Ws), bk[:, 0:256], [], [bkk, T_("Ws")])
            yield
            bk, bkk = bank()
            for i in range(4):
                P.mm(bk[:, i * 64:(i + 1) * 64], NT[:, i, :], S.Ws[:, i, :], True, True, [NTk, T_("Ws")], [bkk])
            P.tt("dve", fl(S.Us), fl(S.Zs), bk[:, 0:256], ALU.add, [T_("Zs")], [bkk, T_("Us")])
            yield
            ucur, uk = S.Us, T_("Us")
        bk, bkk = bank()
        for i, h in enumerate(heads):
            q, hp = h % 2, h // 2
            P.mm(bk[:, i * 64:(i + 1) * 64], ft(h, 1), ST[e][q * 64:q * 64 + 64, hp, :], True, False, ftk + stk, [bkk])
            P.mm(bk[:, i * 64:(i + 1) * 64], S.MrbT[:, i, :], S.Us[:, i, :], False, False, [T_("MrbT"), T_("Us")], [bkk])
            P.mm(bk[:, i * 64:(i + 1) * 64], S.MrkT[:, i, :], tv_[:, h * 64:(h + 1) * 64], False, True, [T_("MrkT"), tvk], [bkk])
        P.copy("act", Yt[:].rearrange("p (a q n) -> p a q n", q=2, n=64)[:, hp0:hp0 + 4, gq, :],
               bk[:, 0:256].rearrange("p (s t) -> p s t", s=4), [], [bkk, ("Yt", hg)])
        yield
        bk, bkk = bank()
        for i, h in enumerate(heads):
            hp = h // 2
            cs = slice(hp * 128, (hp + 1) * 128)
            P.mm(bk[:, i * 64:(i + 1) * 64], Dg[:, hp, :], ST[e][:, hp, :], True, False, [("Dg", hp)] + stk, [bkk])
            P.mm(bk[:, i * 64:(i + 1) * 64], Be[:, cs], S.Us[:, i, :], False, False, ["Be", T_("Us")], [bkk])
            P.mm(bk[:, i * 64:(i + 1) * 64], Ke[:, cs], tv_[:, h * 64:(h + 1) * 64], False, True, ["Ke", tvk], [bkk])
        b4 = bk[:, 0:256].rearrange("p (s t) -> p s t", s=4)
        P.copy("dve", ST[e][gq * 64:gq * 64 + 64, hp0:hp0 + 4, :], b4[gq * 64:gq * 64 + 64, :, :], [], [bkk] + stk)

    order = {0: [16, 17] + list(range(16)), 1: [17, 16] + list(range(15, -1, -1))}
    cut(-1)
    iters = [(st, e) for st in range(st_lim) for e in range(2)]

    def issue_loads(n):
        st_, e_ = iters[n]
        r0_ = order[e_][st_] * 128
        for (tile_, src, nm) in ((tr_, C.sR, "tr"), (tk_, C.sK, "tk"), (tkk, C.sKK, "tkk"), (tlw, C.sLW[e_], "tlw"), (ta, C.sA[e_], "ta"),
                                 (tvs[n % 2], C.sVv, ("tv", n % 2))):
            P.dma(tile_[:], src[r0_:r0_ + 128, :], [], [nm])

    issue_loads(0)
    pending_store = None
    for n, (st, e) in enumerate(iters):
        if True:
            ti = order[e][st]
            r0 = ti * 128
            tv_, tvk = tvs[n % 2], ("tv", n % 2)
            P.tt("pool", tb[:], tkk[:], ta[:], ALU.mult, ["tkk", "ta"], ["tb"])
            P.tt("dve", tkd[:], ta[:], kat[:], ALU.mult, ["ta", "kat"], ["tkd"])
            P.tt("dve", tkd[:], tkd[:], kat[:], ALU.subtract, ["tkd", "kat"], ["tkd"])
            P.stt("dve", tkd[:], tkd[:], 1.0, tk_[:], ALU.add, ALU.mult, ["tkd", "tk"], ["tkd"])
            for nh in range(2):
                cs = slice(nh * 512, (nh + 1) * 512)
                bk, bkk = bank()
                P.mm(bk[:, :], tri[e][:], tlw[:, cs], True, True, [("tri", e), "tlw"], [bkk])
                P.copy("act", cumS[:, cs], bk[:, :], [], [bkk, ("cumS", nh)])
                bk, bkk = bank()
                P.mm(bk[:, :], ones[:], tlw[:, cs], True, True, ["ones", "tlw"], [bkk])
                P.act(E[:, cs], bk[:, :], AF.Exp, [], [bkk, ("Etot", nh)])
                P.tt("dve", Be[:, cs], bk[:, :], cumS[:, cs], ALU.subtract, [("cumS", nh)], [bkk, ("Be4", nh)])
            cS = [("cumS", 0), ("cumS", 1)]
            for hp in range(8):
                P.tt("pool", Dg[:, hp, :], identf[:], E[:, hp * 128:(hp + 1) * 128], ALU.mult, ["identf", ("Etot", hp // 4)], [("Dg", hp)])
            P.act(Be[:], Be[:], AF.Exp, [("Be4", 0), ("Be4", 1)], ["Be"])
            P.tt("pool", Ke[:], tkd[:], Be[:], ALU.mult, ["tkd", "Be"], ["Ke"])
            P.tt("dve", Be[:], tb[:], Be[:], ALU.mult, ["tb", "Be"], ["Be"])
            dgk = [("Dg", hp) for hp in range(8)]
            P.act(E[:], cumS[:], AF.Exp, cS + dgk, ["E"])
            P.tt("dve", tr_[:], tr_[:], E[:], ALU.mult, ["tr", "E"], ["tr"])
            P.tt("dve", E[:], cumS[:], tlw[:], ALU.subtract, cS + ["tlw", "tr"], ["E"])
            P.act(E[:], E[:], AF.Exp, ["E"], ["E"])
            P.stt("dve", tkk[:], tkk[:], -1.0, E[:], ALU.mult, ALU.mult, ["tkk", "E", "tb"], ["tkk"])
            P.act(E[:], cumS[:], AF.Exp, cS + ["tkk"], ["E"], scale=-1.0)
            P.tt("dve", tb[:], tb[:], E[:], ALU.mult, ["tb", "E", "Be"], ["tb"])
            P.tt("pool", tkd[:], tkd[:], E[:], ALU.mult, ["tkd", "E", "Ke"], ["tkd"])
            cut(4)
            for hp in range(8):
                bk, bkk = bank()
                for si, (srcT, sk) in enumerate(((tkk, "tkk"), (tr_, "tr"), (tb, "tb"), (tkd, "tkd"))):
                    P.mm(bk[:, si * 128:(si + 1) * 128], srcT[:, hp * 128:(hp + 1) * 128], identf[:], True, True, [sk, "identf"], [bkk])
                P.copy("act" if hp % 2 else "dve", FT[:, hp, :, :], bk[:, :].rearrange("p (s t) -> p s t", s=4), [], [bkk, ("FT", hp)])
            cut(5)
            if n + 1 < len(iters):
                issue_loads(n + 1)
            if pending_store is not None:
                pending_store()
                pending_store = None
            gens = [chain(e, hg, slots[hg % NSLOT], tv_, tvk) for hg in range(g_lim)]
            while gens:
                for g in list(gens):
                    try:
                        next(g)
                    except StopIteration:
                        gens.remove(g)
            pending_store = (lambda e=e, r0=r0, ti=ti: P.dma(C.sY[e, r0:r0 + 128, :], Yt[:], [("Yt", hg) for hg in range(4)], [("sY", e, ti)]))
    if pending_store is not None:
        pending_store()


def rwkv_out_phase(P, C, l, j):
    with P.phase():
        cb = alloc_common(P)
        stg = [P.sb(f"stg{i}", [128, 1024], F32) for i in range(2)]
        Wo = P.sb("Wo", [128, 8, 1024], BF16)
        cnt = [0]
        load_w_bf16(P, Wo, C.rwkv_w_o[j], stg, 8, 1024, "Wo", cnt)
        kat, rkt, lnw, lnb = [P.sb(n, [128, D], F32) for n in ("kat", "rkt", "lnw", "lnb")]
        P.dma(kat[:], C.rwkv_k_a[j:j + 1, :].broadcast_to([128, D]), [], ["kat"])
        P.dma(rkt[:], C.rwkv_r_k[j].rearrange("h n -> (h n)").rearrange("(o d) -> o d", o=1).broadcast_to([128, D]), [], ["rkt"])
        P.dma(lnw[:], C.rwkv_ln_w[j:j + 1, :].broadcast_to([128, D]), [], ["lnw"])
        P.dma(lnb[:], C.rwkv_ln_b[j:j + 1, :].broadcast_to([128, D]), [], ["lnb"])
        y0, y1, tr_, tk_, tv_, a0, a1, tg = [P.sb(n, [128, D], F32) for n in ("y0", "y1", "tr_", "tk_", "tv_", "a0", "a1", "tg")]
        s16 = P.sb("s16", [128, 64], F32)
        ob = P.sb("ob", [128, D], BF16)
        oT = P.sb("oT", [128, 8, 128], BF16)
        psY = P.ps("psY", [128, D])
        for (who, ntl, hb) in ((0, 16, C.hL), (1, 2, C.hC)):
            mod_tiles(P, cb, l, 1, who, 1.0)
            for ti in range(ntl):
                r0 = (ti + (16 if who == 1 else 0)) * 128
                for (tile_, src, nm, q) in ((y0, C.sY[0], "y0", "sp"), (y1, C.sY[1], "y1", "act"), (tr_, C.sR, "tr", "sp"), (tk_, C.sK, "tk", "act"),
                                            (tv_, C.sVv, "tv", "sp"), (a0, C.sA[0], "a0", "act"), (a1, C.sA[1], "a1", "sp"), (tg, C.sG, "tg", "act")):
                    P.dma(tile_[:], src[r0:r0 + 128, :], [], [nm], q=q)
                v3 = lambda t_: t_[:].rearrange("p (h n) -> p h n", n=64)
                P.tt("dve", y0[:], y0[:], y1[:], ALU.add, ["y0", "y1"], ["y0"])
                P.add("dve", lambda e, o_=s16[:, 0:16], i_=v3(y0): e.tensor_reduce(out=o_, in_=i_, axis=AX.X, op=ALU.add), ["y0"], ["mean"])
                P.ts("dve", s16[:, 0:16], s16[:, 0:16], 1.0 / 64, None, ALU.mult, None, ["mean"], ["mean"])
                bc16 = lambda c0: s16[:, c0:c0 + 16].unsqueeze(2).broadcast_to([128, 16, 64])
                P.tt("dve", v3(y0), v3(y0), bc16(0), ALU.subtract, ["y0", "mean"], ["y0"])
                P.tt("pool", y1[:], y0[:], y0[:], ALU.mult, ["y0", "y1"], ["y1"])
                P.add("dve", lambda e, o_=s16[:, 16:32], i_=v3(y1): e.tensor_reduce(out=o_, in_=i_, axis=AX.X, op=ALU.add), ["y1"], ["var"])
                P.ts("dve", s16[:, 16:32], s16[:, 16:32], 1.0 / 64, 64e-5, ALU.mult, ALU.add, ["var"], ["var"])
                P.act(s16[:, 16:32], s16[:, 16:32], AF.Sqrt, ["var"], ["var"])
                P.recip(s16[:, 16:32], s16[:, 16:32], ["var"], ["var"])
                P.tt("dve", v3(y0), v3(y0), bc16(16), ALU.mult, ["y0", "var"], ["y0"])
                P.tt("dve", y0[:], y0[:], lnw[:], ALU.mult, ["y0", "lnw"], ["y0"])
                P.tt("pool", y0[:], y0[:], lnb[:], ALU.add, ["y0", "lnb"], ["y0"])
                P.tt("dve", a0[:], a0[:], a1[:], ALU.add, ["a0", "a1"], ["a0"])
                P.stt("dve", a0[:], a0[:], -2.0, kat[:], ALU.add, ALU.mult, ["a0", "kat"], ["a0"])
                P.stt("dve", a0[:], a0[:], 2.0, tk_[:], ALU.add, ALU.mult, ["a0", "tk"], ["a0"])
                P.tt("pool", a0[:], a0[:], tr_[:], ALU.mult, ["a0", "tr"], ["a0"])
                P.tt("dve", a0[:], a0[:], rkt[:], ALU.mult, ["a0", "rkt"], ["a0"])
                P.add("dve", lambda e, o_=s16[:, 32:48], i_=v3(a0): e.tensor_reduce(out=o_, in_=i_, axis=AX.X, op=ALU.add), ["a0"], ["coef"])
                P.tt("pool", v3(a0), v3(tv_), bc16(32), ALU.mult, ["tv", "coef", "a0"], ["a0"])
                P.tt("dve", y0[:], y0[:], a0[:], ALU.add, ["y0", "a0"], ["y0"])
                P.tt("dve", ob[:], y0[:], tg[:], ALU.mult, ["y0", "tg"], ["ob"])
                for k in range(8):
                    P.tr(cb.psT[:, k, :], ob[:, k * 128:(k + 1) * 128], cb.ident[:], ["ob", "ident"], ["cpsT"])
                P.copy("act", oT[:], cb.psT[:], ["cpsT"], ["oT"])
                hap = hb[ti * 128:(ti + 1) * 128, :]
                x = cb.xb[cb.gi % 2]; xk = ("cxb", cb.gi % 2); cb.gi += 1
                P.dma(x[:], hap, [("h", who, ti)], [xk])
                for nh in range(2):
                    for k in range(8):
                        P.mm(psY[:, nh * 512:(nh + 1) * 512], oT[:, k, :], Wo[:, k, nh * 512:(nh + 1) * 512], k == 0, k == 7, ["oT", ("Wo", k)], ["psY"])
                epi_tile(P, cb, psY[:], ["psY"], x, xk, hap, ("h", who, ti))


def rwkv_phases(P, C, l, j, ctx_out):
    import os
    stop = int(os.environ.get("RW_STOP", "3"))
    rwkv_feat_phase(P, C, l, j)
    if stop >= 2:
        rwkv_scan_phase(P, C, l, j)
    if stop >= 3:
        rwkv_out_phase(P, C, l, j)
```

```python
import numpy as np
import ml_dtypes
from contextlib import ExitStack, contextmanager
import concourse.bass as bass
import concourse.mybir as mybir
from concourse.bass_utils import run_bass_kernel_spmd

F32 = mybir.dt.float32
BF16 = mybir.dt.bfloat16
AF = mybir.ActivationFunctionType
ALU = mybir.AluOpType
AX = mybir.AxisListType

D = 1024
T = 2048
TC = 256
NT = (T + TC) // 128
FF = 2816
NFC = FF // 128
DEPTH = 4
EPS = 1e-6


class Op:
    __slots__ = ("eng", "fn", "waits", "signal", "pos", "dma", "dslot", "dval", "know", "idx", "sigval")


class Prog:
    R = 14
    CE = ("pe", "dve", "act", "pool")
    QE = ("sp", "act", "pool")
    ALLE = ("pe", "dve", "act", "pool", "sp")

    def __init__(self, nc, stack):
        self.nc = nc
        self.sem = {e: stack.enter_context(nc.semaphore("s_" + e)) for e in self.CE}
        self.dsem = {q: [stack.enter_context(nc.semaphore(f"d_{q}{i}")) for i in range(self.R)] for q in self.QE}
        self.sigcount = {e: 0 for e in self.CE}
        self.dq = {q: {"n": 0, "hist": [None] * self.R, "val": [0] * self.R} for q in self.QE}
        self.ops = []
        self.tok = {}
        self.npos = {e: 0 for e in self.CE}
        self.know = {e: {} for e in self.ALLE}
        self.dknown = {e: set() for e in self.ALLE}
        self.phase_start = 0
        self.pstack = None
        self.nphase = 0
        self.same_engine_sync = True

    def sb(self, name, shape, dtype):
        return self.pstack.enter_context(self.nc.sbuf_tensor(f"{name}_{self.nphase}", list(shape), dtype))

    def ps(self, name, shape, dtype=F32):
        return self.pstack.enter_context(self.nc.psum_tensor(f"{name}_{self.nphase}", list(shape), dtype))

    @contextmanager
    def phase(self):
        with ExitStack() as st:
            self.pstack = st
            yield self
            self.flush()
            self.pstack = None
        self.nphase += 1

    def add(self, eng, fn, reads=(), writes=(), dma=False):
        op = Op()
        op.eng, op.fn, op.dma, op.signal, op.idx, op.sigval = eng, fn, dma, False, len(self.ops), None
        deps = set()
        for t in reads:
            s = self.tok.get(t)
            if s is not None and s[0] is not None:
                deps.add(s[0])
        for t in writes:
            s = self.tok.get(t)
            if s is not None:
                if s[0] is not None:
                    deps.add(s[0])
                deps.update(s[1])
        if dma:
            q = self.dq[eng]
            n = q["n"]
            q["n"] += 1
            slot = n % self.R
            op.dslot, op.dval = slot, 16 * (n // self.R + 1)
            q["val"][slot] = op.dval
            if q["hist"][slot] is not None:
                deps.add(q["hist"][slot])
            q["hist"][slot] = op.idx
        know, dk = self.know[eng], self.dknown[eng]
        waits = []
        for d in sorted(deps):
            if d < self.phase_start:
                continue
            Dp = self.ops[d]
            if Dp.dma:
                if d in dk:
                    continue
                dk.add(d)
                waits.append(d)
            else:
                if Dp.eng == eng and (eng == "pe" or not self.same_engine_sync):
                    continue
                if know.get(Dp.eng, -1) >= Dp.pos:
                    continue
                Dp.signal = True
                waits.append(d)
            for e2, p2 in Dp.know.items():
                if know.get(e2, -1) < p2:
                    know[e2] = p2
        op.waits = waits
        if dma:
            op.pos = None
            op.know = dict(know)
        else:
            op.pos = self.npos[eng]
            self.npos[eng] += 1
            op.know = dict(know)
            op.know[eng] = op.pos
        for t in reads:
            self.tok.setdefault(t, [None, []])[1].append(op.idx)
        for t in writes:
            self.tok[t] = [op.idx, []]
        self.ops.append(op)
        return op

    def flush(self):
        ops = self.ops[self.phase_start:]
        last = {}
        for op in ops:
            if not op.dma:
                last[op.eng] = op
        for op in last.values():
            op.signal = True
        for op in ops:
            if not op.dma and op.signal:
                self.sigcount[op.eng] += 1
                op.sigval = self.sigcount[op.eng]
        final_sig = dict(self.sigcount)
        final_d = {q: list(self.dq[q]["val"]) for q in self.QE}
        per = {e: [op for op in ops if op.eng == e] for e in self.ALLE}
        allops = self.ops
        sem, dsem = self.sem, self.dsem

        def mk(name):
            def body(e):
                for op in per[name]:
                    for d in op.waits:
                        Dp = allops[d]
                        if Dp.dma:
                            e.wait_ge(dsem[Dp.eng][Dp.dslot], Dp.dval)
                        else:
                            e.wait_ge(sem[Dp.eng], Dp.sigval)
                    ins = op.fn(e)
                    if op.dma:
                        ins.then_inc(dsem[op.eng][op.dslot], 16)
                    elif op.signal:
                        ins.then_inc(sem[op.eng], 1)
                for e2 in self.CE:
                    if final_sig[e2] > 0:
                        e.wait_ge(sem[e2], final_sig[e2])
                for q in self.QE:
                    for s in range(self.R):
                        if final_d[q][s] > 0:
                            e.wait_ge(dsem[q][s], final_d[q][s])
            return body

        with self.nc.Block() as blk:
            blk.tensor(mk("pe"))
            blk.vector(mk("dve"))
            blk.scalar(mk("act"))
            blk.gpsimd(mk("pool"))
            blk.sync(mk("sp"))
        self.phase_start = len(self.ops)
        for e in self.ALLE:
            self.know[e] = {c: self.npos[c] - 1 for c in self.CE}
            self.dknown[e] = set()

    def dma(self, out, in_, reads, writes, q="sp", **kw):
        return self.add(q, lambda e: e.dma_start(out=out, in_=in_, **kw), reads, writes, dma=True)

    def mm(self, out, lhsT, rhs, start, stop, reads, writes, r32=False):
        if r32 and lhsT.dtype == F32:
            lhsT = lhsT.bitcast(mybir.dt.float32r)
            rhs = rhs.bitcast(mybir.dt.float32r)
        return self.add("pe", lambda e: e.matmul(out, lhsT, rhs, start=start, stop=stop), reads, writes)

    def tr(self, out, in_, ident, reads, writes):
        if in_.dtype == F32:
            return self.add("pe", lambda e: e.matmul(out, in_, ident, start=True, stop=True), reads, writes)
        return self.add("pe", lambda e: e.transpose(out, in_, ident), reads, writes)

    def act(self, out, in_, func, reads, writes, **kw):
        return self.add("act", lambda e: e.activation(out=out, in_=in_, func=func, **kw), reads, writes)

    def tt(self, eng, out, in0, in1, op, reads, writes):
        return self.add(eng, lambda e: e.tensor_tensor(out=out, in0=in0, in1=in1, op=op), reads, writes)

    def ts(self, eng, out, in0, s1, s2, op0, op1, reads, writes):
        if s2 is None:
            return self.add(eng, lambda e: e.tensor_scalar(out=out, in0=in0, scalar1=s1, scalar2=None, op0=op0), reads, writes)
        return self.add(eng, lambda e: e.tensor_scalar(out=out, in0=in0, scalar1=s1, scalar2=s2, op0=op0, op1=op1), reads, writes)

    def stt(self, eng, out, in0, scalar, in1, op0, op1, reads, writes):
        return self.add(eng, lambda e: e.scalar_tensor_tensor(out=out, in0=in0, scalar=scalar, in1=in1, op0=op0, op1=op1), reads, writes)

    def recip(self, out, in_, reads, writes):
        return self.add("dve", lambda e: e.reciprocal(out=out, in_=in_), reads, writes)

    def copy(self, eng, out, in_, reads, writes):
        if eng == "act":
            return self.add("act", lambda e: e.activation(out=out, in_=in_, func=AF.Copy), reads, writes)
        return self.add(eng, lambda e: e.tensor_copy(out=out, in_=in_), reads, writes)


class Ctx:
    pass


def mod_phase(P, C):
    NB = 1152
    with P.phase():
        cc = P.sb("cc", [128, 8, 2], F32)
        sc = P.sb("sc", [128, 8, 2], F32)
        wb = [P.sb(f"mw{i}", [128, 8, NB], F32) for i in range(2)]
        bb = [P.sb(f"mb{i}", [2, NB], F32) for i in range(2)]
        rs = [P.sb(f"mr{i}", [2, NB], F32) for i in range(2)]
        pp = [P.ps(f"mp{i}", [2, 512]) for i in range(4)]
        P.dma(cc[:, :, 0], C.c.rearrange("o (k p) -> p (o k)", p=128), [], ["cc0"], allow_slow_non_contiguous=True)
        P.dma(cc[:, :, 1], C.c_ctx.rearrange("(k p) -> p k", p=128), [], ["cc1"], allow_slow_non_contiguous=True)
        P.act(sc[:], cc[:], AF.Silu, ["cc0", "cc1"], ["sc"])
        it = 0
        pi = 0
        for l in range(DEPTH):
            for cb in range(9216 // NB):
                w = wb[it % 2]
                wt = ("mw", it % 2)
                c0 = cb * NB
                P.dma(w[:], C.mod_w[l, :, c0:c0 + NB].rearrange("(k p) n -> p k n", p=128), [], [wt],
                      q=("sp" if it % 2 == 0 else "pool"))
                P.dma(bb[it % 2][:], C.mod_b[l:l + 1, c0:c0 + NB].broadcast_to([2, NB]), [], [("mb", it % 2)])
                for (o, n) in ((0, 512), (512, 512), (1024, 128)):
                    pt = pp[pi % 4]
                    ptk = ("mp", pi % 4)
                    for k in range(8):
                        P.mm(pt[:, 0:n], sc[:, k, :], w[:, k, o:o + n], k == 0, k == 7, ["sc", wt], [ptk])
                    P.tt("dve", rs[it % 2][:, o:o + n], pt[:, 0:n], bb[it % 2][:, o:o + n], ALU.add,
                         [ptk, ("mb", it % 2)], [("mr", it % 2, o)])
                    pi += 1
                P.dma(C.modv[l, :, c0:c0 + NB], rs[it % 2][:], [("mr", it % 2, 0), ("mr", it % 2, 512), ("mr", it % 2, 1024)],
                      [("modv", l, cb)])
                it += 1


def load_mod_tiles(P, C, l, sub, who, weight, A, Bt, G, tmp1, tmp2, key, k1, k2):
    base = sub * 3 * D
    def row(ap):
        return ap.broadcast_to([128, D])
    P.dma(Bt[:], row(C.modv[l, who:who + 1, base:base + D]), [], [("B", key)])
    P.dma(A[:], row(C.modv[l, who:who + 1, base + D:base + 2 * D]), [], [("A", key)])
    P.dma(G[:], row(C.modv[l, who:who + 1, base + 2 * D:base + 3 * D]), [], [("G", key)])
    P.dma(tmp1[:, 0:D], row(C.norm_pre[l, sub:sub + 1, :]), [], [k1])
    P.dma(tmp2[:, 0:D], row(C.norm_post[l, sub:sub + 1, :]), [], [k2])
    P.stt("dve", A[:], A[:], 1.0, tmp1[:, 0:D], ALU.add, ALU.mult, [("A", key), k1], [("A", key)])
    P.stt("dve", G[:], G[:], float(weight), tmp2[:, 0:D], ALU.mult, ALU.mult, [("G", key), k2], [("G", key)])


def rms_rstd(P, src, junk, ss, rstd, n, reads, key, eps=EPS):
    P.act(junk, src, AF.Square, reads + [], [("junk", key), ("ss", key)], accum_out=ss)
    P.ts("dve", rstd, ss, 1.0 / n, eps, ALU.mult, ALU.add, [("ss", key)], [("rstd", key)])
    P.act(rstd, rstd, AF.Sqrt, [("rstd", key)], [("rstd", key)])
    P.recip(rstd, rstd, [("rstd", key)], [("rstd", key)])


def ffn_phase(P, C, l, f, src, dst, ntiles_lat, do_ctx):
    sub = 0 if f == 0 else 2
    TB = 256
    with P.phase():
        Wg = P.sb("Wg", [128, 8, FF], BF16)
        Wu = P.sb("Wu", [128, 8, FF], BF16)
        Wd = P.sb("Wd", [128, NFC, D], BF16)
        stg = [P.sb(f"stg{i}", [128, 1024], F32) for i in range(2)]
        A = P.sb("A", [128, D], F32)
        Bt = P.sb("B", [128, D], F32)
        G = P.sb("G", [128, D], F32)
        xT = [P.sb(f"xT{i}", [128, 8, TB], BF16) for i in range(2)]
        hT = P.sb("hT", [128, NFC, TB], BF16)
        xb = [P.sb(f"xb{i}", [128, D], F32) for i in range(4)]
        xm = [P.sb(f"xm{i}", [128, D], BF16) for i in range(2)]
        tmp = P.sb("tmp", [128, D], F32)
        junk = P.sb("junk", [128, D], BF16)
        sg = [P.sb(f"sg{i}", [128, TB], F32) for i in range(3)]
        small = P.sb("small", [128, 16], F32)
        ident = P.sb("ident", [128, 128], BF16)
        psT = P.ps("psT", [128, 8, 128], BF16)
        psGU = [P.ps(f"psGU{i}", [128, 512]) for i in range(3)]
        psYs = [P.ps(f"psY{i}", [128, D]) for i in range(2)]
        P.dma(ident[:], C.ident_bf[:, :], [], ["ident"])
        cv = ["pool", "act", "dve"]
        qs = ["sp", "act", "sp"]
        n = 0
        NS = 2
        for (Wsrc, Wdst, nm) in ((C.ffn_w_gate[l, f], Wg, "Wg"), (C.ffn_w_up[l, f], Wu, "Wu")):
            for k in range(8):
                for (c0, cw) in ((0, 1024), (1024, 1024), (2048, 768)):
                    s_ = stg[n % NS]
                    P.dma(s_[:, 0:cw], Wsrc[k * 128:(k + 1) * 128, c0:c0 + cw], [], [("stg", n % NS)], q=qs[n % 3])
                    P.copy(cv[n % 3], Wdst[:, k, c0:c0 + cw], s_[:, 0:cw], [("stg", n % NS)], [(nm, k)])
                    n += 1
        for fc in range(NFC):
            s_ = stg[n % NS]
            P.dma(s_[:, 0:D], C.ffn_w_down[l, f, fc * 128:(fc + 1) * 128, :], [], [("stg", n % NS)], q=qs[n % 3])
            P.copy(cv[n % 3], Wd[:, fc, :], s_[:, 0:D], [("stg", n % NS)], [("Wd", fc)])
            n += 1
        Wg_r = [("Wg", k) for k in range(8)]
        Wu_r = [("Wu", k) for k in range(8)]
        blocks = [(0, ti) for ti in range(0, ntiles_lat, 2)]
        if do_ctx:
            blocks.append((1, 0))
        st_ = {"gi": 0, "who": None}
        tiles_of = {}

        def proA(bi):
            who, t0 = blocks[bi]
            if who != st_["who"]:
                load_mod_tiles(P, C, l, sub, who, 0.5, A, Bt, G, stg[0], stg[1], "m", ("stg", 0), ("stg", 1))
                st_["who"] = who
            tiles = []
            for j in range(2):
                ti = t0 + j
                x = xb[st_["gi"] % 4]
                xk = ("xb", st_["gi"] % 4)
                st_["gi"] += 1
                tiles.append((ti, x, xk))
                P.dma(x[:], src(who, ti), [("h", who, ti)], [xk])
                ss, rstd = small[:, 8 + 2 * j:9 + 2 * j], small[:, 9 + 2 * j:10 + 2 * j]
                rms_rstd(P, x[:], junk[:], ss, rstd, D, [xk], ("pre", j))
                P.stt("dve", tmp[:], x[:], rstd, A[:], ALU.mult, ALU.mult, [xk, ("rstd", ("pre", j)), ("A", "m")], ["tmp"])
                P.tt("dve", xm[j][:], tmp[:], Bt[:], ALU.add, ["tmp", ("B", "m")], [("xm", j)])
            tiles_of[bi] = tiles

        def proB(bi):
            xTb = xT[bi % 2]
            xtk = ("xT", bi % 2)
            for j in range(2):
                m = xm[j]
                for k in range(8):
                    P.tr(psT[:, k, :], m[:, k * 128:(k + 1) * 128], ident[:], [("xm", j), "ident"], ["psT"])
                P.copy("act", xTb[:, :, j * 128:(j + 1) * 128], psT[:], ["psT"], [(xtk, j)])

        def prologue(bi):
            proA(bi)
            proB(bi)

        prologue(0)
        for bi, (who, t0) in enumerate(blocks):
            xTb = xT[bi % 2]
            xtk = ("xT", bi % 2)
            tiles = tiles_of[bi]
            xr = [(xtk, 0), (xtk, 1)]
            nxt_same = bi + 1 < len(blocks) and blocks[bi + 1][0] == who
            for fc in range(NFC):
                if fc == 2 and nxt_same:
                    proA(bi + 1)
                if fc == 15 and nxt_same:
                    proB(bi + 1)
                pgu, pk = psGU[fc % 3], ("psGU", fc % 3)
                for k in range(8):
                    P.mm(pgu[:, 0:TB], Wg[:, k, fc * 128:(fc + 1) * 128], xTb[:, k, :], k == 0, k == 7,
                         [("Wg", k)] + xr, [pk])
                for k in range(8):
                    P.mm(pgu[:, TB:2 * TB], Wu[:, k, fc * 128:(fc + 1) * 128], xTb[:, k, :], k == 0, k == 7,
                         [("Wu", k)] + xr, [pk])
                P.act(sg[fc % 3][:], pgu[:, 0:TB], AF.Silu, [], [pk, ("sg", fc % 3)])
                P.tt("dve", hT[:, fc, :], sg[fc % 3][:], pgu[:, TB:2 * TB], ALU.mult, [("sg", fc % 3)], [pk, ("hT", fc)])
            for j, (ti, x, xk) in enumerate(tiles):
                psY, pyk = psYs[j], ("psY", j)
                for nh in range(2):
                    for fc in range(NFC):
                        P.mm(psY[:, nh * 512:(nh + 1) * 512], hT[:, fc, j * 128:(j + 1) * 128],
                             Wd[:, fc, nh * 512:(nh + 1) * 512], fc == 0, fc == NFC - 1,
                             [("hT", fc), ("Wd", fc)], [pyk])
                ss, rstd = small[:, 2 + 2 * j:3 + 2 * j], small[:, 3 + 2 * j:4 + 2 * j]
                rms_rstd(P, psY[:], junk[:], ss, rstd, D, [pyk], ("post", j))
                P.stt("dve", tmp[:], psY[:], rstd, G[:], ALU.mult, ALU.mult, [pyk, ("rstd", ("post", j)), ("G", "m")], ["tmp"])
                P.tt("pool", x[:], tmp[:], x[:], ALU.add, ["tmp", xk], [xk])
                P.dma(dst(who, ti), x[:], [xk], [("h", who, ti)])
            if bi + 1 < len(blocks) and not nxt_same:
                prologue(bi + 1)


class CB:
    pass


def alloc_common(P, nxb=2):
    cb = CB()
    cb.A = P.sb("cA", [128, D], F32)
    cb.Bt = P.sb("cB", [128, D], F32)
    cb.G = P.sb("cG", [128, D], F32)
    cb.t1 = P.sb("ct1", [128, D], F32)
    cb.t2 = P.sb("ct2", [128, D], F32)
    cb.xb = [P.sb(f"cxb{i}", [128, D], F32) for i in range(nxb)]
    cb.xm = [P.sb(f"cxm{i}", [128, D], BF16) for i in range(2)]
    cb.tmp = P.sb("ctmp", [128, D], F32)
    cb.junk = P.sb("cjunk", [128, D], BF16)
    cb.small = P.sb("csmall", [128, 16], F32)
    cb.ident = P.sb("cident", [128, 128], BF16)
    cb.psT = P.ps("cpsT", [128, 8, 128], BF16)
    cb.gi = 0
    cb.mi = 0
    P.dma(cb.ident[:], P.C.ident_bf[:, :], [], ["ident"])
    return cb


def mod_tiles(P, cb, l, sub, who, weight):
    load_mod_tiles(P, P.C, l, sub, who, weight, cb.A, cb.Bt, cb.G, cb.t1, cb.t2, "m", "ct1", "ct2")


def pro_tile(P, cb, src_ap, htok):
    x = cb.xb[cb.gi % len(cb.xb)]
    xk = ("cxb", cb.gi % len(cb.xb))
    cb.gi += 1
    P.dma(x[:], src_ap, [htok], [xk])
    ss, rstd = cb.small[:, 0:1], cb.small[:, 1:2]
    rms_rstd(P, x[:], cb.junk[:], ss, rstd, D, [xk], "pre")
    P.stt("dve", cb.tmp[:], x[:], rstd, cb.A[:], ALU.mult, ALU.mult, [xk, ("rstd", "pre"), ("A", "m")], ["ctmp"])
    j = cb.mi % 2
    cb.mi += 1
    m = cb.xm[j]
    P.tt("pool", m[:], cb.tmp[:], cb.Bt[:], ALU.add, ["ctmp", ("B", "m")], [("cxm", j)])
    for k in range(8):
        P.tr(cb.psT[:, k, :], m[:, k * 128:(k + 1) * 128], cb.ident[:], [("cxm", j), "ident"], ["cpsT"])
    return x, xk


def epi_tile(P, cb, psY, pkeys, x, xk, dst_ap, htok, bias=None):
    ss, rstd = cb.small[:, 2:3], cb.small[:, 3:4]
    src = psY
    sk = list(pkeys)
    if bias is not None:
        P.tt("dve", cb.t1[:], psY, bias[0], ALU.add, sk + [bias[1]], ["ct1"])
        src = cb.t1[:]
        sk = ["ct1"]
    rms_rstd(P, src, cb.junk[:], ss, rstd, D, sk, "post")
    P.stt("dve", cb.tmp[:], src, rstd, cb.G[:], ALU.mult, ALU.mult, sk + [("rstd", "post"), ("G", "m")], ["ctmp"])
    P.tt("pool", x[:], cb.tmp[:], x[:], ALU.add, ["ctmp", xk], [xk])
    P.dma(dst_ap, x[:], [xk], [htok])


def load_w_bf16(P, dst, src_ap, stg, nk, ncols, name, cnt):
    cv = ["pool", "act", "dve"]
    for k in range(nk):
        n = cnt[0]
        s = stg[n % 2]
        P.dma(s[:, 0:ncols], src_ap[k * 128:(k + 1) * 128, :], [], [("cstg", n % 2)], q=("sp" if n % 2 == 0 else "act"))
        P.copy(cv[n % 3], dst[:, k, 0:ncols], s[:, 0:ncols], [("cstg", n % 2)], [(name, k)])
        cnt[0] += 1


MLA_SCALE = 192.0 ** -0.5


def mla_proj_phase(P, C, l, j, ctx_q):
    with P.phase():
        cb = alloc_common(P)
        stg = [P.sb(f"stg{i}", [128, 1024], F32) for i in range(2)]
        Wdq = P.sb("Wdq", [128, 8, 512], BF16)
        Wdc = P.sb("Wdc", [128, 8, 256], BF16)
        Wdr = P.sb("Wdr", [128, 8, 128], BF16)
        Wdrp = P.sb("Wdrp", [128, 8, 128], BF16)
        Wqn = P.sb("Wqn", [128, 4, 1024], BF16)
        Wqr = P.sb("Wqr", [128, 4, 512], BF16)
        Wqrp = P.sb("Wqrp", [128, 4, 512], BF16)
        Wk = P.sb("Wk", [128, 2, 1024], BF16)
        Wv = P.sb("Wv", [128, 2, 1024], BF16)
        cnt = [0]
        load_w_bf16(P, Wdq, C.mla_w_dq[j], stg, 8, 512, "Wdq", cnt)
        load_w_bf16(P, Wdc, C.mla_dkv_c[j], stg, 8, 256, "Wdc", cnt)
        load_w_bf16(P, Wdr, C.mla_dkv_r[j], stg, 8, 128, "Wdr", cnt)
        load_w_bf16(P, Wdrp, C.mla_dkv_rp[j], stg, 8, 128, "Wdrp", cnt)
        load_w_bf16(P, Wqn, C.mla_uq_n[j], stg, 4, 1024, "Wqn", cnt)
        load_w_bf16(P, Wqr, C.mla_uq_r[j], stg, 4, 512, "Wqr", cnt)
        load_w_bf16(P, Wqrp, C.mla_uq_rp[j], stg, 4, 512, "Wqrp", cnt)
        load_w_bf16(P, Wk, C.mla_ukv_k[j], stg, 2, 1024, "Wk", cnt)
        load_w_bf16(P, Wv, C.mla_ukv_v[j], stg, 2, 1024, "Wv", cnt)
        gq = P.sb("gq", [128, 512], F32)
        gkv = P.sb("gkv", [128, 256], F32)
        P.dma(gq[:], C.mla_q_norm[j:j + 1, :].broadcast_to([128, 512]), [], ["gq"])
        P.dma(gkv[:], C.mla_kv_norm[j:j + 1, :].broadcast_to([128, 256]), [], ["gkv"])
        cos2 = P.sb("cos2", [128, T], F32)
        sin2 = P.sb("sin2", [128, T], F32)
        P.dma(cos2[:], C.rope_cos[:, :], [], ["cos2"])
        P.dma(sin2[:], C.rope_sin[:, :], [], ["sin2"], q="act")
        uT = P.sb("uT", [128, 8, 512], BF16)
        cqnT = P.sb("cqnT", [128, 4, 512], BF16)
        ckvnT = P.sb("ckvnT", [128, 2, 512], BF16)
        cqn = P.sb("cqn", [128, 512], BF16)
        ckvn = P.sb("ckvn", [128, 256], BF16)
        Vt = [P.sb(f"Vt{i}", [128, 1024], BF16) for i in range(2)]
        QTb = P.sb("QTb", [128, 8, 512], BF16)
        KTb = P.sb("KTb", [128, 8, 512], BF16)
        QrTb = P.sb("QrTb", [128, 4, 512], BF16)
        KrTb = P.sb("KrTb", [128, 512], BF16)
        r1 = P.sb("r1", [128, 512], F32)
        r2 = P.sb("r2", [128, 512], F32)
        sm2 = P.sb("sm2", [128, 8], F32)
        psq = P.ps("psq", [128, 512])
        pskv = P.ps("pskv", [128, 512])
        psT2 = P.ps("psT2", [128, 8, 128], BF16)
        psV = P.ps("psV", [128, 1024])
        psB = [P.ps(f"psB{i}", [128, 512]) for i in range(2)]
        blocks = [(0, t0, 4) for t0 in range(0, 16, 4)] + [(1, 0, 2)]
        cur = None
        bctr = 0
        for (who, t0, nt) in blocks:
            if who != cur:
                mod_tiles(P, cb, l, 1, who, 1.0)
                cur = who
            nq = nt * 128
            c0 = t0 * 128 + (T if who == 1 else 0)
            for jt in range(nt):
                ti = t0 + jt
                gt = ti + (16 if who == 1 else 0)
                src = (C.hL if who == 0 else C.hC)[ti * 128:(ti + 1) * 128, :]
                pro_tile(P, cb, src, ("h", who, ti))
                P.copy("act", uT[:, :, jt * 128:(jt + 1) * 128], cb.psT[:], ["cpsT"], [("uT", jt)])
                for k in range(8):
                    P.mm(psq[:, :], uT[:, k, jt * 128:(jt + 1) * 128], Wdq[:, k, :], k == 0, k == 7, [("uT", jt), ("Wdq", k)], ["psq"])
                for k in range(8):
                    P.mm(pskv[:, 0:256], uT[:, k, jt * 128:(jt + 1) * 128], Wdc[:, k, :], k == 0, k == 7, [("uT", jt), ("Wdc", k)], ["pskv"])
                P.act(cb.junk[:, 0:512], psq[:, :], AF.Square, ["psq"], [("junk", "q"), "ssq"], accum_out=sm2[:, 0:1])
                P.ts("dve", sm2[:, 1:2], sm2[:, 0:1], 1.0 / 512, EPS, ALU.mult, ALU.add, ["ssq"], ["rq"])
                P.act(sm2[:, 1:2], sm2[:, 1:2], AF.Sqrt, ["rq"], ["rq"])
                P.recip(sm2[:, 1:2], sm2[:, 1:2], ["rq"], ["rq"])
                P.stt("dve", cqn[:], psq[:, :], sm2[:, 1:2], gq[:], ALU.mult, ALU.mult, ["psq", "rq", "gq"], ["cqn"])
                P.act(cb.junk[:, 512:768], pskv[:, 0:256], AF.Square, ["pskv"], [("junk", "kv"), "sskv"], accum_out=sm2[:, 2:3])
                P.ts("dve", sm2[:, 3:4], sm2[:, 2:3], 1.0 / 256, EPS, ALU.mult, ALU.add, ["sskv"], ["rkv"])
                P.act(sm2[:, 3:4], sm2[:, 3:4], AF.Sqrt, ["rkv"], ["rkv"])
                P.recip(sm2[:, 3:4], sm2[:, 3:4], ["rkv"], ["rkv"])
                P.stt("dve", ckvn[:], pskv[:, 0:256], sm2[:, 3:4], gkv[:], ALU.mult, ALU.mult, ["pskv", "rkv", "gkv"], ["ckvn"])
                for r in range(4):
                    P.tr(psT2[:, r, :], cqn[:, r * 128:(r + 1) * 128], cb.ident[:], ["cqn", "ident"], ["psT2"])
                for r in range(2):
                    P.tr(psT2[:, 4 + r, :], ckvn[:, r * 128:(r + 1) * 128], cb.ident[:], ["ckvn", "ident"], ["psT2"])
                P.copy("act", cqnT[:, :, jt * 128:(jt + 1) * 128], psT2[:, 0:4, :], [], ["psT2", ("cqnT", jt)])
                P.copy("act", ckvnT[:, :, jt * 128:(jt + 1) * 128], psT2[:, 4:6, :], [], ["psT2", ("ckvnT", jt)])
                for nh in range(2):
                    for r in range(2):
                        P.mm(psV[:, nh * 512:(nh + 1) * 512], ckvnT[:, r, jt * 128:(jt + 1) * 128], Wv[:, r, nh * 512:(nh + 1) * 512],
                             r == 0, r == 1, [("ckvnT", jt), ("Wv", r)], ["psV"])
                vt = Vt[gt % 2]
                P.copy("act", vt[:], psV[:], ["psV"], [("Vt", gt % 2)])
                P.dma(C.sV[:, gt, :], vt[:], [("Vt", gt % 2)], [("sV", gt)])
            uTr = [("uT", jt) for jt in range(nt)]
            cqr = [("cqnT", jt) for jt in range(nt)]
            ckr = [("ckvnT", jt) for jt in range(nt)]
            for h in range(8):
                pb = psB[bctr % 2]; pk = ("psB", bctr % 2); bctr += 1
                for r in range(4):
                    P.mm(pb[:, 0:nq], Wqn[:, r, h * 128:(h + 1) * 128], cqnT[:, r, 0:nq], r == 0, r == 3, cqr + [("Wqn", r)], [pk])
                P.copy("act" if h % 2 == 0 else "dve", QTb[:, h, 0:nq], pb[:, 0:nq], [pk], [("QTb", h)])
                pb = psB[bctr % 2]; pk = ("psB", bctr % 2); bctr += 1
                for r in range(2):
                    P.mm(pb[:, 0:nq], Wk[:, r, h * 128:(h + 1) * 128], ckvnT[:, r, 0:nq], r == 0, r == 1, ckr + [("Wk", r)], [pk])
                P.copy("dve" if h % 2 == 0 else "act", KTb[:, h, 0:nq], pb[:, 0:nq], [pk], [("KTb", h)])
            jobs = [(Wqr, Wqrp, 4, hp, cqnT, cqr, "Wqr", "Wqrp", QrTb[:, hp, 0:nq], ("QrTb", hp)) for hp in range(4)]
            jobs.append((Wdr, Wdrp, 8, 0, uT, uTr, "Wdr", "Wdrp", KrTb[:, 0:nq], ("KrTb", 0)))
            for (Wa, Wb, nk, hp, rhsT, rk, na, nb_, dst, dk) in jobs:
                pa = psB[bctr % 2]; pka = ("psB", bctr % 2); bctr += 1
                for r in range(nk):
                    P.mm(pa[:, 0:nq], Wa[:, r, hp * 128:(hp + 1) * 128], rhsT[:, r, 0:nq], r == 0, r == nk - 1, rk + [(na, r)], [pka])
                if who == 1:
                    P.copy("act", dst, pa[:, 0:nq], [pka], [dk])
                    continue
                P.tt("dve", r1[:, 0:nq], pa[:, 0:nq], cos2[:, c0:c0 + nq], ALU.mult, [pka, "cos2"], ["r1"])
                pb = psB[bctr % 2]; pkb = ("psB", bctr % 2); bctr += 1
                for r in range(nk):
                    P.mm(pb[:, 0:nq], Wb[:, r, hp * 128:(hp + 1) * 128], rhsT[:, r, 0:nq], r == 0, r == nk - 1, rk + [(nb_, r)], [pkb])
                P.tt("dve", r2[:, 0:nq], pb[:, 0:nq], sin2[:, c0:c0 + nq], ALU.mult, [pkb, "sin2"], ["r2"])
                P.tt("pool", dst, r1[:, 0:nq], r2[:, 0:nq], ALU.add, ["r1", "r2"], [dk])
            P.dma(C.sQT[:, :, c0:c0 + nq], QTb[:, :, 0:nq], [("QTb", h) for h in range(8)], [("sQT", c0)])
            P.dma(C.sKT[:, :, c0:c0 + nq], KTb[:, :, 0:nq], [("KTb", h) for h in range(8)], [("sKT", c0)], q="act")
            P.dma(C.sQrT[:, :, c0:c0 + nq], QrTb[:, :, 0:nq], [("QrTb", hp) for hp in range(4)], [("sQrT", c0)])
            P.dma(C.sKrT[:, c0:c0 + nq], KrTb[:, 0:nq], [("KrTb", 0)], [("sKrT", c0)], q="act")


def mla_attn_phase(P, C, l, j, ctx_q):
    with P.phase():
        cb = alloc_common(P)
        stg = [P.sb(f"stg{i}", [128, 1024], F32) for i in range(2)]
        Wo = P.sb("Wo", [128, 8, 1024], BF16)
        cnt = [0]
        load_w_bf16(P, Wo, C.mla_w_o[j], stg, 8, 1024, "Wo", cnt)
        KT = P.sb("KT", [128, 8, T + TC], BF16)
        KrT = P.sb("KrT", [128, T + TC], BF16)
        V = P.sb("V", [128, NT, 1024], BF16)
        P.dma(KT[:], C.sKT[:, :, :], [], ["KT"])
        P.dma(KrT[:], C.sKrT[:, :], [], ["KrT"], q="act")
        P.dma(V[:, 0:9, :], C.sV[:, 0:9, :], [], ["V0"])
        P.dma(V[:, 9:18, :], C.sV[:, 9:18, :], [], ["V1"], q="act")
        ones = P.sb("ones", [128, 128], BF16)
        P.add("pool", lambda e: e.memset(ones[:], 1.0), [], ["ones"])
        QTb = [P.sb(f"QTb{i}", [128, 8, 512], BF16) for i in range(2)]
        QrTb = [P.sb(f"QrTb{i}", [128, 4, 512], BF16) for i in range(2)]
        OTb = P.sb("OTb", [128, 8, 512], BF16)
        PT = [P.sb(f"PT{i}", [128, 512], BF16) for i in range(4)]
        rz = P.sb("rz", [128, 512], F32)
        psS = [P.ps(f"psS{i}", [128, 512]) for i in range(3)]
        psO = P.ps("psO", [128, 512])
        psZ = P.ps("psZ", [128, 512])
        psY = P.ps("psY", [128, 1024])
        blocks = [(0, t0, 4, list(range(NT))) for t0 in range(0, 16, 4)]
        if ctx_q:
            blocks.append((1, 0, 2, [16, 17]))
        cur = None
        sc_ = 0
        pc_ = 0
        for bi, (who, t0, nt, kcs) in enumerate(blocks):
            if who != cur:
                mod_tiles(P, cb, l, 1, who, 1.0)
                cur = who
            nq = nt * 128
            c0 = t0 * 128 + (T if who == 1 else 0)
            qt, qr = QTb[bi % 2], QrTb[bi % 2]
            qk, qrk = ("QTb", bi % 2), ("QrTb", bi % 2)
            P.dma(qt[:, :, 0:nq], C.sQT[:, :, c0:c0 + nq], [], [qk])
            P.dma(qr[:, :, 0:nq], C.sQrT[:, :, c0:c0 + nq], [], [qrk], q="act")
            for h in range(8):
                pbase = (h % 2) * 64
                pend = []
                n = len(kcs)
                SK = 2
                for i in range(n + SK):
                    if i < n:
                        kc = kcs[i]
                        ps_ = psS[sc_ % 3]; psk = ("psS", sc_ % 3); sc_ += 1
                        P.mm(ps_[:, 0:nq], KT[:, h, kc * 128:(kc + 1) * 128], qt[:, h, 0:nq], True, False, ["KT", qk], [psk])
                        P.mm(ps_[:, 0:nq], KrT[pbase:pbase + 64, kc * 128:(kc + 1) * 128], qr[pbase:pbase + 64, h // 2, 0:nq],
                             False, True, ["KrT", qrk], [psk])
                        pt = PT[pc_ % 4]; ptk = ("PT", pc_ % 4); pc_ += 1
                        P.act(pt[:, 0:nq], ps_[:, 0:nq], AF.Exp, [psk], [ptk], scale=MLA_SCALE)
                        pend.append((kc, pt, ptk))
                    if i >= SK:
                        kc, pt, ptk = pend[i - SK]
                        P.mm(psO[:, 0:nq], V[:, kc, h * 128:(h + 1) * 128], pt[:, 0:nq], i == SK, i == n + SK - 1, ["V0", "V1", ptk], ["psO"])
                        P.mm(psZ[:, 0:nq], ones[:], pt[:, 0:nq], i == SK, i == n + SK - 1, ["ones", ptk], ["psZ"])
                P.recip(rz[:, 0:nq], psZ[:, 0:nq], ["psZ"], ["rz"])
                P.tt("dve", OTb[:, h, 0:nq], psO[:, 0:nq], rz[:, 0:nq], ALU.mult, ["psO", "rz"], [("OTb", h)])
            for jt in range(nt):
                ti = t0 + jt
                hap = (C.hL if who == 0 else C.hC)[ti * 128:(ti + 1) * 128, :]
                x = cb.xb[cb.gi % 2]; xk = ("cxb", cb.gi % 2); cb.gi += 1
                P.dma(x[:], hap, [("h", who, ti)], [xk])
                for nh in range(2):
                    for h in range(8):
                        P.mm(psY[:, nh * 512:(nh + 1) * 512], OTb[:, h, jt * 128:(jt + 1) * 128], Wo[:, h, nh * 512:(nh + 1) * 512],
                             h == 0, h == 7, [("OTb", h), ("Wo", h)], ["psY"])
                epi_tile(P, cb, psY[:], ["psY"], x, xk, hap, ("h", who, ti))


def _perm64():
    p = np.arange(64)
    return np.where((p % 32) < 16, p + 16, p - 16)


def host_layout(inputs):
    o = {}
    perm = _perm64()
    wdkv = inputs["mla_w_dkv"]
    o["mla_dkv_c"] = np.ascontiguousarray(wdkv[:, :, :256])
    r = wdkv[:, :, 256:320]
    o["mla_dkv_r"] = np.ascontiguousarray(np.concatenate([r, r], axis=-1))
    rp = r[:, :, perm]
    o["mla_dkv_rp"] = np.ascontiguousarray(np.concatenate([rp, rp], axis=-1))
    wuq = inputs["mla_w_uq"].reshape(-1, 512, 8, 192)
    o["mla_uq_n"] = np.ascontiguousarray(wuq[:, :, :, :128].reshape(-1, 512, 1024))
    o["mla_uq_r"] = np.ascontiguousarray(wuq[:, :, :, 128:].reshape(-1, 512, 512))
    o["mla_uq_rp"] = np.ascontiguousarray(wuq[:, :, :, 128:][:, :, :, perm].reshape(-1, 512, 512))
    wukv = inputs["mla_w_ukv"].reshape(-1, 256, 8, 256)
    o["mla_ukv_k"] = np.ascontiguousarray(wukv[:, :, :, :128].reshape(-1, 256, 1024))
    o["mla_ukv_v"] = np.ascontiguousarray(wukv[:, :, :, 128:].reshape(-1, 256, 1024))
    for k in ("mla_w_dq", "mla_q_norm", "mla_kv_norm", "mla_w_o", "c_ctx", "mod_w", "mod_b", "norm_pre", "norm_post",
              "ffn_w_gate", "ffn_w_up", "ffn_w_down"):
        o[k] = np.ascontiguousarray(inputs[k])
    o["ident_bf"] = np.eye(128, dtype=np.float32).astype(ml_dtypes.bfloat16)
    t = np.arange(T)
    p = np.arange(64)
    axis, half, pair = p // 32, (p % 32) // 16, p % 16
    inv = (np.float32(10000.0) ** (-(pair.astype(np.float32)) / np.float32(16.0))).astype(np.float32)
    pos = np.where(axis[:, None] == 0, (t // 64)[None, :], (t % 64)[None, :]).astype(np.float32)
    ang = (pos * inv[:, None]).astype(np.float32)
    cs = np.cos(ang).astype(np.float32)
    sn = (np.sin(ang) * np.where(half == 0, -1.0, 1.0)[:, None]).astype(np.float32)
    def dft(n):
        k = (np.arange(n)[:, None] * np.arange(n)[None, :]) % n
        a = 2.0 * np.pi * k.astype(np.float64) / n
        return np.cos(a), np.sin(a)
    bf = ml_dtypes.bfloat16
    c_, s_ = dft(T)
    o["dft_ct"] = c_.astype(np.float32).astype(bf); o["dft_st"] = s_.astype(np.float32).astype(bf)
    c_, s_ = dft(TC)
    o["dft_ct_c"] = c_.astype(np.float32).astype(bf); o["dft_st_c"] = s_.astype(np.float32).astype(bf)
    c_, s_ = dft(128)
    o["dft_cc"] = c_.astype(np.float32).astype(bf); o["dft_scn"] = (-s_).astype(np.float32).astype(bf)
    for k in ("fnet_w_o", "fnet_b_o", "rwkv_mix", "rwkv_w_r", "rwkv_w_k", "rwkv_w_v", "rwkv_w0", "rwkv_w2", "rwkv_a0", "rwkv_a2",
              "rwkv_g1", "rwkv_g2", "rwkv_k_k", "rwkv_k_a", "rwkv_r_k", "rwkv_ln_w", "rwkv_ln_b", "rwkv_w_o"):
        o[k] = np.ascontiguousarray(inputs[k])
    o["rwkv_w1c"] = np.ascontiguousarray(np.concatenate([inputs["rwkv_w1"][:, 0], inputs["rwkv_w1"][:, 1]], axis=-1))
    o["rwkv_a1c"] = np.ascontiguousarray(np.concatenate([inputs["rwkv_a1"][:, 0], inputs["rwkv_a1"][:, 1]], axis=-1))
    o["ident_f"] = np.eye(128, dtype=np.float32)
    ii = np.arange(128)
    s_, t_ = ii[:, None], ii[None, :]
    tri0 = (s_ <= t_).astype(np.float32); tri1 = (s_ >= t_).astype(np.float32)
    st0 = (s_ < t_).astype(np.float32); st1 = (s_ > t_).astype(np.float32)
    o["rw_tri"] = np.stack([tri0, tri1])
    rep4 = lambda m: np.ascontiguousarray(np.concatenate([m] * 4, axis=1))
    o["rw_mS"] = np.stack([rep4(st0), rep4(st1)])
    o["rw_mI"] = np.stack([rep4(tri0), rep4(tri1)])
    bd = np.kron(np.eye(4, dtype=np.float32), np.ones((32, 32), np.float32))
    o["rw_mSd"] = np.stack([rep4(st0 * bd), rep4(st1 * bd)])
    o["rw_mSo"] = np.stack([rep4(st0 * (1 - bd)), rep4(st1 * (1 - bd))])
    o["rw_mTd"] = np.stack([rep4(st0.T * bd), rep4(st1.T * bd)])
    o["rw_I4"] = rep4(np.eye(128, dtype=np.float32))
    o["rope_cos"] = np.ascontiguousarray(np.concatenate([cs, cs], axis=0))
    o["rope_sin"] = np.ascontiguousarray(np.concatenate([sn, sn], axis=0))
    return o


IN_SHAPES = {
    "c_ctx": ([D], F32), "mod_w": ([DEPTH, D, 9 * D], F32), "mod_b": ([DEPTH, 9 * D], F32),
    "norm_pre": ([DEPTH, 3, D], F32), "norm_post": ([DEPTH, 3, D], F32),
    "ffn_w_gate": ([DEPTH, 2, D, FF], F32), "ffn_w_up": ([DEPTH, 2, D, FF], F32), "ffn_w_down": ([DEPTH, 2, FF, D], F32),
    "ident_bf": ([128, 128], BF16), "rope_cos": ([128, T], F32), "rope_sin": ([128, T], F32),
    "mla_w_dq": ([2, D, 512], F32), "mla_q_norm": ([2, 512], F32), "mla_kv_norm": ([2, 256], F32), "mla_w_o": ([2, D, D], F32),
    "mla_dkv_c": ([2, D, 256], F32), "mla_dkv_r": ([2, D, 128], F32), "mla_dkv_rp": ([2, D, 128], F32),
    "mla_uq_n": ([2, 512, 1024], F32), "mla_uq_r": ([2, 512, 512], F32), "mla_uq_rp": ([2, 512, 512], F32),
    "mla_ukv_k": ([2, 256, 1024], F32), "mla_ukv_v": ([2, 256, 1024], F32),
    "fnet_w_o": ([1, D, D], F32), "fnet_b_o": ([1, D], F32),
    "dft_ct": ([T, T], BF16), "dft_st": ([T, T], BF16), "dft_ct_c": ([TC, TC], BF16), "dft_st_c": ([TC, TC], BF16),
    "dft_cc": ([128, 128], BF16), "dft_scn": ([128, 128], BF16),
    "rwkv_mix": ([1, 6, D], F32), "rwkv_w_r": ([1, D, D], F32), "rwkv_w_k": ([1, D, D], F32), "rwkv_w_v": ([1, D, D], F32),
    "rwkv_w0": ([1, 2, D], F32), "rwkv_w2": ([1, 2, 64, D], F32), "rwkv_a0": ([1, 2, D], F32), "rwkv_a2": ([1, 2, 64, D], F32),
    "rwkv_g1": ([1, D, 160], F32), "rwkv_g2": ([1, 160, D], F32), "rwkv_k_k": ([1, D], F32), "rwkv_k_a": ([1, D], F32),
    "rwkv_r_k": ([1, 16, 64], F32), "rwkv_ln_w": ([1, D], F32), "rwkv_ln_b": ([1, D], F32), "rwkv_w_o": ([1, D, D], F32),
    "rwkv_w1c": ([1, D, 128], F32), "rwkv_a1c": ([1, D, 128], F32), "ident_f": ([128, 128], F32),
    "rw_tri": ([2, 128, 128], F32), "rw_mS": ([2, 128, 512], F32), "rw_mI": ([2, 128, 512], F32), "rw_mSd": ([2, 128, 512], F32), "rw_mSo": ([2, 128, 512], F32),
    "rw_mTd": ([2, 128, 512], F32), "rw_I4": ([128, 512], F32),
}


def build(nsteps=None, dbg=False):
    nc = bass.Bass("TRN2", target_bir_lowering=False)
    C = Ctx()

    def din(name, shape, dt=F32):
        return nc.dram_tensor(name, list(shape), dt, kind="ExternalInput").ap()

    def scratch(name, shape, dt=F32):
        return nc.dram_tensor(name, list(shape), dt, kind="Internal").ap()

    C.x = din("x", [T, D])
    C.c = din("c", [1, D])
    C.ctx = din("ctx", [TC, D])
    for k, (shp, dt) in IN_SHAPES.items():
        setattr(C, k, din(k, shp, dt))
    C.out = nc.dram_tensor("out", [T, D], F32, kind="ExternalOutput").ap()
    C.modv = scratch("modv", [DEPTH, 2, 9 * D])
    C.hL = scratch("hL", [T, D])
    C.hC = scratch("hC", [TC, D])
    C.sQT = scratch("sQT", [128, 8, T + TC], BF16)
    C.sQrT = scratch("sQrT", [128, 4, T + TC], BF16)
    C.sKT = scratch("sKT", [128, 8, T + TC], BF16)
    C.sKrT = scratch("sKrT", [128, T + TC], BF16)
    C.sV = scratch("sV", [128, NT, 1024], BF16)
    for nm in ("sR", "sK", "sVv", "sKK", "sG"):
        setattr(C, nm, scratch(nm, [NTOK, D]))
    C.sLW = scratch("sLW", [2, NTOK, D])
    C.sA = scratch("sA", [2, NTOK, D])
    C.sY = scratch("sY", [2, NTOK, D])
    if dbg:
        C.dbg_hC = nc.dram_tensor("dbg_hC", [TC, D], F32, kind="ExternalOutput").ap()

    def tile_ap(base_l, base_c):
        def f(who, ti):
            b = base_l if who == 0 else base_c
            return b[ti * 128:(ti + 1) * 128, :]
        return f

    steps = []
    for l in range(DEPTH):
        steps += [("ffn", l, 0), ("mix", l), ("ffn", l, 1)]
    if nsteps is not None:
        steps = steps[:nsteps]
    with ExitStack() as st:
        P = Prog(nc, st)
        P.C = C
        mod_phase(P, C)
        for si, stp in enumerate(steps):
            l = stp[1]
            kind, j, last = l % 3, l // 3, l == DEPTH - 1
            final = (nsteps is None and si == len(steps) - 1)
            if stp[0] == "ffn":
                f = stp[2]
                src = tile_ap(C.x, C.ctx) if si == 0 else tile_ap(C.hL, C.hC)
                dst = tile_ap(C.out, C.hC) if final else tile_ap(C.hL, C.hC)
                ffn_phase(P, C, l, f, src, dst, T // 128, (f == 0) or (not last))
            else:
                if kind == 0:
                    mla_proj_phase(P, C, l, j, not last)
                    mla_attn_phase(P, C, l, j, not last)
                elif kind == 1:
                    fnet_phase(P, C, l, j, not last)
                else:
                    rwkv_phases(P, C, l, j, not last)
        if nsteps is not None:
            with P.phase():
                P.dma(C.out[:, :], C.hL[:, :], [], ["dm"])
                if dbg:
                    P.dma(C.dbg_hC[:, :], C.hC[:, :], [], ["dh"])
    return nc


_NC_CACHE = {}


def make_in_maps(inputs, ncores=8):
    shared = host_layout(inputs)
    in_maps = []
    for b in range(ncores):
        m = dict(shared)
        m["x"] = np.ascontiguousarray(inputs["x"][b])
        m["c"] = np.ascontiguousarray(inputs["c"][b:b + 1])
        m["ctx"] = np.ascontiguousarray(inputs["ctx"][b])
        in_maps.append(m)
    return in_maps


def kernel(**inputs):
    inputs = {k: np.asarray(v) for k, v in inputs.items()}
    if "nc" not in _NC_CACHE:
        _NC_CACHE["nc"] = build()
    nc = _NC_CACHE["nc"]
    in_maps = make_in_maps(inputs, 8)
    res = run_bass_kernel_spmd(nc, in_maps, core_ids=list(range(8)))
    return np.stack([np.asarray(r["out"]) for r in res.results], axis=0).astype(np.float32)


def fnet_phase(P, C, l, j, ctx_out):
    with P.phase():
        cb = alloc_common(P)
        stg = [P.sb(f"stg{i}", [128, 1024], F32) for i in range(2)]
        Wo = P.sb("Wo", [128, 8, 1024], BF16)
        cnt = [0]
        load_w_bf16(P, Wo, C.fnet_w_o[j], stg, 8, 1024, "Wo", cnt)
        bo = P.sb("bo", [128, D], F32)
        P.dma(bo[:], C.fnet_b_o[j:j + 1, :].broadcast_to([128, D]), [], ["bo"])
        cc = P.sb("cc", [128, 128], BF16)
        scn = P.sb("scn", [128, 128], BF16)
        P.dma(cc[:], C.dft_cc[:, :], [], ["cc"])
        P.dma(scn[:], C.dft_scn[:, :], [], ["scn"])
        Zc = P.sb("Zc", [128, 16, 1024], BF16)
        Zs = P.sb("Zs", [128, 16, 1024], BF16)
        NQ = 256
        CTb = [P.sb(f"CTb{i}", [128, 16, NQ], BF16) for i in range(2)]
        STb = [P.sb(f"STb{i}", [128, 16, NQ], BF16) for i in range(2)]
        uT = P.sb("uT", [128, 8, 128], BF16)
        FT = P.sb("FT", [128, 8, NQ], BF16)
        psZ = [P.ps(f"psZ{i}", [128, 1024]) for i in range(2)]
        psF = [P.ps(f"psF{i}", [128, 512]) for i in range(1)]
        psY = P.ps("psY", [128, 1024])
        groups = [(0, T, C.hL, C.dft_ct, C.dft_st)]
        if ctx_out:
            groups.append((1, TC, C.hC, C.dft_ct_c, C.dft_st_c))
        bi = 0
        fi = 0
        for (who, Tt, hbase, ct, st_) in groups:
            ntl = Tt // 128
            mod_tiles(P, cb, l, 1, who, 1.0)
            scale = float((Tt * 128) ** -0.5)
            for ti in range(ntl):
                pro_tile(P, cb, hbase[ti * 128:(ti + 1) * 128, :], ("h", who, ti))
                P.copy("act", uT[:], cb.psT[:], ["cpsT"], ["uT"])
                for (tab, tk, pz, pzk, Zd, zk, ce) in ((cc, "cc", psZ[0], "psZ0", Zc, "Zc", "act"), (scn, "scn", psZ[1], "psZ1", Zs, "Zs", "dve")):
                    for g in range(8):
                        P.mm(pz[:, g * 128:(g + 1) * 128], uT[:, g, :], tab[:], True, True, ["uT", tk], [pzk])
                    P.copy(ce, Zd[:, ti, :], pz[:], [pzk], [(zk, ti)])
            zr = [("Zc", ti) for ti in range(ntl)] + [("Zs", ti) for ti in range(ntl)]
            for b0 in range(0, Tt, NQ):
                cbuf, sbuf_ = CTb[bi % 2], STb[bi % 2]
                ck, sk = ("CTb", bi % 2), ("STb", bi % 2)
                bi += 1
                P.dma(cbuf[:, 0:ntl, :], ct[:, b0:b0 + NQ].rearrange("(c p) n -> p c n", p=128), [], [ck])
                P.dma(sbuf_[:, 0:ntl, :], st_[:, b0:b0 + NQ].rearrange("(c p) n -> p c n", p=128), [], [sk], q="act")
                for g in range(8):
                    pf = psF[0]; pfk = ("psF", 0); fi += 1
                    for tc_ in range(ntl):
                        P.mm(pf[:, 0:NQ], Zc[:, tc_, g * 128:(g + 1) * 128], cbuf[:, tc_, :], tc_ == 0, False, zr + [ck], [pfk])
                        P.mm(pf[:, 0:NQ], Zs[:, tc_, g * 128:(g + 1) * 128], sbuf_[:, tc_, :], False, tc_ == ntl - 1, zr + [sk], [pfk])
                    P.act(FT[:, g, :], pf[:, 0:NQ], AF.Copy, [pfk], [("FT", g)], scale=scale)
                for jt in range(NQ // 128):
                    ti = b0 // 128 + jt
                    hap = hbase[ti * 128:(ti + 1) * 128, :]
                    x = cb.xb[cb.gi % 2]; xk = ("cxb", cb.gi % 2); cb.gi += 1
                    P.dma(x[:], hap, [("h", who, ti)], [xk])
                    for nh in range(2):
                        for g in range(8):
                            P.mm(psY[:, nh * 512:(nh + 1) * 512], FT[:, g, jt * 128:(jt + 1) * 128], Wo[:, g, nh * 512:(nh + 1) * 512],
                                 g == 0, g == 7, [("FT", g), ("Wo", g)], ["psY"])
                    epi_tile(P, cb, psY[:], ["psY"], x, xk, hap, ("h", who, ti), bias=(bo[:], "bo"))


NTOK = T + TC
DECAY_C = float(np.exp(-0.5))


def rwkv_feat_phase(P, C, l, j):
    UW = 2308
    with P.phase():
        cb = alloc_common(P)
        stg = [P.sb(f"stg{i}", [128, 1024], F32) for i in range(2)]
        uT = P.sb("uT", [128, 8, UW], BF16)
        xxT = P.sb("xxT", [128, 8, 256], BF16)
        tmpw = P.sb("tmpw", [128, 256], F32)
        P.add("pool", lambda e: e.memset(uT[:], 0.0), [], ["uTall"])
        Wr = P.sb("Wr", [128, 8, 1024], BF16)
        Wk = P.sb("Wk", [128, 8, 1024], BF16)
        Wv = P.sb("Wv", [128, 8, 1024], BF16)
        W1 = P.sb("W1", [128, 8, 128], BF16)
        A1 = P.sb("A1", [128, 8, 128], BF16)
        G1 = P.sb("G1", [128, 8, 160], BF16)
        W2 = P.sb("W2", [128, 1, 1024], BF16)
        A2 = P.sb("A2", [128, 1, 1024], BF16)
        G2 = P.sb("G2", [128, 2, 1024], BF16)
        cnt = [0]
        load_w_bf16(P, Wr, C.rwkv_w_r[j], stg, 8, 1024, "Wr", cnt)
        load_w_bf16(P, Wk, C.rwkv_w_k[j], stg, 8, 1024, "Wk", cnt)
        load_w_bf16(P, Wv, C.rwkv_w_v[j], stg, 8, 1024, "Wv", cnt)
        load_w_bf16(P, W1, C.rwkv_w1c[j], stg, 8, 128, "W1", cnt)
        load_w_bf16(P, A1, C.rwkv_a1c[j], stg, 8, 128, "A1", cnt)
        load_w_bf16(P, G1, C.rwkv_g1[j], stg, 8, 160, "G1", cnt)
        load_w_bf16(P, W2, C.rwkv_w2[j].rearrange("e r d -> (e r) d"), stg, 1, 1024, "W2", cnt)
        load_w_bf16(P, A2, C.rwkv_a2[j].rearrange("e r d -> (e r) d"), stg, 1, 1024, "A2", cnt)
        load_w_bf16(P, G2, C.rwkv_g2[j, 0:128, :], stg, 1, 1024, "G2", cnt)
        n = cnt[0]; s = stg[n % 2]
        P.dma(s[0:32, :], C.rwkv_g2[j, 128:160, :], [], [("cstg", n % 2)])
        P.copy("dve", G2[0:32, 1, :], s[0:32, :], [("cstg", n % 2)], [("G2", 1)])
        cnt[0] += 1
        mixc = P.sb("mixc", [128, 6, 8], F32)
        for m_ in range(6):
            P.dma(mixc[:, m_, :], C.rwkv_mix[j, m_].rearrange("(k p) -> p k", p=128), [], ["mixc"], allow_slow_non_contiguous=True)
        w0t = [P.sb(f"w0t{e}", [128, D], F32) for e in range(2)]
        a0t = [P.sb(f"a0t{e}", [128, D], F32) for e in range(2)]
        kkt = cb.t1
        for e in range(2):
            P.dma(w0t[e][:], C.rwkv_w0[j, e:e + 1, :].broadcast_to([128, D]), [], [("w0t", e)])
            P.dma(a0t[e][:], C.rwkv_a0[j, e:e + 1, :].broadcast_to([128, D]), [], [("a0t", e)], q="act")
        def ucol(who, ti):
            return (1 if who == 0 else 2051) + ti * 128
        for (who, ntl, hb) in ((0, 16, C.hL), (1, 2, C.hC)):
            mod_tiles(P, cb, l, 1, who, 1.0)
            for ti in range(ntl):
                pro_tile(P, cb, hb[ti * 128:(ti + 1) * 128, :], ("h", who, ti))
                c0 = ucol(who, ti)
                P.copy("act", uT[:, :, c0:c0 + 128], cb.psT[:], ["cpsT", "uTall"], [("uTt", who, ti)])
        allu = [("uTt", 0, ti) for ti in range(16)] + [("uTt", 1, ti) for ti in range(2)]
        P.dma(kkt[:], C.rwkv_k_k[j:j + 1, :].broadcast_to([128, D]), [], ["kkt", "ct1"])
        xm = [P.sb(f"xm{m}", [128, 8, 256], BF16) for m in range(6)]
        hW = P.sb("hW", [128, 256], BF16)
        hA = P.sb("hA", [128, 256], BF16)
        hG = P.sb("hG", [128, 2, 256], BF16)
        ot = [P.sb(f"ot{i}", [128, D], F32) for i in range(2)]
        sq = cb.tmp
        s16 = P.sb("s16", [128, 32], F32)
        psH = [P.ps(f"psH{i}", [128, 512]) for i in range(2)]
        psO = [P.ps(f"psO{i}", [128, 1024]) for i in range(2)]
        oc = [0]
        pc = [0]

        def out_tile(name_key):
            i = oc[0] % 2; oc[0] += 1
            return ot[i], ("ot", i)

        def ps_tile():
            i = pc[0] % 2; pc[0] += 1
            return psO[i], ("psO", i)

        blocks = [(0, t0, 2) for t0 in range(0, 16, 2)] + [(1, 0, 2)]
        hc_ = 0
        for (who, t0, nt) in blocks:
            nq = nt * 128
            c0 = ucol(who, t0)
            g0 = (t0 + (16 if who == 1 else 0)) * 128
            for k in range(8):
                P.tt("dve", tmpw[:, 0:nq], uT[:, k, c0 - 1:c0 - 1 + nq], uT[:, k, c0 + 1:c0 + 1 + nq], ALU.add, allu, ["tmpw"])
                P.stt("dve", xxT[:, k, 0:nq], tmpw[:, 0:nq], 0.5, uT[:, k, c0:c0 + nq], ALU.mult, ALU.subtract, ["tmpw"] + allu, [("xxT", k)])
            allx = [("xxT", k) for k in range(8)]
            for m in range(6):
                for k in range(8):
                    P.stt("dve" if (m + k) % 2 == 0 else "dve", xm[m][:, k, 0:nq], xxT[:, k, 0:nq], mixc[:, m, k:k + 1], uT[:, k, c0:c0 + nq],
                          ALU.mult, ALU.add, allx + allu + ["mixc"], [("xm", m, k)])
            xr_ = lambda m: [("xm", m, k) for k in range(8)]
            ph = psH[hc_ % 2]; phk = ("psH", hc_ % 2); hc_ += 1
            for k in range(8):
                P.mm(ph[:, 0:nq], W1[:, k, :], xm[1][:, k, 0:nq], k == 0, k == 7, xr_(1) + [("W1", k)], [phk])
            P.act(hW[:, 0:nq], ph[:, 0:nq], AF.Tanh, [phk], ["hW"])
            ph = psH[hc_ % 2]; phk = ("psH", hc_ % 2); hc_ += 1
            for k in range(8):
                P.mm(ph[:, 0:nq], A1[:, k, :], xm[4][:, k, 0:nq], k == 0, k == 7, xr_(4) + [("A1", k)], [phk])
            P.copy("dve", hA[:, 0:nq], ph[:, 0:nq], [phk], ["hA"])
            for (gi_, lo, hi) in ((0, 0, 128), (1, 128, 160)):
                ph = psH[hc_ % 2]; phk = ("psH", hc_ % 2); hc_ += 1
                for k in range(8):
                    P.mm(ph[0:hi - lo, 0:nq], G1[:, k, lo:hi], xm[5][:, k, 0:nq], k == 0, k == 7, xr_(5) + [("G1", k)], [phk])
                P.act(hG[0:hi - lo, gi_, 0:nq], ph[0:hi - lo, 0:nq], AF.Sigmoid, [phk], [("hG", gi_)])
            for jt in range(nt):
                r0 = g0 + jt * 128
                cs = slice(jt * 128, (jt + 1) * 128)
                for (m, Wm, wn, dstA) in ((0, Wr, "Wr", C.sR), (3, Wv, "Wv", C.sVv)):
                    pt, ptk = ps_tile()
                    for nh in range(2):
                        for k in range(8):
                            P.mm(pt[:, nh * 512:(nh + 1) * 512], xm[m][:, k, cs], Wm[:, k, nh * 512:(nh + 1) * 512], k == 0, k == 7,
                                 xr_(m) + [(wn, k)], [ptk])
                    o, ok = out_tile(0)
                    P.copy("act", o[:], pt[:], [ptk], [ok])
                    P.dma(dstA[r0:r0 + 128, :], o[:], [ok], [(wn, "out", r0)])
                pt, ptk = ps_tile()
                for nh in range(2):
                    for k in range(8):
                        P.mm(pt[:, nh * 512:(nh + 1) * 512], xm[2][:, k, cs], Wk[:, k, nh * 512:(nh + 1) * 512], k == 0, k == 7,
                             xr_(2) + [("Wk", k)], [ptk])
                o, ok = out_tile(0)
                P.copy("act", o[:], pt[:], [ptk], [ok])
                P.dma(C.sK[r0:r0 + 128, :], o[:], [ok], [("k", "out", r0)])
                o2, ok2 = out_tile(0)
                P.tt("dve", o2[:], o[:], kkt[:], ALU.mult, [ok, "kkt"], [ok2])
                P.tt("pool", sq[:], o2[:], o2[:], ALU.mult, [ok2], ["ctmp"])
                P.add("dve", lambda e, o_=s16[:, 0:16], i_=sq[:].rearrange("p (h n) -> p h n", n=64): e.tensor_reduce(out=o_, in_=i_, axis=AX.X, op=ALU.add), ["ctmp"], ["s16"])
                P.act(s16[:, 0:16], s16[:, 0:16], AF.Sqrt, ["s16"], ["s16"])
                P.ts("dve", s16[:, 0:16], s16[:, 0:16], 1e-12, None, ALU.max, None, ["s16"], ["s16"])
                P.recip(s16[:, 16:32], s16[:, 0:16], ["s16"], ["s16r"])
                o23 = o2[:].rearrange("p (h n) -> p h n", n=64)
                P.tt("dve", o23, o23, s16[:, 16:32].unsqueeze(2).broadcast_to([128, 16, 64]), ALU.mult, [ok2, "s16r"], [ok2])
                P.dma(C.sKK[r0:r0 + 128, :], o2[:], [ok2], [("kk", "out", r0)])
                pt, ptk = ps_tile()
                for nh in range(2):
                    P.mm(pt[:, nh * 512:(nh + 1) * 512], hG[:, 0, cs], G2[:, 0, nh * 512:(nh + 1) * 512], True, False, [("hG", 0), ("hG", 1), ("G2", 0)], [ptk])
                    P.mm(pt[:, nh * 512:(nh + 1) * 512], hG[0:32, 1, cs], G2[0:32, 1, nh * 512:(nh + 1) * 512], False, True, [("hG", 1), ("G2", 1)], [ptk])
                o, ok = out_tile(0)
                P.copy("act", o[:], pt[:], [ptk], [ok])
                P.dma(C.sG[r0:r0 + 128, :], o[:], [ok], [("g", "out", r0)])
                for e in range(2):
                    pt, ptk = ps_tile()
                    for nh in range(2):
                        P.mm(pt[:, nh * 512:(nh + 1) * 512], hW[e * 64:(e + 1) * 64, cs], W2[e * 64:(e + 1) * 64, 0, nh * 512:(nh + 1) * 512], True, True,
                             ["hW", ("W2", 0)], [ptk])
                    o, ok = out_tile(0)
                    P.tt("dve", o[:], pt[:], w0t[e][:], ALU.add, [ptk, ("w0t", e)], [ok])
                    P.act(o[:], o[:], AF.Sigmoid, [ok], [ok])
                    P.ts("dve", o[:], o[:], -DECAY_C, None, ALU.mult, None, [ok], [ok])
                    P.dma(C.sLW[e, r0:r0 + 128, :], o[:], [ok], [("lw", e, r0)])
                    pt, ptk = ps_tile()
                    for nh in range(2):
                        P.mm(pt[:, nh * 512:(nh + 1) * 512], hA[e * 64:(e + 1) * 64, cs], A2[e * 64:(e + 1) * 64, 0, nh * 512:(nh + 1) * 512], True, True,
                             ["hA", ("A2", 0)], [ptk])
                    o, ok = out_tile(0)
                    P.tt("dve", o[:], pt[:], a0t[e][:], ALU.add, [ptk, ("a0t", e)], [ok])
                    P.act(o[:], o[:], AF.Sigmoid, [ok], [ok])
                    P.dma(C.sA[e, r0:r0 + 128, :], o[:], [ok], [("a", e, r0)])


class _Cut(Exception):
    pass


def rwkv_scan_phase(P, C, l, j, st_lim=NT, g_lim=4, stage=99):
    with P.phase():
        try:
            _rwkv_scan_body(P, C, l, j, st_lim, g_lim, stage)
        except _Cut:
            pass


def _rwkv_scan_body(P, C, l, j, st_lim, g_lim, stage):
    def cut(n):
        if stage == n:
            raise _Cut()
    NSLOT = 4
    identf = P.sb("identf", [128, 128], F32)
    ones = P.sb("onesf", [128, 128], F32)
    tri = [P.sb(f"tri{e}", [128, 128], F32) for e in range(2)]
    mSd = [P.sb(f"mSd{e}", [128, 512], F32) for e in range(2)]
    mSo = [P.sb(f"mSo{e}", [128, 512], F32) for e in range(2)]
    mS = [P.sb(f"mS{e}", [128, 512], F32) for e in range(2)]
    mI = [P.sb(f"mI{e}", [128, 512], F32) for e in range(2)]
    mTd = [P.sb(f"mTd{e}", [128, 512], F32) for e in range(2)]
    I4 = P.sb("I4", [128, 512], F32)
    P.dma(I4[:], C.rw_I4[:, :], [], ["I4"])
    P.dma(identf[:], C.ident_f[:, :], [], ["identf"])
    P.add("pool", lambda e_: e_.memset(ones[:], 1.0), [], ["ones"])
    for e in range(2):
        P.dma(tri[e][:], C.rw_tri[e], [], [("tri", e)])
        P.dma(mS[e][:], C.rw_mS[e], [], [("mS", e)])
        P.dma(mI[e][:], C.rw_mI[e], [], [("mI", e)], q="act")
        P.dma(mSd[e][:], C.rw_mSd[e], [], [("mSd", e)])
        P.dma(mSo[e][:], C.rw_mSo[e], [], [("mSo", e)], q="act")
        P.dma(mTd[e][:], C.rw_mTd[e], [], [("mTd", e)])
    kat = P.sb("kat", [128, D], F32)
    P.dma(kat[:], C.rwkv_k_a[j:j + 1, :].broadcast_to([128, D]), [], ["kat"])
    ST = [P.sb(f"ST{e}", [128, 8, 64], F32) for e in range(2)]
    for e in range(2):
        P.add("pool", lambda e_, t_=ST[e]: e_.memset(t_[:], 0.0), [], [("ST", e, hp) for hp in range(8)])
    tr_, tk_, tkk, tlw, ta = [P.sb(n, [128, D], F32) for n in ("tr_", "tk_", "tkk", "tlw", "ta")]
    tvs = [P.sb(f"tv{i}", [128, D], F32) for i in range(2)]
    tb, tkd, cumS, E = [P.sb(n, [128, D], F32) for n in ("tb", "tkd", "cumS", "E")]
    Be, Ke = [P.sb(n, [128, D], F32) for n in ("Be", "Ke")]
    FT = P.sb("FT", [128, 8, 4, 128], F32)
    Dg = P.sb("Dg", [128, 8, 128], F32)
    Yt = P.sb("Yt", [128, D], F32)

    class Slot:
        pass
    slots = []
    for si in range(NSLOT):
        S = Slot()
        S.i = si
        S.Gb = [P.sb(f"Gb{si}_{i}", [128, 4, 128], F32) for i in range(2)]
        S.Lb = [P.sb(f"Lb{si}_{i}", [128, 4, 128], F32) for i in range(2)]
        S.NTb = [P.sb(f"NTb{si}_{i}", [128, 4, 128], F32) for i in range(2)]
        S.Go, S.LakT, S.MrbT, S.MrkT = [P.sb(f"{n}{si}", [128, 4, 128], F32) for n in ("Go", "LakT", "MrbT", "MrkT")]
        S.Xs, S.Zs, S.Ws, S.Us = [P.sb(f"{n}{si}", [128, 4, 64], F32) for n in ("Xs", "Zs", "Ws", "Us")]
        slots.append(S)
    banks = [P.ps(f"bk{i}", [128, 512]) for i in range(8)]
    bc = [0]

    def bank():
        i = bc[0] % 8
        bc[0] += 1
        return banks[i], ("bank", i)

    fl = lambda t_: t_[:].rearrange("p s t -> p (s t)")

    def chain(e, hg, S, tv_, tvk):
        si = S.i
        T_ = lambda n: (n, si)
        gq, hp0 = hg % 2, (hg // 2) * 4
        heads = [2 * (hp0 + i) + gq for i in range(4)]
        hps = [hp0 + i for i in range(4)]
        ftk = [("FT", hp) for hp in hps]
        stk = [("ST", e, hp) for hp in hps]

        def ft(h, si_):
            q = h % 2
            return FT[q * 64:q * 64 + 64, h // 2, si_, :]

        def grp(a_si, b_si, outs):
            bk, bkk = bank()
            for i, h in enumerate(heads):
                P.mm(bk[:, i * 128:(i + 1) * 128], ft(h, a_si), ft(h, b_si), True, True, ftk, [bkk])
            for (dst, dk, mask, mk) in outs:
                P.tt("dve", fl(dst), bk[:, :], mask[:], ALU.mult, [mk], [bkk, dk])

        grp(2, 0, [(S.Gb[0], T_("Gb0"), mSd[e], ("mSd", e)), (S.Go, T_("Go"), mSo[e], ("mSo", e))])
        yield
        grp(0, 2, [(S.Lb[0], T_("Lb0"), mTd[e], ("mTd", e))])
        yield
        grp(3, 0, [(S.LakT, T_("LakT"), mS[e], ("mS", e))])
        yield
        grp(2, 1, [(S.MrbT, T_("MrbT"), mI[e], ("mI", e))])
        yield
        grp(3, 1, [(S.MrkT, T_("MrkT"), mI[e], ("mI", e))])
        yield
        bk, bkk = bank()
        for i, h in enumerate(heads):
            q, hp = h % 2, h // 2
            P.mm(bk[:, i * 64:(i + 1) * 64], ft(h, 0), ST[e][q * 64:q * 64 + 64, hp, :], True, False, ftk + stk, [bkk])
            P.mm(bk[:, i * 64:(i + 1) * 64], S.LakT[:, i, :], tv_[:, h * 64:(h + 1) * 64], False, True, [T_("LakT"), tvk], [bkk])
        P.copy("act", fl(S.Xs), bk[:, 0:256], [], [bkk, T_("Xs")])
        P.tt("pool", fl(S.NTb[0]), I4[:], fl(S.Gb[0]), ALU.add, ["I4", T_("Gb0")], [T_("NT0")])
        yield
        for k in range(1, 5):
            gp, lp = S.Gb[(k - 1) % 2], S.Lb[(k - 1) % 2]
            gpk, lpk = T_("Gb%d" % ((k - 1) % 2)), T_("Lb%d" % ((k - 1) % 2))
            gn, ln = S.Gb[k % 2], S.Lb[k % 2]
            gnk, lnk = T_("Gb%d" % (k % 2)), T_("Lb%d" % (k % 2))
            ntp, ntn = T_("NT%d" % ((k - 1) % 2)), T_("NT%d" % (k % 2))
            bl, blk = bank()
            for i in range(4):
                P.mm(bl[:, i * 128:(i + 1) * 128], gp[:, i, :], lp[:, i, :], True, True, [gpk, lpk], [blk])
            if k < 4:
                bg, bgk = bank()
                for i in range(4):
                    P.mm(bg[:, i * 128:(i + 1) * 128], lp[:, i, :], gp[:, i, :], True, True, [gpk, lpk], [bgk])
            P.copy("act", fl(ln), bl[:, :], [], [blk, lnk])
            if k < 4:
                P.copy("dve", fl(gn), bg[:, :], [], [bgk, gnk])
            yield
            bn, bnk = bank()
            for i in range(4):
                P.mm(bn[:, i * 128:(i + 1) * 128], ln[:, i, :], S.NTb[(k - 1) % 2][:, i, :], True, True, [lnk, ntp], [bnk])
            P.tt("dve", fl(S.NTb[k % 2]), fl(S.NTb[(k - 1) % 2]), bn[:, :], ALU.add, [ntp], [bnk, ntn])
            yield
        NT, NTk = S.NTb[0], T_("NT0")
        bk, bkk = bank()
        for i in range(4):
            P.mm(bk[:, i * 64:(i + 1) * 64], NT[:, i, :], S.Xs[:, i, :], True, True, [NTk, T_("Xs")], [bkk])
        P.copy("act", fl(S.Zs), bk[:, 0:256], [], [bkk, T_("Zs")])
        yield
        ucur, uk = S.Zs, T_("Zs")
        for it in range(3):
            bk, bkk = bank()
            for i in range(4):
                P.mm(bk[:, i * 64:(i + 1) * 64], S.Go[:, i, :], ucur[:, i, :], True, True, [T_("Go"), uk], [bkk])
            P.copy("act", fl(S.Ws), bk[:, 0:256], [], [bkk, T_("Ws")])
            yield
            bk, bkk = bank()
            for i in range(4):
                P.mm(bk[:, i * 64:(i + 1) * 64], NT[:, i, :], S.Ws[:, i, :], True, True, [NTk, T_("Ws")], [bkk])
            P.tt("dve", fl(S.Us), fl(S.Zs), bk[:, 0:256], ALU.add, [T_("Zs")], [bkk, T_("Us")])
            yield
            ucur, uk = S.Us, T_("Us")
        bk, bkk = bank()
        for i, h in enumerate(heads):
            q, hp = h % 2, h // 2
            P.mm(bk[:, i * 64:(i + 1) * 64], ft(h, 1), ST[e][q * 64:q * 64 + 64, hp, :], True, False, ftk + stk, [bkk])
            P.mm(bk[:, i * 64:(i + 1) * 64], S.MrbT[:, i, :], S.Us[:, i, :], False, False, [T_("MrbT"), T_("Us")], [bkk])
            P.mm(bk[:, i * 64:(i + 1) * 64], S.MrkT[:, i, :], tv_[:, h * 64:(h + 1) * 64], False, True, [T_("MrkT"), tvk], [bkk])
        P.copy("act", Yt[:].rearrange("p (a q n) -> p a q n", q=2, n=64)[:, hp0:hp0 + 4, gq, :],
               bk[:, 0:256].rearrange("p (s t) -> p s t", s=4), [], [bkk, ("Yt", hg)])
        yield
        bk, bkk = bank()
        for i, h in enumerate(heads):
            hp = h // 2
            cs = slice(hp * 128, (hp + 1) * 128)
            P.mm(bk[:, i * 64:(i + 1) * 64], Dg[:, hp, :], ST[e][:, hp, :], True, False, [("Dg", hp)] + stk, [bkk])
            P.mm(bk[:, i * 64:(i + 1) * 64], Be[:, cs], S.Us[:, i, :], False, False, ["Be", T_("Us")], [bkk])
            P.mm(bk[:, i * 64:(i + 1) * 64], Ke[:, cs], tv_[:, h * 64:(h + 1) * 64], False, True, ["Ke", tvk], [bkk])
        b4 = bk[:, 0:256].rearrange("p (s t) -> p s t", s=4)
        P.copy("dve", ST[e][gq * 64:gq * 64 + 64, hp0:hp0 + 4, :], b4[gq * 64:gq * 64 + 64, :, :], [], [bkk] + stk)

    order = {0: [16, 17] + list(range(16)), 1: [17, 16] + list(range(15, -1, -1))}
    cut(-1)
    iters = [(st, e) for st in range(st_lim) for e in range(2)]

    def issue_loads(n):
        st_, e_ = iters[n]
        r0_ = order[e_][st_] * 128
        for (tile_, src, nm) in ((tr_, C.sR, "tr"), (tk_, C.sK, "tk"), (tkk, C.sKK, "tkk"), (tlw, C.sLW[e_], "tlw"), (ta, C.sA[e_], "ta"),
                                 (tvs[n % 2], C.sVv, ("tv", n % 2))):
            P.dma(tile_[:], src[r0_:r0_ + 128, :], [], [nm])

    issue_loads(0)
    pending_store = None
    for n, (st, e) in enumerate(iters):
        if True:
            ti = order[e][st]
            r0 = ti * 128
            tv_, tvk = tvs[n % 2], ("tv", n % 2)
            P.tt("pool", tb[:], tkk[:], ta[:], ALU.mult, ["tkk", "ta"], ["tb"])
            P.tt("dve", tkd[:], ta[:], kat[:], ALU.mult, ["ta", "kat"], ["tkd"])
            P.tt("dve", tkd[:], tkd[:], kat[:], ALU.subtract, ["tkd", "kat"], ["tkd"])
            P.stt("dve", tkd[:], tkd[:], 1.0, tk_[:], ALU.add, ALU.mult, ["tkd", "tk"], ["tkd"])
            for nh in range(2):
                cs = slice(nh * 512, (nh + 1) * 512)
                bk, bkk = bank()
                P.mm(bk[:, :], tri[e][:], tlw[:, cs], True, True, [("tri", e), "tlw"], [bkk])
                P.copy("act", cumS[:, cs], bk[:, :], [], [bkk, ("cumS", nh)])
                bk, bkk = bank()
                P.mm(bk[:, :], ones[:], tlw[:, cs], True, True, ["ones", "tlw"], [bkk])
                P.act(E[:, cs], bk[:, :], AF.Exp, [], [bkk, ("Etot", nh)])
                P.tt("dve", Be[:, cs], bk[:, :], cumS[:, cs], ALU.subtract, [("cumS", nh)], [bkk, ("Be4", nh)])
            cS = [("cumS", 0), ("cumS", 1)]
            for hp in range(8):
                P.tt("pool", Dg[:, hp, :], identf[:], E[:, hp * 128:(hp + 1) * 128], ALU.mult, ["identf", ("Etot", hp // 4)], [("Dg", hp)])
            P.act(Be[:], Be[:], AF.Exp, [("Be4", 0), ("Be4", 1)], ["Be"])
            P.tt("pool", Ke[:], tkd[:], Be[:], ALU.mult, ["tkd", "Be"], ["Ke"])
            P.tt("dve", Be[:], tb[:], Be[:], ALU.mult, ["tb", "Be"], ["Be"])
            dgk = [("Dg", hp) for hp in range(8)]
            P.act(E[:], cumS[:], AF.Exp, cS + dgk, ["E"])
            P.tt("dve", tr_[:], tr_[:], E[:], ALU.mult, ["tr", "E"], ["tr"])
            P.tt("dve", E[:], cumS[:], tlw[:], ALU.subtract, cS + ["tlw", "tr"], ["E"])
            P.act(E[:], E[:], AF.Exp, ["E"], ["E"])
            P.stt("dve", tkk[:], tkk[:], -1.0, E[:], ALU.mult, ALU.mult, ["tkk", "E", "tb"], ["tkk"])
            P.act(E[:], cumS[:], AF.Exp, cS + ["tkk"], ["E"], scale=-1.0)
            P.tt("dve", tb[:], tb[:], E[:], ALU.mult, ["tb", "E", "Be"], ["tb"])
            P.tt("pool", tkd[:], tkd[:], E[:], ALU.mult, ["tkd", "E", "Ke"], ["tkd"])
            cut(4)
            for hp in range(8):
                bk, bkk = bank()
                for si, (srcT, sk) in enumerate(((tkk, "tkk"), (tr_, "tr"), (tb, "tb"), (tkd, "tkd"))):
                    P.mm(bk[:, si * 128:(si + 1) * 128], srcT[:, hp * 128:(hp + 1) * 128], identf[:], True, True, [sk, "identf"], [bkk])
                P.copy("act" if hp % 2 else "dve", FT[:, hp, :, :], bk[:, :].rearrange("p (s t) -> p s t", s=4), [], [bkk, ("FT", hp)])
            cut(5)
            if n + 1 < len(iters):
                issue_loads(n + 1)
            if pending_store is not None:
                pending_store()
                pending_store = None
            gens = [chain(e, hg, slots[hg % NSLOT], tv_, tvk) for hg in range(g_lim)]
            while gens:
                for g in list(gens):
                    try:
                        next(g)
                    except StopIteration:
                        gens.remove(g)
            pending_store = (lambda e=e, r0=r0, ti=ti: P.dma(C.sY[e, r0:r0 + 128, :], Yt[:], [("Yt", hg) for hg in range(4)], [("sY", e, ti)]))
    if pending_store is not None:
        pending_store()


def rwkv_out_phase(P, C, l, j):
    with P.phase():
        cb = alloc_common(P)
        stg = [P.sb(f"stg{i}", [128, 1024], F32) for i in range(2)]
        Wo = P.sb("Wo", [128, 8, 1024], BF16)
        cnt = [0]
        load_w_bf16(P, Wo, C.rwkv_w_o[j], stg, 8, 1024, "Wo", cnt)
        kat, rkt, lnw, lnb = [P.sb(n, [128, D], F32) for n in ("kat", "rkt", "lnw", "lnb")]
        P.dma(kat[:], C.rwkv_k_a[j:j + 1, :].broadcast_to([128, D]), [], ["kat"])
        P.dma(rkt[:], C.rwkv_r_k[j].rearrange("h n -> (h n)").rearrange("(o d) -> o d", o=1).broadcast_to([128, D]), [], ["rkt"])
        P.dma(lnw[:], C.rwkv_ln_w[j:j + 1, :].broadcast_to([128, D]), [], ["lnw"])
        P.dma(lnb[:], C.rwkv_ln_b[j:j + 1, :].broadcast_to([128, D]), [], ["lnb"])
        y0, y1, tr_, tk_, tv_, a0, a1, tg = [P.sb(n, [128, D], F32) for n in ("y0", "y1", "tr_", "tk_", "tv_", "a0", "a1", "tg")]
        s16 = P.sb("s16", [128, 64], F32)
        ob = P.sb("ob", [128, D], BF16)
        oT = P.sb("oT", [128, 8, 128], BF16)
        psY = P.ps("psY", [128, D])
        for (who, ntl, hb) in ((0, 16, C.hL), (1, 2, C.hC)):
            mod_tiles(P, cb, l, 1, who, 1.0)
            for ti in range(ntl):
                r0 = (ti + (16 if who == 1 else 0)) * 128
                for (tile_, src, nm, q) in ((y0, C.sY[0], "y0", "sp"), (y1, C.sY[1], "y1", "act"), (tr_, C.sR, "tr", "sp"), (tk_, C.sK, "tk", "act"),
                                            (tv_, C.sVv, "tv", "sp"), (a0, C.sA[0], "a0", "act"), (a1, C.sA[1], "a1", "sp"), (tg, C.sG, "tg", "act")):
                    P.dma(tile_[:], src[r0:r0 + 128, :], [], [nm], q=q)
                v3 = lambda t_: t_[:].rearrange("p (h n) -> p h n", n=64)
                P.tt("dve", y0[:], y0[:], y1[:], ALU.add, ["y0", "y1"], ["y0"])
                P.add("dve", lambda e, o_=s16[:, 0:16], i_=v3(y0): e.tensor_reduce(out=o_, in_=i_, axis=AX.X, op=ALU.add), ["y0"], ["mean"])
                P.ts("dve", s16[:, 0:16], s16[:, 0:16], 1.0 / 64, None, ALU.mult, None, ["mean"], ["mean"])
                bc16 = lambda c0: s16[:, c0:c0 + 16].unsqueeze(2).broadcast_to([128, 16, 64])
                P.tt("dve", v3(y0), v3(y0), bc16(0), ALU.subtract, ["y0", "mean"], ["y0"])
                P.tt("pool", y1[:], y0[:], y0[:], ALU.mult, ["y0", "y1"], ["y1"])
                P.add("dve", lambda e, o_=s16[:, 16:32], i_=v3(y1): e.tensor_reduce(out=o_, in_=i_, axis=AX.X, op=ALU.add), ["y1"], ["var"])
                P.ts("dve", s16[:, 16:32], s16[:, 16:32], 1.0 / 64, 64e-5, ALU.mult, ALU.add, ["var"], ["var"])
                P.act(s16[:, 16:32], s16[:, 16:32], AF.Sqrt, ["var"], ["var"])
                P.recip(s16[:, 16:32], s16[:, 16:32], ["var"], ["var"])
                P.tt("dve", v3(y0), v3(y0), bc16(16), ALU.mult, ["y0", "var"], ["y0"])
                P.tt("dve", y0[:], y0[:], lnw[:], ALU.mult, ["y0", "lnw"], ["y0"])
                P.tt("pool", y0[:], y0[:], lnb[:], ALU.add, ["y0", "lnb"], ["y0"])
                P.tt("dve", a0[:], a0[:], a1[:], ALU.add, ["a0", "a1"], ["a0"])
                P.stt("dve", a0[:], a0[:], -2.0, kat[:], ALU.add, ALU.mult, ["a0", "kat"], ["a0"])
                P.stt("dve", a0[:], a0[:], 2.0, tk_[:], ALU.add, ALU.mult, ["a0", "tk"], ["a0"])
                P.tt("pool", a0[:], a0[:], tr_[:], ALU.mult, ["a0", "tr"], ["a0"])
                P.tt("dve", a0[:], a0[:], rkt[:], ALU.mult, ["a0", "rkt"], ["a0"])
                P.add("dve", lambda e, o_=s16[:, 32:48], i_=v3(a0): e.tensor_reduce(out=o_, in_=i_, axis=AX.X, op=ALU.add), ["a0"], ["coef"])
                P.tt("pool", v3(a0), v3(tv_), bc16(32), ALU.mult, ["tv", "coef", "a0"], ["a0"])
                P.tt("dve", y0[:], y0[:], a0[:], ALU.add, ["y0", "a0"], ["y0"])
                P.tt("dve", ob[:], y0[:], tg[:], ALU.mult, ["y0", "tg"], ["ob"])
                for k in range(8):
                    P.tr(cb.psT[:, k, :], ob[:, k * 128:(k + 1) * 128], cb.ident[:], ["ob", "ident"], ["cpsT"])
                P.copy("act", oT[:], cb.psT[:], ["cpsT"], ["oT"])
                hap = hb[ti * 128:(ti + 1) * 128, :]
                x = cb.xb[cb.gi % 2]; xk = ("cxb", cb.gi % 2); cb.gi += 1
                P.dma(x[:], hap, [("h", who, ti)], [xk])
                for nh in range(2):
                    for k in range(8):
                        P.mm(psY[:, nh * 512:(nh + 1) * 512], oT[:, k, :], Wo[:, k, nh * 512:(nh + 1) * 512], k == 0, k == 7, ["oT", ("Wo", k)], ["psY"])
                epi_tile(P, cb, psY[:], ["psY"], x, xk, hap, ("h", who, ti))


def rwkv_phases(P, C, l, j, ctx_out):
    import os
    stop = int(os.environ.get("RW_STOP", "3"))
    rwkv_feat_phase(P, C, l, j)
    if stop >= 2:
        rwkv_scan_phase(P, C, l, j)
    if stop >= 3:
        rwkv_out_phase(P, C, l, j)
```
